# Optimizing a Trainium2 kernel written in Bass

```python
import math
import jax
import jax.numpy as jnp
from jax import lax
import numpy as np

D_MODEL = 2048
BATCH = 4
SEQ = 4096
DEPTH = 2

GRID_W = 64
CTX_LEN = 256

DA_HEADS = 4
DA_HALF = 64
DA_V = 2 * DA_HALF
ML_HEADS = 4
ML_DK = 128
ML_DV = 128
ML_CHUNK = 64
ML_CONV = 3
MLA_HEADS = 8
MLA_Q_RANK = 512
MLA_KV_RANK = 256
MLA_NOPE = 128
MLA_ROPE = 64
MLA_V = 128
MLA_QK = MLA_NOPE + MLA_ROPE

ROPE_DIM = 64
ROPE_BASE = 10000.0
Q_BLOCK = 128
EPS = 1e-6

MIX_W = DA_HEADS * DA_V + ML_HEADS * ML_DV + MLA_HEADS * MLA_V
FF_HIDDEN = -(-8 * D_MODEL // (3 * 256)) * 256
IN_SIZES = (
    DA_HEADS * 2 * DA_HALF, DA_HEADS * 2 * DA_HALF, DA_HEADS * DA_V,
    ML_HEADS * ML_DK, ML_HEADS * ML_DK, ML_HEADS * ML_DV, ML_HEADS * ML_DV,
    4 * ML_HEADS,
    MLA_Q_RANK, MLA_KV_RANK, MLA_ROPE,
)
IN_W = sum(IN_SIZES)

kernel_name = 'hybrid_diffattn_mlstm_mla_dit'


def rms_norm(x, g):
    xf = x.astype(jnp.float32)
    y = xf * lax.rsqrt(jnp.mean(xf * xf, axis=-1, keepdims=True) + EPS)
    return (y * g.astype(jnp.float32)).astype(x.dtype)


def modulate(x, g, shift, scale):
    return rms_norm(x, g) * (1.0 + scale) + shift


def split_cols(p):
    parts, off = [], 0
    for size in IN_SIZES:
        parts.append(p[..., off:off + size])
        off += size
    return parts


def to_heads(t, n_heads):
    b, t_len, _ = t.shape
    return t.reshape(b, t_len, n_heads, -1).transpose(0, 2, 1, 3)


def from_heads(t):
    b, h, t_len, d = t.shape
    return t.transpose(0, 2, 1, 3).reshape(b, t_len, h * d)


def rope_tables(n_lat):
    rows = n_lat // GRID_W
    r, col = jnp.meshgrid(jnp.arange(rows, dtype=jnp.float32),
                          jnp.arange(GRID_W, dtype=jnp.float32), indexing='ij')
    half = ROPE_DIM // 4
    inv = ROPE_BASE ** (-jnp.arange(half, dtype=jnp.float32) / half)
    ang_r = r.reshape(-1)[:, None] * inv
    ang_c = col.reshape(-1)[:, None] * inv
    return (jnp.cos(ang_r), jnp.sin(ang_r), jnp.cos(ang_c), jnp.sin(ang_c))


def _rotate(x, cos, sin):
    x1, x2 = jnp.split(x, 2, axis=-1)
    return jnp.concatenate([x1 * cos - x2 * sin, x2 * cos + x1 * sin], axis=-1)


def rope_2d(x, tabs):
    cr, sr, cc, sc = tabs
    xr, xc = jnp.split(x, 2, axis=-1)
    return jnp.concatenate([_rotate(xr, cr, sr), _rotate(xc, cc, sc)], axis=-1)


def sweep_query_blocks(fn, q):
    *lead, t_len, d = q.shape
    nb = t_len // Q_BLOCK
    qb = jnp.moveaxis(q.reshape(*lead, nb, Q_BLOCK, d), -3, 0)
    out = jnp.moveaxis(lax.map(fn, qb), 0, -3)
    return out.reshape(*out.shape[:-3], nb * Q_BLOCK, out.shape[-1])


def dwconv_centred(x, w, b):
    k = w.shape[0]
    pad = k // 2
    t_len = x.shape[1]
    xp = jnp.pad(x, ((0, 0), (pad, pad), (0, 0)))
    y = b
    for j in range(k):
        y = y + xp[:, j:j + t_len] * w[j]
    return y


def swiglu(h, w_gu, w_down):
    gate, up = jnp.split(h @ w_gu, 2, axis=-1)
    return (jax.nn.silu(gate) * up) @ w_down


def mlstm_chunkwise(q, k, v, ig, lf, state):
    b_sz, h_sz, t_len, _ = q.shape
    dv = v.shape[-1]
    nc = t_len // ML_CHUNK

    def chunks(t):
        return jnp.moveaxis(t.reshape(b_sz, h_sz, nc, ML_CHUNK, *t.shape[3:]), 2, 0)

    lower = jnp.tril(jnp.ones((ML_CHUNK, ML_CHUNK), dtype=bool))

    def step(carry, inp):
        c_mat, n_vec, m = carry
        qc, kc, vc, igc, lfc = inp
        bcum = jnp.cumsum(lfc, axis=-1)
        log_d = jnp.where(lower, bcum[..., :, None] - bcum[..., None, :] + igc[..., None, :], -jnp.inf)
        log_prev = bcum + m[..., None]
        m_t = jnp.maximum(log_prev, jnp.max(log_d, axis=-1))
        dmat = jnp.exp(log_d - m_t[..., None])
        w_prev = jnp.exp(log_prev - m_t)
        s = jnp.einsum('bhtd,bhsd->bhts', qc, kc) * dmat
        num = jnp.einsum('bhts,bhsv->bhtv', s, vc) + w_prev[..., None] * jnp.einsum('bhtd,bhdv->bhtv', qc, c_mat)
        den = jnp.sum(s, axis=-1) + w_prev * jnp.einsum('bhtd,bhd->bht', qc, n_vec)
        h = num / jnp.maximum(jnp.abs(den), jnp.exp(-m_t))[..., None]
        m_new = m_t[..., -1]
        w_s = jnp.exp(bcum[..., -1:] - bcum + igc - m_new[..., None])
        decay = jnp.exp(bcum[..., -1] + m - m_new)
        c_new = decay[..., None, None] * c_mat + jnp.einsum('bhs,bhsd,bhsv->bhdv', w_s, kc, vc)
        n_new = decay[..., None] * n_vec + jnp.einsum('bhs,bhsd->bhd', w_s, kc)
        return (c_new, n_new, m_new), h

    final, h = lax.scan(step, state, tuple(chunks(t) for t in (q, k, v, ig, lf)))
    h = jnp.moveaxis(h, 0, 2).reshape(b_sz, h_sz, t_len, dv)
    return h, final


def diff_attention(lat, ctx_p, qk_g, lam_p, out_g, lam_init, tabs, need_ctx):
    (ql, kl, vl), (qc, kc, vc) = lat, ctx_p

    def prep(t, g, rope):
        b_sz, t_len, _ = t.shape
        t = t.reshape(b_sz, t_len, DA_HEADS, 2, DA_HALF).transpose(0, 2, 3, 1, 4).astype(jnp.float32)
        t = rms_norm(t, g)
        return rope_2d(t, tabs) if rope else t

    lam = (jnp.exp(jnp.sum(lam_p[0] * lam_p[1])) - jnp.exp(jnp.sum(lam_p[2] * lam_p[3])) + lam_init).astype(jnp.float32)
    scale = DA_HALF ** -0.5

    def attend(qb, k, v):
        p = jax.nn.softmax(jnp.einsum('bhmqd,bhmkd->bhmqk', qb, k) * scale, axis=-1)
        return jnp.einsum('bhqk,bhkv->bhqv', p[:, :, 0] - lam * p[:, :, 1], v)

    k_ctx = prep(kc, qk_g[1], False)
    v_ctx = to_heads(vc, DA_HEADS).astype(jnp.float32)
    k_all = jnp.concatenate([k_ctx, prep(kl, qk_g[1], True)], axis=-2)
    v_all = jnp.concatenate([v_ctx, to_heads(vl, DA_HEADS).astype(jnp.float32)], axis=-2)
    o_lat = sweep_query_blocks(lambda qb: attend(qb, k_all, v_all), prep(ql, qk_g[0], True))

    def post(o):
        return from_heads(rms_norm(o, out_g) * (1.0 - lam_init))

    out_ctx = post(attend(prep(qc, qk_g[0], False), k_ctx, v_ctx)) if need_ctx else None
    return post(o_lat), out_ctx


def mlstm(lat, ctx_p, conv_w, conv_b, gate_b, out_g, need_ctx):
    hh = ML_HEADS

    def prep(q, k, v, o, g):
        qk = jax.nn.silu(dwconv_centred(jnp.concatenate([q, k], axis=-1), conv_w, conv_b))
        q, k = jnp.split(qk, 2, axis=-1)
        q = to_heads(q, hh).astype(jnp.float32)
        k = to_heads(k, hh).astype(jnp.float32) * ML_DK ** -0.5
        v = to_heads(v, hh).astype(jnp.float32)
        g = (g + gate_b).astype(jnp.float32).transpose(0, 2, 1)
        fwd = (g[:, 0:hh], jax.nn.log_sigmoid(g[:, hh:2 * hh]))
        bwd = (g[:, 2 * hh:3 * hh], jax.nn.log_sigmoid(g[:, 3 * hh:]))
        return (q, k, v), fwd, bwd, o

    lat_qkv, lat_f, lat_b, lat_o = prep(*lat)
    ctx_qkv, ctx_f, ctx_b, ctx_o = prep(*ctx_p)
    b_sz = lat_qkv[0].shape[0]
    s0 = (jnp.zeros((b_sz, hh, ML_DK, ML_DV), jnp.float32),
          jnp.zeros((b_sz, hh, ML_DK), jnp.float32),
          jnp.zeros((b_sz, hh), jnp.float32))

    def flip(ts):
        return tuple(jnp.flip(t, axis=2) for t in ts)

    h_cf, st_f = mlstm_chunkwise(*ctx_qkv, *ctx_f, s0)
    h_lf, _ = mlstm_chunkwise(*lat_qkv, *lat_f, st_f)
    h_cb, st_b = mlstm_chunkwise(*flip(ctx_qkv), *flip(ctx_b), s0)
    h_lb, _ = mlstm_chunkwise(*flip(lat_qkv), *flip(lat_b), st_b)
    gain = out_g.reshape(hh, 1, ML_DV)

    def post(h, o):
        return from_heads(rms_norm(h, gain)) * jax.nn.sigmoid(o.astype(jnp.float32))

    out_ctx = post(h_cf + jnp.flip(h_cb, axis=2), ctx_o) if need_ctx else None
    return post(h_lf + jnp.flip(h_lb, axis=2), lat_o), out_ctx


def mla(lat, ctx_p, q_norm_g, kv_norm_g, w_uq, w_ukv, qk_g, tabs, need_ctx):
    (cql, ckvl, kpel), (cqc, ckvc, kpec) = lat, ctx_p

    def rope_tail(t):
        return jnp.concatenate([t[..., :MLA_NOPE], rope_2d(t[..., MLA_NOPE:], tabs)], axis=-1)

    def queries(cq, rope):
        q = to_heads(rms_norm(cq, q_norm_g) @ w_uq, MLA_HEADS).astype(jnp.float32)
        q = rms_norm(q, qk_g[0])
        return rope_tail(q) if rope else q

    def keys_values(ckv, kpe, rope):
        b_sz, t_len, _ = ckv.shape
        kv = to_heads(rms_norm(ckv, kv_norm_g) @ w_ukv, MLA_HEADS).astype(jnp.float32)
        k_pe = jnp.broadcast_to(kpe.astype(jnp.float32)[:, None], (b_sz, MLA_HEADS, t_len, MLA_ROPE))
        k = rms_norm(jnp.concatenate([kv[..., :MLA_NOPE], k_pe], axis=-1), qk_g[1])
        return (rope_tail(k) if rope else k), kv[..., MLA_NOPE:]

    scale = MLA_QK ** -0.5

    def attend(qb, k, v):
        p = jax.nn.softmax(jnp.einsum('bhqd,bhkd->bhqk', qb, k) * scale, axis=-1)
        return jnp.einsum('bhqk,bhkv->bhqv', p, v)

    k_ctx, v_ctx = keys_values(ckvc, kpec, False)
    k_lat, v_lat = keys_values(ckvl, kpel, True)
    k_all = jnp.concatenate([k_ctx, k_lat], axis=-2)
    v_all = jnp.concatenate([v_ctx, v_lat], axis=-2)
    o_lat = sweep_query_blocks(lambda qb: attend(qb, k_all, v_all), queries(cql, True))
    out_ctx = from_heads(attend(queries(cqc, False), k_ctx, v_ctx)) if need_ctx else None
    return from_heads(o_lat), out_ctx


def hybrid_layer(x, xc, mod_lat, mod_ctx, layer_idx, need_ctx, tabs,
                 norm1_g, norm2_g, w_in, da_qk_g, da_lambda, da_out_g,
                 ml_conv_w, ml_conv_b, ml_gate_b, ml_out_g,
                 mla_q_norm_g, mla_kv_norm_g, mla_w_uq, mla_w_ukv, mla_qk_g,
                 w_out, ffn_w_gu, ffn_w_down):
    sh1, sc1, gt1, sh2, sc2, gt2 = jnp.split(mod_lat[:, None, :], 6, axis=-1)
    csh1, csc1, cgt1, csh2, csc2, cgt2 = jnp.split(mod_ctx, 6, axis=-1)
    pl = split_cols(modulate(x, norm1_g, sh1, sc1) @ w_in)
    pc = split_cols(modulate(xc, norm1_g, csh1, csc1) @ w_in)
    lam_init = 0.8 - 0.6 * math.exp(-0.3 * layer_idx)
    da_l, da_c = diff_attention(pl[0:3], pc[0:3], da_qk_g, da_lambda, da_out_g, lam_init, tabs, need_ctx)
    ml_l, ml_c = mlstm(pl[3:8], pc[3:8], ml_conv_w, ml_conv_b, ml_gate_b, ml_out_g, need_ctx)
    mla_l, mla_c = mla(pl[8:11], pc[8:11], mla_q_norm_g, mla_kv_norm_g, mla_w_uq, mla_w_ukv,
                       mla_qk_g, tabs, need_ctx)

    def finish(h, mix, g1, s2, c2, g2):
        h = h + g1 * (jnp.concatenate(mix, axis=-1).astype(h.dtype) @ w_out)
        return h + g2 * swiglu(modulate(h, norm2_g, s2, c2), ffn_w_gu, ffn_w_down)

    x = finish(x, [da_l, ml_l, mla_l], gt1, sh2, sc2, gt2)
    if need_ctx:
        xc = finish(xc, [da_c, ml_c, mla_c], cgt1, csh2, csc2, cgt2)
    return x, xc


def setup_inputs(seed: int = 0) -> dict:
    key = jax.random.key(seed)
    ks = jax.random.split(key, 32)
    f32 = jnp.float32

    def w(k, shape, fan_in):
        return jax.random.normal(k, shape, f32) * fan_in ** -0.5

    def gain(k, shape):
        return 1.0 + 0.05 * jax.random.normal(k, shape, f32)

    def small(k, shape, s=0.02):
        return s * jax.random.normal(k, shape, f32)

    ig_b = small(ks[14], (DEPTH, 2, ML_HEADS), 0.1)
    fg_b = jnp.linspace(3.0, 6.0, ML_HEADS, dtype=f32) + small(ks[15], (DEPTH, 2, ML_HEADS), 0.1)
    ml_gate_b = jnp.stack([ig_b, fg_b], axis=2).reshape(DEPTH, 4 * ML_HEADS)
    return {
        'x': jax.random.normal(ks[0], (BATCH, SEQ, D_MODEL), f32),
        'c': jax.random.normal(ks[1], (BATCH, D_MODEL), f32),
        'ctx': jax.random.normal(ks[2], (BATCH, CTX_LEN, D_MODEL), f32),
        'c_ctx': jax.random.normal(ks[3], (D_MODEL,), f32),
        'mod_w': w(ks[4], (DEPTH, D_MODEL, 6 * D_MODEL), D_MODEL),
        'mod_b': small(ks[5], (DEPTH, 6 * D_MODEL)),
        'norm1_g': gain(ks[6], (DEPTH, D_MODEL)),
        'norm2_g': gain(ks[7], (DEPTH, D_MODEL)),
        'w_in': w(ks[8], (DEPTH, D_MODEL, IN_W), D_MODEL),
        'da_qk_g': gain(ks[9], (DEPTH, 2, DA_HALF)),
        'da_lambda': small(ks[10], (DEPTH, 4, DA_HALF), 0.1),
        'da_out_g': gain(ks[11], (DEPTH, DA_V)),
        'ml_conv_w': w(ks[12], (DEPTH, ML_CONV, 2 * ML_HEADS * ML_DK), ML_CONV),
        'ml_conv_b': small(ks[13], (DEPTH, 2 * ML_HEADS * ML_DK)),
        'ml_gate_b': ml_gate_b,
        'ml_out_g': gain(ks[16], (DEPTH, ML_HEADS * ML_DV)),
        'mla_q_norm_g': gain(ks[17], (DEPTH, MLA_Q_RANK)),
        'mla_kv_norm_g': gain(ks[18], (DEPTH, MLA_KV_RANK)),
        'mla_w_uq': w(ks[19], (DEPTH, MLA_Q_RANK, MLA_HEADS * MLA_QK), MLA_Q_RANK),
        'mla_w_ukv': w(ks[20], (DEPTH, MLA_KV_RANK, MLA_HEADS * (MLA_NOPE + MLA_V)), MLA_KV_RANK),
        'mla_qk_g': gain(ks[21], (DEPTH, 2, MLA_QK)),
        'w_out': w(ks[22], (DEPTH, MIX_W, D_MODEL), MIX_W),
        'ffn_w_gu': w(ks[23], (DEPTH, D_MODEL, 2 * FF_HIDDEN), D_MODEL),
        'ffn_w_down': w(ks[24], (DEPTH, FF_HIDDEN, D_MODEL), FF_HIDDEN),
    }


def reference(x, c, ctx, c_ctx, mod_w, mod_b, norm1_g, norm2_g, w_in, da_qk_g, da_lambda,
              da_out_g, ml_conv_w, ml_conv_b, ml_gate_b, ml_out_g, mla_q_norm_g, mla_kv_norm_g,
              mla_w_uq, mla_w_ukv, mla_qk_g, w_out, ffn_w_gu, ffn_w_down):
    in_dtype = x.dtype
    tabs = rope_tables(x.shape[1])
    xc = ctx
    for l in range(DEPTH):
        mod_lat = jax.nn.silu(c) @ mod_w[l] + mod_b[l]
        mod_ctx = jax.nn.silu(c_ctx) @ mod_w[l] + mod_b[l]
        x, xc = hybrid_layer(
            x, xc, mod_lat, mod_ctx, l, l < DEPTH - 1, tabs,
            norm1_g[l], norm2_g[l], w_in[l], da_qk_g[l], da_lambda[l], da_out_g[l],
            ml_conv_w[l], ml_conv_b[l], ml_gate_b[l], ml_out_g[l],
            mla_q_norm_g[l], mla_kv_norm_g[l], mla_w_uq[l], mla_w_ukv[l], mla_qk_g[l],
            w_out[l], ffn_w_gu[l], ffn_w_down[l])
    return x.astype(in_dtype)
```

```python
import contextlib
import math
import numpy as np
import concourse.bass as bass
import concourse.mybir as mybir
from concourse.bass_utils import run_bass_kernel_spmd

F32 = mybir.dt.float32
BF16 = mybir.dt.bfloat16
AF = mybir.ActivationFunctionType
ALU = mybir.AluOpType

SAME_ENGINE_SYNC = True
RELAX_SAME_ENGINE = False
SMALL_FREE = 256
N_DMA_SLOTS = 24

D = 2048
SEQ = 4096
CTX = 256
NTOK = SEQ + CTX
DEPTH = 2
GRID_W = 64
IN_W = 4432
FF = 5632
EPS = 1e-6
NT128 = NTOK // 128
BLOCKS = [(0, 256)] + [(256 + 512 * i, 512) for i in range(8)]
C_DAQ, C_DAK, C_DAV, C_MLQ, C_MLK, C_MLV, C_MLO, C_GAT, C_CQ, C_CKV, C_KPE = (
    0, 512, 1024, 1536, 2048, 2560, 3072, 3584, 3600, 4112, 4368)
R_DAQ, R_DAK, R_MLQ, R_MLK, R_MLO, R_CQ, R_CKV, R_KPE, N_RAW = 0, 4, 8, 12, 16, 20, 24, 26, 27
PV_N1G, PV_N2G, PV_DAQG, PV_DAOG, PV_MLCW, PV_MLCB, PV_MLOG, PV_MQG, PV_MKVG, PV_MQKN, PV_MQKR, PV_DALAM, PV_MLGB, PVL = (
    0, 16, 32, 34, 35, 59, 67, 71, 75, 77, 79, 81, 337, 353)
CS_ID, CS_ONES, CS_BLK64, CS_ROT, CS_U, CS_L, NCS = 0, 128, 256, 384, 512, 640, 768


class Eng:
    def __init__(self, name, h, sem, is_pe=False):
        self.name, self.h, self.sem = name, h, sem
        self.cnt = 0
        self.seen = {}
        self.is_pe = is_pe


class Tl:
    __slots__ = ("ap", "w", "r", "name", "small")

    def __init__(self, ap, name=""):
        self.ap, self.w, self.r, self.name = ap, None, {}, name
        n = 1
        for d in list(ap.shape)[1:]:
            n *= int(d)
        self.small = n < SMALL_FREE

    def __getitem__(self, idx):
        return self.ap[idx]


class K:
    def __init__(self, nc):
        self.nc = nc
        self.es = contextlib.ExitStack()
        sem = lambda n: self.es.enter_context(nc.semaphore(n))
        self.pe = Eng("pe", nc.tensor, sem("s_pe"), is_pe=True)
        self.act = Eng("act", nc.scalar, sem("s_act"))
        self.dve = Eng("dve", nc.vector, sem("s_dve"))
        self.pool = Eng("pool", nc.gpsimd, sem("s_pool"))
        self.sp = Eng("sp", nc.sync, sem("s_sp"))
        self.engs = [self.pe, self.act, self.dve, self.pool, self.sp]
        self.slots = [Eng(f"dq{i}", None, sem(f"s_dq{i}")) for i in range(N_DMA_SLOTS)]
        self.slot_i = 0
        self.n_ins = 0
        self.uid = 0

    def close(self):
        self.es.close()

    def sb(self, stack, name, shape, dt):
        self.uid += 1
        return Tl(stack.enter_context(self.nc.sbuf_tensor(f"sb{self.uid}_{name}", list(shape), dt)), name)

    def ps(self, stack, name, shape, dt=F32):
        self.uid += 1
        shape = list(shape)
        esz = 4 if dt == F32 else 2
        rest = esz
        for d in shape[2:]:
            rest *= d
        full1 = 2048 // rest
        assert full1 >= shape[1] and full1 * rest == 2048, (name, shape)
        t = stack.enter_context(self.nc.psum_tensor(f"ps{self.uid}_{name}", [shape[0], full1] + shape[2:], dt))
        if full1 == shape[1]:
            return Tl(t, name)
        return Tl(t[:, 0:shape[1]], name)

    def _deps(self, reads, writes, eng=None):
        deps = {}
        for t in reads:
            if t.w is not None and deps.get(t.w[0], 0) < t.w[1] and not (t.w[0] is eng and not t.small):
                deps[t.w[0]] = t.w[1]
        for t in writes:
            if t.w is not None and deps.get(t.w[0], 0) < t.w[1] and not (t.w[0] is eng and not t.small):
                deps[t.w[0]] = t.w[1]
            for e, c in t.r.items():
                if deps.get(e, 0) < c and not (e is eng and not t.small):
                    deps[e] = c
        return deps

    def _waits(self, eng, deps):
        for e, c in deps.items():
            if e is eng and (eng.is_pe or not SAME_ENGINE_SYNC):
                continue
            if eng.seen.get(e, 0) >= c:
                continue
            eng.h.wait_ge(e.sem, c)
            eng.seen[e] = c

    def _mark(self, who, val, reads, writes):
        for t in reads:
            if t.r.get(who, 0) < val:
                t.r[who] = val
        for t in writes:
            t.w = (who, val)
            t.r = {}

    def op(self, eng, fn, reads=(), writes=()):
        self._waits(eng, self._deps(reads, writes, eng if RELAX_SAME_ENGINE else None))
        ins = fn()
        eng.cnt += 1
        ins.then_inc(eng.sem, 1)
        self._mark(eng, eng.cnt, reads, writes)
        self.n_ins += 1
        return ins

    def dma(self, out, in_, reads=(), writes=(), q=None):
        q = q or self.sp
        slot = self.slots[self.slot_i]
        self.slot_i = (self.slot_i + 1) % len(self.slots)
        deps = self._deps(reads, writes)
        if slot.cnt > 0:
            deps[slot] = max(deps.get(slot, 0), slot.cnt)
        self._waits(q, deps)
        ins = q.h.dma_start(out=out, in_=in_)
        slot.cnt += 16
        ins.then_inc(slot.sem, 16)
        self._mark(slot, slot.cnt, reads, writes)
        self.n_ins += 1
        return ins

    def barrier(self):
        sp = self.sp
        for s in self.slots + self.engs:
            if s is sp:
                continue
            if s.cnt > 0 and sp.seen.get(s, 0) < s.cnt:
                sp.h.wait_ge(s.sem, s.cnt)
                sp.seen[s] = s.cnt
        sp.cnt += 1
        sp.h.nop().then_inc(sp.sem, 1)
        for e in self.engs:
            if e is sp:
                continue
            e.h.wait_ge(sp.sem, sp.cnt)
            for o in self.engs + self.slots:
                e.seen[o] = o.cnt
        for o in self.engs + self.slots:
            sp.seen[o] = o.cnt


def build(n_layers=DEPTH, stop_after=None, dump=()):
    nc = bass.Bass("TRN2", target_bir_lowering=False)
    k = K(nc)
    print("[build] sbuf bytes remaining at start:", nc.sbuf_bytes_remaining, flush=True)
    PE, ACT, DVE, POOL = k.pe, k.act, k.dve, k.pool

    def din(name, shape, dt=F32):
        return nc.dram_tensor(name, list(shape), dt, kind="ExternalInput").ap()

    def dscr(name, shape, dt):
        kind = "ExternalOutput" if name in dump else "Internal"
        return nc.dram_tensor(name, list(shape), dt, kind=kind).ap()

    xin = din("xin", [NTOK, D])
    cc_d = din("cc", [128, 16, 2])
    modw_d = din("mod_w", [DEPTH, D, 6 * D])
    modb_d = din("mod_bT", [DEPTH, 128, 96])
    win_d = din("w_in", [DEPTH, D, IN_W])
    wout_d = din("w_out", [DEPTH, D, D])
    wgu_d = din("w_gu", [DEPTH, D, 2 * FF])
    wdn_d = din("w_down", [DEPTH, FF, D])
    wuq_d = din("w_uq", [DEPTH, 512, 1536])
    wukv_d = din("w_ukv", [DEPTH, 256, 2048])
    pv_d = din("pv", [128, DEPTH * PVL])
    cst_d = din("cst", [128, NCS])
    rope_d = din("rope", [2, 128, SEQ])
    y_d = nc.dram_tensor("y", [SEQ, D], F32, kind="ExternalOutput").ap()

    WIN = [dscr(f"WIN{l}", [D, IN_W], BF16) for l in range(DEPTH)]
    WOUT = [dscr(f"WOUT{l}", [D, D], BF16) for l in range(DEPTH)]
    WGU = [dscr(f"WGU{l}", [D, 2 * FF], BF16) for l in range(DEPTH)]
    WDN = [dscr(f"WDN{l}", [FF, D], BF16) for l in range(DEPTH)]
    WUQ = [dscr(f"WUQ{l}", [512, 1536], BF16) for l in range(DEPTH)]
    WUKV = [dscr(f"WUKV{l}", [256, 2048], BF16) for l in range(DEPTH)]
    XR = dscr("XR", [NTOK, D], F32)
    XH = dscr("XH", [NTOK, D], F32)
    RAWT = dscr("RAWT", [N_RAW, 128, NTOK], F32)
    VDA = dscr("VDA", [NTOK, 512], BF16)
    VML = dscr("VML", [NTOK, 512], BF16)
    GATES = dscr("GATES", [NTOK, 16], F32)
    MIXT = dscr("MIXT", [16, 128, NTOK], BF16)

    def act(out, in_, func, R, W, **kw):
        return k.op(ACT, lambda: nc.scalar.activation(out=out, in_=in_, func=func, **kw), R, W)

    def tt(e, out, a, b, op, R, W):
        return k.op(e, lambda: e.h.tensor_tensor(out=out, in0=a, in1=b, op=op), R, W)

    def ts(e, out, a, s1, s2, op0, op1, R, W):
        if s2 is None:
            return k.op(e, lambda: e.h.tensor_scalar(out=out, in0=a, scalar1=s1, scalar2=None, op0=op0), R, W)
        return k.op(e, lambda: e.h.tensor_scalar(out=out, in0=a, scalar1=s1, scalar2=s2, op0=op0, op1=op1), R, W)

    def stt(e, out, a, s, b, op0, op1, R, W):
        return k.op(e, lambda: e.h.scalar_tensor_tensor(out=out, in0=a, scalar=s, in1=b, op0=op0, op1=op1), R, W)

    def cp(e, out, in_, R, W):
        if e is ACT:
            return k.op(e, lambda: nc.scalar.copy(out=out, in_=in_), R, W)
        return k.op(e, lambda: e.h.tensor_copy(out=out, in_=in_), R, W)

    def mm(out, lhsT, rhs, start, stop, R, W):
        return k.op(PE, lambda: nc.tensor.matmul(out, lhsT=lhsT, rhs=rhs, start=start, stop=stop), R, W)

    def tr(out, in_, ident, R, W):
        return k.op(PE, lambda: nc.tensor.transpose(out, in_, ident), R, W)

    with contextlib.ExitStack() as glob:
        cst = k.sb(glob, "cst", [128, NCS], F32)
        cstb = k.sb(glob, "cstb", [128, 256], BF16)
        pv = k.sb(glob, "pv", [128, DEPTH * PVL], F32)
        modT = k.sb(glob, "modT", [128, DEPTH, 96, 2], F32)
        AB = k.sb(glob, "AB", [128, DEPTH, 2, 16, 2], F32)
        k.dma(cst[:], cst_d[:, :], writes=[cst])
        k.dma(pv[:], pv_d[:, :], writes=[pv])
        cp(DVE, cstb[:], cst[:, 0:256], [cst], [cstb])
        identb = cstb[:, 0:128]
        onesb = cstb[:, 128:256]
        identf = cst[:, CS_ID:CS_ID + 128]
        onesf = cst[:, CS_ONES:CS_ONES + 128]

        def pvc(l, off, n=1):
            return pv[:, l * PVL + off:l * PVL + off + n]

        def phase_cast(layers):
            with contextlib.ExitStack() as st:
                CW = 2048
                fb = [k.sb(st, f"cw_f{i}", [128, CW], F32) for i in range(3)]
                bb = [k.sb(st, f"cw_b{i}", [128, CW], BF16) for i in range(3)]
                engs = [DVE, POOL, ACT]
                it = 0
                for l in layers:
                    for src, dst, Rr, Cc in ((win_d[l], WIN[l], D, IN_W), (wuq_d[l], WUQ[l], 512, 1536),
                                             (wukv_d[l], WUKV[l], 256, 2048), (wout_d[l], WOUT[l], D, D),
                                             (wgu_d[l], WGU[l], D, 2 * FF), (wdn_d[l], WDN[l], FF, D)):
                        for r0 in range(0, Rr, 128):
                            for c0 in range(0, Cc, CW):
                                cw = min(CW, Cc - c0)
                                f, b, e = fb[it % 3], bb[it % 3], engs[it % 3]
                                k.dma(f[:, 0:cw], src[r0:r0 + 128, c0:c0 + cw], writes=[f])
                                cp(e, b[:, 0:cw], f[:, 0:cw], [f], [b])
                                k.dma(dst[r0:r0 + 128, c0:c0 + cw], b[:, 0:cw], reads=[b], q=POOL)
                                it += 1
            k.barrier()

        def phase_mod(layers):
            with contextlib.ExitStack() as st:
                cc = k.sb(st, "cc", [128, 16, 2], F32)
                sc = k.sb(st, "sc", [128, 16, 2], F32)
                mb = k.sb(st, "mb", [128, 96], F32)
                wb = [k.sb(st, f"mw{i}", [128, 16, 512], F32) for i in range(2)]
                mps = k.ps(st, "mps", [128, 96, 2], F32)
                k.dma(cc[:], cc_d[:, :, :], writes=[cc])
                act(sc[:], cc[:], AF.Silu, [cc], [sc])
                it = 0
                for l in layers:
                    k.dma(mb[:], modb_d[l], writes=[mb])
                    for g in range(24):
                        w = wb[it % 2]
                        it += 1
                        k.dma(w[:], modw_d[l][:, g * 512:(g + 1) * 512].rearrange("(kc p) c -> p kc c", p=128), writes=[w])
                        for j in range(4):
                            c = g * 4 + j
                            for kc in range(16):
                                mm(mps[:, c, :], w[:, kc, j * 128:(j + 1) * 128], sc[:, kc, :], kc == 0, kc == 15, [w, sc], [mps])
                    for j in range(2):
                        tt(DVE, modT[:, l, :, j], mps[:, :, j], mb[:], ALU.add, [mps, mb], [modT])
                    for n, (goff, soff) in enumerate(((PV_N1G, 16), (PV_N2G, 64))):
                        for j in range(2):
                            stt(DVE, AB[:, l, n, :, j], modT[:, l, soff:soff + 16, j], 1.0, pvc(l, goff, 16),
                                ALU.add, ALU.mult, [modT, pv], [AB])
            k.barrier()

        def norm_load(st_bufs, src, t0, nt):
            xt, xs, junk, stats, tps, xmT = st_bufs
            for s in range(nt // 128):
                x = xt[s % len(xt)]
                stat = stats[s]
                k.dma(x[:], src[t0 + s * 128:t0 + (s + 1) * 128, :], writes=[x])
                act(junk[:], x[:], AF.Square, [x], [junk, stat], accum_out=stat[:, 0:1])
                act(stat[:, 1:2], stat[:, 0:1], AF.Sqrt, [stat], [stat], scale=1.0 / D, bias=EPS)
                k.op(DVE, lambda: nc.vector.reciprocal(out=stat[:, 2:3], in_=stat[:, 1:2]), [stat], [stat])
                e = DVE if s % 2 == 0 else POOL
                ts(e, xs[s][:], x[:], stat[:, 2:3], None, ALU.mult, None, [x, stat], [xs[s]])

        def norm_T(st_bufs, nt, l, which, is_ctx):
            xt, xs, junk, stats, tps, xmT = st_bufs
            ns = nt // 128
            j = 1 if is_ctx else 0
            shoff = 0 if which == 0 else 48
            for kp in range(8):
                tp = tps[kp % 2]
                for q in range(2):
                    kc = kp * 2 + q
                    for s in range(ns):
                        tr(tp[:, q, s * 128:(s + 1) * 128], xs[s][:, kc * 128:(kc + 1) * 128], identb, [xs[s], cstb], [tp])
                for q in range(2):
                    kc = kp * 2 + q
                    a_ap = AB[:, l, which, kc, j:j + 1]
                    b_ap = modT[:, l, shoff + kc, j:j + 1]
                    if q == 0:
                        act(xmT[:, kc, 0:nt], tp[:, q, 0:nt], AF.Identity, [tp, AB, modT], [xmT], scale=a_ap, bias=b_ap)
                    else:
                        ts(DVE, xmT[:, kc, 0:nt], tp[:, q, 0:nt], a_ap, b_ap, ALU.mult, ALU.add, [tp, AB, modT], [xmT])

        def alloc_norm_bufs(st, nx=2):
            xt = [k.sb(st, f"nm_x{i}", [128, D], F32) for i in range(nx)]
            xs = [k.sb(st, f"nm_xs{i}", [128, D], BF16) for i in range(4)]
            junk = k.sb(st, "nm_junk", [128, D], BF16)
            stat = [k.sb(st, f"nm_stat{i}", [128, 4], F32) for i in range(4)]
            tps = [k.ps(st, f"nm_tp{i}", [128, 2, 512], BF16) for i in range(2)]
            xmT = k.sb(st, "nm_xmT", [128, 16, 512], BF16)
            return xt, xs, junk, stat, tps, xmT

        GROUPS = [
            (C_DAQ, 512, "F", R_DAQ), (C_DAK, 512, "F", R_DAK), (C_DAV, 512, "V", VDA),
            (C_MLQ, 512, "F", R_MLQ), (C_MLK, 512, "F", R_MLK), (C_MLV, 512, "V", VML),
            (C_MLO, 512, "F", R_MLO), (C_GAT, 528, "G", R_CQ), (C_CKV, 320, "F", R_CKV)]

        def phase_A(l, src):
            with contextlib.ExitStack() as st:
                nb = alloc_norm_bufs(st)
                xmT = nb[5]
                wt = [k.sb(st, f"a_w{i}", [128, 16, 528], BF16) for i in range(2)]
                ps = [k.ps(st, f"a_ps{i}", [128, 512], F32) for i in range(4)]
                stg = [k.sb(st, f"a_sf{i}", [128, 512], F32) for i in range(4)]
                stb = [k.sb(st, f"a_sb{i}", [128, 512], BF16) for i in range(4)]
                stgt = [k.sb(st, f"a_sg{i}", [128, 16], F32) for i in range(2)]
                wi = 0
                oi = 0
                norm_load(nb, src, *BLOCKS[0])
                for bi, (t0, nt) in enumerate(BLOCKS):
                    is_ctx = t0 == 0
                    ns = nt // 128
                    norm_T(nb, nt, l, 0, is_ctx)
                    for gi, (c0, ncol, kind, arg) in enumerate(GROUPS):
                        if gi == 2 and bi + 1 < len(BLOCKS):
                            norm_load(nb, src, *BLOCKS[bi + 1])
                        w = wt[wi % 2]
                        wi += 1
                        k.dma(w[:, :, 0:ncol], WIN[l][:, c0:c0 + ncol].rearrange("(kc p) c -> p kc c", p=128), writes=[w])

                        def feat(cofs, rows, rawrow):
                            nonlocal oi
                            p, sf = ps[oi % 4], stg[oi % 4]
                            for kc in range(16):
                                mm(p[0:rows, 0:nt], w[:, kc, cofs:cofs + rows], xmT[:, kc, 0:nt], kc == 0, kc == 15, [w, xmT], [p])
                            cp(ACT if oi % 2 == 0 else DVE, sf[0:rows, 0:nt], p[0:rows, 0:nt], [p], [sf])
                            k.dma(RAWT[rawrow, 0:rows, t0:t0 + nt], sf[0:rows, 0:nt], reads=[sf], q=POOL)
                            oi += 1

                        if kind == "F":
                            nch = (ncol + 127) // 128
                            for ch in range(nch):
                                rows = min(128, ncol - ch * 128)
                                feat(ch * 128, rows, arg + ch)
                        elif kind == "G":
                            for s in range(ns):
                                p, sg = ps[oi % 4], stgt[s % 2]
                                for kc in range(16):
                                    mm(p[:, 0:16], xmT[:, kc, s * 128:(s + 1) * 128], w[:, kc, 0:16], kc == 0, kc == 15, [w, xmT], [p])
                                cp(DVE, sg[:], p[:, 0:16], [p], [sg])
                                k.dma(GATES[t0 + s * 128:t0 + (s + 1) * 128, :], sg[:], reads=[sg], q=POOL)
                                oi += 1
                            for ch in range(4):
                                feat(16 + ch * 128, 128, R_CQ + ch)
                        else:
                            for s in range(ns):
                                p, sbt = ps[oi % 4], stb[oi % 4]
                                for kc in range(16):
                                    mm(p[:, :], xmT[:, kc, s * 128:(s + 1) * 128], w[:, kc, 0:512], kc == 0, kc == 15, [w, xmT], [p])
                                cp(ACT if oi % 2 == 0 else DVE, sbt[:], p[:], [p], [sbt])
                                k.dma(arg[t0 + s * 128:t0 + (s + 1) * 128, :], sbt[:], reads=[sbt], q=POOL)
                                oi += 1
            k.barrier()


        def rstd_from(ss_ps, rows, nt, inv_n, r_sb, R):
            act(r_sb[0:rows, 0:nt], ss_ps[0:rows, 0:nt], AF.Sqrt, [ss_ps] + R, [r_sb], scale=inv_n, bias=EPS)
            k.op(DVE, lambda: nc.vector.reciprocal(out=r_sb[0:rows, 0:nt], in_=r_sb[0:rows, 0:nt]), [r_sb], [r_sb])

        def rope_apply(xn, rows, tl0, nt, ropet, rot_ps, t1, out_t):
            mm(rot_ps[0:rows, 0:nt], cst[0:rows, CS_ROT:CS_ROT + rows], xn[0:rows, 0:nt], True, True, [cst, xn], [rot_ps])
            tt(POOL, t1[0:rows, 0:nt], xn[0:rows, 0:nt], ropet[0:rows, 0, tl0:tl0 + nt], ALU.mult, [xn, ropet], [t1])
            tt(DVE, xn[0:rows, 0:nt], rot_ps[0:rows, 0:nt], ropet[0:rows, 1, tl0:tl0 + nt], ALU.mult, [rot_ps, ropet, xn], [xn])
            tt(POOL, out_t[0:rows, 0:nt], t1[0:rows, 0:nt], xn[0:rows, 0:nt], ALU.add, [t1, xn], [out_t])

        DAQT = dscr("DAQT", [4, 128, NTOK], BF16)
        DAKT = dscr("DAKT", [4, 128, NTOK], BF16)
        MQN = dscr("MQN", [8, 128, NTOK], BF16)
        MQR = dscr("MQR", [8, 64, NTOK], BF16)
        MKN = dscr("MKN", [8, 128, NTOK], BF16)
        MKR = dscr("MKR", [8, 64, NTOK], BF16)
        MV = dscr("MV", [NTOK, 1024], BF16)

        def phase_da_prep(l):
            with contextlib.ExitStack() as st:
                ropet = k.sb(st, "dp_rope", [128, 2, SEQ], F32)
                k.dma(ropet[:], rope_d.rearrange("a p t -> p a t"), writes=[ropet])
                rt = [k.sb(st, f"dp_rt{i}", [128, 512], F32) for i in range(2)]
                sq_ = [k.sb(st, f"dp_sq{i}", [128, 512], F32) for i in range(2)]
                rr_ = [k.sb(st, f"dp_r{i}", [128, 512], F32) for i in range(2)]
                xn_ = [k.sb(st, f"dp_xn{i}", [128, 512], F32) for i in range(2)]
                t1_ = [k.sb(st, f"dp_t1{i}", [128, 512], F32) for i in range(2)]
                ob = [k.sb(st, f"dp_o{i}", [128, 512], BF16) for i in range(2)]
                ss_ = [k.ps(st, f"dp_ss{i}", [128, 512], F32) for i in range(2)]
                rps_ = [k.ps(st, f"dp_rot{i}", [128, 512], F32) for i in range(2)]
                it = 0
                for which, (rrow, dst) in enumerate(((R_DAQ, DAQT), (R_DAK, DAKT))):
                    for h in range(4):
                        for (t0, nt) in BLOCKS:
                            r, o = rt[it % 2], ob[it % 2]
                            sq, rr, xn, t1, ss, rps = sq_[it % 2], rr_[it % 2], xn_[it % 2], t1_[it % 2], ss_[it % 2], rps_[it % 2]
                            it += 1
                            k.dma(r[:, 0:nt], RAWT[rrow + h, :, t0:t0 + nt], writes=[r])
                            act(sq[:, 0:nt], r[:, 0:nt], AF.Square, [r], [sq])
                            mm(ss[:, 0:nt], cst[:, CS_BLK64:CS_BLK64 + 128], sq[:, 0:nt], True, True, [cst, sq], [ss])
                            rstd_from(ss, 128, nt, 1.0 / 64, rr, [])
                            stt(DVE, xn[:, 0:nt], r[:, 0:nt], pvc(l, PV_DAQG + which), rr[:, 0:nt], ALU.mult, ALU.mult, [r, rr, pv], [xn])
                            if t0 == 0:
                                cp(POOL, o[:, 0:nt], xn[:, 0:nt], [xn], [o])
                            else:
                                rope_apply(xn, 128, t0 - CTX, nt, ropet, rps, t1, o)
                            k.dma(dst[h, :, t0:t0 + nt], o[:, 0:nt], reads=[o], q=POOL)
            k.barrier()

        def phase_mla_prep(l):
            with contextlib.ExitStack() as st:
                ropet = k.sb(st, "mp_rope", [64, 2, SEQ], F32)
                k.dma(ropet[:], rope_d[:, 0:64, :].rearrange("a p t -> p a t"), writes=[ropet])
                wuq = k.sb(st, "mp_wuq", [128, 4, 1536], BF16)
                wukv = k.sb(st, "mp_wukv", [128, 2, 2048], BF16)
                k.dma(wuq[:], WUQ[l].rearrange("(c p) n -> p c n", p=128), writes=[wuq])
                k.dma(wukv[:], WUKV[l].rearrange("(c p) n -> p c n", p=128), writes=[wukv])
                rq = k.sb(st, "mp_rq", [128, 4, 512], F32)
                rkv = k.sb(st, "mp_rkv", [128, 2, 512], F32)
                rkp = k.sb(st, "mp_rkp", [64, 512], F32)
                sq4 = k.sb(st, "mp_sq4", [128, 4, 512], F32)
                rr = k.sb(st, "mp_rr", [128, 512], F32)
                cqn = k.sb(st, "mp_cqn", [128, 4, 512], BF16)
                ckvn = k.sb(st, "mp_ckvn", [128, 2, 512], BF16)
                sqN = [k.sb(st, f"mp_sqN{i}", [128, 512], F32) for i in range(2)]
                sqR = [k.sb(st, f"mp_sqR{i}", [64, 512], F32) for i in range(2)]
                sqRk = k.sb(st, "mp_sqRk", [64, 512], F32)
                kpr = k.sb(st, "mp_kpr", [64, 512], F32)
                kprb = k.sb(st, "mp_kprb", [64, 512], BF16)
                rh = [k.sb(st, f"mp_rh{i}", [128, 512], F32) for i in range(2)]
                qr0 = [k.sb(st, f"mp_qr0{i}", [64, 512], F32) for i in range(2)]
                t1 = k.sb(st, "mp_t1", [64, 512], F32)
                on = [k.sb(st, f"mp_on{i}", [128, 512], BF16) for i in range(3)]
                orr = [k.sb(st, f"mp_or{i}", [64, 512], BF16) for i in range(3)]
                vb = [k.sb(st, f"mp_vb{i}", [128, 512], BF16) for i in range(2)]
                ss = [k.ps(st, f"mp_ss{i}", [128, 512], F32) for i in range(2)]
                pn = [k.ps(st, f"mp_pn{i}", [128, 512], F32) for i in range(2)]
                pr = [k.ps(st, f"mp_pr{i}", [64, 512], F32) for i in range(2)]
                rps = k.ps(st, "mp_rot", [64, 512], F32)
                gq = lambda c: pvc(l, PV_MQG + c)
                gkv = lambda c: pvc(l, PV_MKVG + c)
                it = 0
                vi = 0
                for (t0, nt) in BLOCKS:
                    is_ctx = t0 == 0
                    ns = nt // 128
                    k.dma(rq[:, :, 0:nt], RAWT[R_CQ:R_CQ + 4, :, t0:t0 + nt].rearrange("c p t -> p c t"), writes=[rq])
                    k.dma(rkv[:, :, 0:nt], RAWT[R_CKV:R_CKV + 2, :, t0:t0 + nt].rearrange("c p t -> p c t"), writes=[rkv])
                    k.dma(rkp[:, 0:nt], RAWT[R_KPE, 0:64, t0:t0 + nt], writes=[rkp])
                    act(sq4[:, :, 0:nt], rq[:, :, 0:nt], AF.Square, [rq], [sq4])
                    for c in range(4):
                        mm(ss[0][:, 0:nt], onesf, sq4[:, c, 0:nt], c == 0, c == 3, [cst, sq4], [ss[0]])
                    rstd_from(ss[0], 128, nt, 1.0 / 512, rr, [])
                    for c in range(4):
                        stt(DVE, cqn[:, c, 0:nt], rq[:, c, 0:nt], gq(c), rr[:, 0:nt], ALU.mult, ALU.mult, [rq, rr, pv], [cqn])
                    act(sq4[:, 0:2, 0:nt], rkv[:, :, 0:nt], AF.Square, [rkv], [sq4])
                    for c in range(2):
                        mm(ss[1][:, 0:nt], onesf, sq4[:, c, 0:nt], c == 0, c == 1, [cst, sq4], [ss[1]])
                    rstd_from(ss[1], 128, nt, 1.0 / 256, rr, [])
                    for c in range(2):
                        stt(DVE, ckvn[:, c, 0:nt], rkv[:, c, 0:nt], gkv(c), rr[:, 0:nt], ALU.mult, ALU.mult, [rkv, rr, pv], [ckvn])
                    act(sqRk[:, 0:nt], rkp[:, 0:nt], AF.Square, [rkp], [sqRk])
                    ts(DVE, kpr[:, 0:nt], rkp[:, 0:nt], pv[0:64, l * PVL + PV_MQKR + 1:l * PVL + PV_MQKR + 2], None, ALU.mult, None, [rkp, pv], [kpr])
                    if not is_ctx:
                        rope_apply(kpr, 64, t0 - CTX, nt, ropet, rps, t1, kprb)
                        cp(POOL, kpr[:, 0:nt], kprb[:, 0:nt], [kprb], [kpr])
                    for s in range(ns):
                        for g2 in range(2):
                            p, v = pn[vi % 2], vb[vi % 2]
                            vi += 1
                            for c in range(2):
                                mm(p[:, :], ckvn[:, c, s * 128:(s + 1) * 128], wukv[:, c, 1024 + g2 * 512:1024 + (g2 + 1) * 512], c == 0, c == 1, [ckvn, wukv], [p])
                            cp(ACT, v[:], p[:], [p], [v])
                            k.dma(MV[t0 + s * 128:t0 + (s + 1) * 128, g2 * 512:(g2 + 1) * 512], v[:], reads=[v], q=POOL)
                    for h in range(8):
                        for side in range(2):
                            b = it % 2
                            it += 1
                            PN, PR, SS, SQN, SQR, RH = pn[b], pr[b], ss[b], sqN[b], sqR[b], rh[b]
                            o_n, o_r = on[it % 3], orr[it % 3]
                            if side == 0:
                                for c in range(4):
                                    mm(PN[:, 0:nt], wuq[:, c, h * 192:h * 192 + 128], cqn[:, c, 0:nt], c == 0, c == 3, [wuq, cqn], [PN])
                                for c in range(4):
                                    mm(PR[:, 0:nt], wuq[:, c, h * 192 + 128:h * 192 + 192], cqn[:, c, 0:nt], c == 0, c == 3, [wuq, cqn], [PR])
                                act(SQN[:, 0:nt], PN[:, 0:nt], AF.Square, [PN], [SQN])
                                act(SQR[:, 0:nt], PR[:, 0:nt], AF.Square, [PR], [SQR])
                                sqr_t = SQR
                            else:
                                for c in range(2):
                                    mm(PN[:, 0:nt], wukv[:, c, h * 128:(h + 1) * 128], ckvn[:, c, 0:nt], c == 0, c == 1, [wukv, ckvn], [PN])
                                act(SQN[:, 0:nt], PN[:, 0:nt], AF.Square, [PN], [SQN])
                                sqr_t = sqRk
                            mm(SS[:, 0:nt], onesf, SQN[:, 0:nt], True, False, [cst, SQN], [SS])
                            mm(SS[:, 0:nt], cst[0:64, CS_ONES:CS_ONES + 128], sqr_t[:, 0:nt], False, True, [cst, sqr_t], [SS])
                            rstd_from(SS, 128, nt, 1.0 / 192, RH, [])
                            stt(DVE, o_n[:, 0:nt], PN[:, 0:nt], pvc(l, PV_MQKN + side), RH[:, 0:nt], ALU.mult, ALU.mult, [PN, RH, pv], [o_n])
                            if side == 0:
                                Q0 = qr0[b]
                                stt(DVE, Q0[:, 0:nt], PR[:, 0:nt], pv[0:64, l * PVL + PV_MQKR:l * PVL + PV_MQKR + 1], RH[0:64, 0:nt], ALU.mult, ALU.mult, [PR, RH, pv], [Q0])
                                if is_ctx:
                                    cp(POOL, o_r[:, 0:nt], Q0[:, 0:nt], [Q0], [o_r])
                                else:
                                    rope_apply(Q0, 64, t0 - CTX, nt, ropet, rps, t1, o_r)
                                k.dma(MQN[h, :, t0:t0 + nt], o_n[:, 0:nt], reads=[o_n], q=POOL)
                                k.dma(MQR[h, :, t0:t0 + nt], o_r[:, 0:nt], reads=[o_r], q=POOL)
                            else:
                                tt(POOL, o_r[:, 0:nt], kpr[:, 0:nt], RH[0:64, 0:nt], ALU.mult, [kpr, RH], [o_r])
                                k.dma(MKN[h, :, t0:t0 + nt], o_n[:, 0:nt], reads=[o_n], q=POOL)
                                k.dma(MKR[h, :, t0:t0 + nt], o_r[:, 0:nt], reads=[o_r], q=POOL)
            k.barrier()

        def phase_attn(l, kind):
            need_ctx = l < DEPTH - 1
            da = kind == "da"
            ncomp = 2 if da else 1
            nh = 4 if da else 8
            scale = (64 if da else 192) ** -0.5
            lam_init = 0.8 - 0.6 * math.exp(-0.3 * l)
            with contextlib.ExitStack() as st:
                K0 = k.sb(st, "at_k0", [128, NTOK], BF16)
                K1 = None if da else k.sb(st, "at_k1", [64, NTOK], BF16)
                Vg = k.sb(st, "at_v", [128, NT128, 512], BF16)
                Q0 = [k.sb(st, f"at_q0{i}", [128, 512], BF16) for i in range(2)]
                Q1 = None if da else [k.sb(st, f"at_q1{i}", [64, 512], BF16) for i in range(2)]
                SD = 3 if da else 4
                S = [[k.ps(st, f"at_s{c}{b}", [128, 512], F32) for b in range(SD)] for c in range(ncomp)]
                O2 = [k.ps(st, f"at_o{c}", [128, 512], F32) for c in range(2)]
                Dps = None if da else k.ps(st, "at_d", [128, 512], F32)
                E = [[k.sb(st, f"at_e{c}{b}", [128, 512], BF16) for b in range(SD)] for c in range(ncomp)]
                Es = [[k.sb(st, f"at_es{c}{p}", [128, 512], F32) for p in range(2)] for c in range(ncomp)]
                rec = [k.sb(st, f"at_rec{c}", [128, 512], F32) for c in range(2)]
                oa = [k.sb(st, f"at_oa{c}", [128, 512], F32) for c in range(2)]
                sqt = k.sb(st, "at_sq", [128, 512], F32)
                ob = [k.sb(st, f"at_ob{i}", [128, 512], BF16) for i in range(2)]
                sm = k.sb(st, "at_sm", [128, 8], F32)
                junk = k.sb(st, "at_junk", [128, 64], F32)
                if da:
                    lamv = pvc(l, PV_DALAM, 256)
                    for i in range(2):
                        tt(DVE, junk[:], pv[:, l * PVL + PV_DALAM + 128 * i:l * PVL + PV_DALAM + 128 * i + 64],
                           pv[:, l * PVL + PV_DALAM + 128 * i + 64:l * PVL + PV_DALAM + 128 * i + 128], ALU.mult, [pv], [junk])
                        act(junk[:], junk[:], AF.Identity, [junk], [junk, sm], accum_out=sm[:, i:i + 1])
                        act(sm[:, 2 + i:3 + i], sm[:, i:i + 1], AF.Exp, [sm], [sm])
                    tt(DVE, sm[:, 4:5], sm[:, 3:4], sm[:, 2:3], ALU.subtract, [sm], [sm])
                    ts(DVE, sm[:, 4:5], sm[:, 4:5], -lam_init, None, ALU.add, None, [sm], [sm])
                    ts(DVE, sm[:, 5:6], pvc(l, PV_DAOG), 1.0 - lam_init, None, ALU.mult, None, [pv], [sm])
                qblocks = BLOCKS if need_ctx else BLOCKS[1:]
                qi = 0
                for h in range(nh):
                    if h % 4 == 0:
                        vsrc = VDA if da else MV[:, (h // 4) * 512:(h // 4 + 1) * 512]
                        k.dma(Vg[:], vsrc.rearrange("(kt p) v -> p kt v", p=128), writes=[Vg])
                    if da:
                        k.dma(K0[:], DAKT[h], writes=[K0])
                    else:
                        k.dma(K0[:], MKN[h], writes=[K0])
                        k.dma(K1[:], MKR[h], writes=[K1])
                    vcol = (h % 4) * 128
                    for (t0, nt) in qblocks:
                        is_ctx = t0 == 0
                        nkt = 2 if is_ctx else NT128
                        q0 = Q0[qi % 2]
                        q1 = None if da else Q1[qi % 2]
                        qi += 1
                        if da:
                            k.dma(q0[:, 0:nt], DAQT[h, :, t0:t0 + nt], writes=[q0])
                        else:
                            k.dma(q0[:, 0:nt], MQN[h, :, t0:t0 + nt], writes=[q0])
                            k.dma(q1[:, 0:nt], MQR[h, :, t0:t0 + nt], writes=[q1])

                        def scores(kt):
                            b = kt % SD
                            ks = slice(kt * 128, (kt + 1) * 128)
                            if da:
                                for c in range(2):
                                    mm(S[c][b][:, 0:nt], K0[64 * c:64 * c + 64, ks], q0[64 * c:64 * c + 64, 0:nt], True, True, [K0, q0], [S[c][b]])
                            else:
                                mm(S[0][b][:, 0:nt], K0[:, ks], q0[:, 0:nt], True, False, [K0, q0], [S[0][b]])
                                mm(S[0][b][:, 0:nt], K1[:, ks], q1[:, 0:nt], False, True, [K1, q1], [S[0][b]])

                        O = O2 if da else [O2[qi % 2]]
                        LA = SD - 2
                        for j0 in range(min(LA, nkt)):
                            scores(j0)

                        def pv_step(kt):
                            b = kt % SD
                            for c in range(ncomp):
                                mm(O[c][:, 0:nt], Vg[:, kt, vcol:vcol + 128], E[c][b][:, 0:nt], kt == 0, kt == nkt - 1, [Vg, E[c][b]], [O[c]])
                                p_ = kt % 2
                                e_ = DVE if (c + kt) % 2 == 0 else POOL
                                if kt < 2:
                                    cp(e_, Es[c][p_][:, 0:nt], E[c][b][:, 0:nt], [E[c][b]], [Es[c][p_]])
                                else:
                                    tt(e_, Es[c][p_][:, 0:nt], Es[c][p_][:, 0:nt], E[c][b][:, 0:nt], ALU.add, [Es[c][p_], E[c][b]], [Es[c][p_]])

                        for kt in range(nkt):
                            b = kt % SD
                            if kt + LA < nkt:
                                scores(kt + LA)
                            for c in range(ncomp):
                                act(E[c][b][:, 0:nt], S[c][b][:, 0:nt], AF.Exp, [S[c][b]], [E[c][b]], scale=scale)
                            if kt >= 1:
                                pv_step(kt - 1)
                        pv_step(nkt - 1)
                        Dn = [S[c][0] for c in range(ncomp)] if da else [Dps]
                        for c in range(ncomp):
                            mm(Dn[c][:, 0:nt], onesf, Es[c][0][:, 0:nt], True, False, [cst, Es[c][0]], [Dn[c]])
                            mm(Dn[c][:, 0:nt], onesf, Es[c][1][:, 0:nt], False, True, [cst, Es[c][1]], [Dn[c]])
                        o = ob[qi % 2]
                        for c in range(ncomp):
                            k.op(DVE, lambda c=c: nc.vector.reciprocal(out=rec[c][:, 0:nt], in_=Dn[c][:, 0:nt]), [Dn[c]], [rec[c]])
                        if da:
                            for c in range(2):
                                tt(DVE, oa[c][:, 0:nt], O[c][:, 0:nt], rec[c][:, 0:nt], ALU.mult, [O[c], rec[c]], [oa[c]])
                            stt(DVE, oa[0][:, 0:nt], oa[1][:, 0:nt], sm[:, 4:5], oa[0][:, 0:nt], ALU.mult, ALU.add, [oa[0], oa[1], sm], [oa[0]])
                            act(sqt[:, 0:nt], oa[0][:, 0:nt], AF.Square, [oa[0]], [sqt])
                            ms = S[0][1]
                            mm(ms[:, 0:nt], onesf, sqt[:, 0:nt], True, True, [cst, sqt], [ms])
                            act(sqt[:, 0:nt], ms[:, 0:nt], AF.Ln, [ms], [sqt], scale=1.0 / 128, bias=EPS)
                            act(sqt[:, 0:nt], sqt[:, 0:nt], AF.Exp, [sqt], [sqt], scale=-0.5)
                            stt(DVE, o[:, 0:nt], oa[0][:, 0:nt], sm[:, 5:6], sqt[:, 0:nt], ALU.mult, ALU.mult, [oa[0], sqt, sm], [o])
                            k.dma(MIXT[h, :, t0:t0 + nt], o[:, 0:nt], reads=[o], q=POOL)
                        else:
                            tt(DVE, o[:, 0:nt], O[0][:, 0:nt], rec[0][:, 0:nt], ALU.mult, [O[0], rec[0]], [o])
                            k.dma(MIXT[8 + h, :, t0:t0 + nt], o[:, 0:nt], reads=[o], q=POOL)
            k.barrier()


        LQT = dscr("LQT", [4, 128, NTOK], BF16)
        LKT = dscr("LKT", [4, 128, NTOK], BF16)
        LK = dscr("LK", [NTOK, 512], BF16)

        def phase_ml_prep(l):
            with contextlib.ExitStack() as st:
                xr = [k.sb(st, f"lp_x{i}", [128, NTOK], F32) for i in range(2)]
                y = k.sb(st, "lp_y", [128, NTOK], F32)
                ob = [k.sb(st, f"lp_o{i}", [128, NTOK], BF16) for i in range(2)]
                tp = [k.ps(st, f"lp_tp{i}", [128, 4, 128], BF16) for i in range(2)]
                tb = [k.sb(st, f"lp_tb{i}", [128, 4, 128], BF16) for i in range(2)]
                ti = 0
                for ch in range(8):
                    x, o = xr[ch % 2], ob[ch % 2]
                    rrow = (R_MLQ + ch) if ch < 4 else (R_MLK + ch - 4)
                    k.dma(x[:, 0:2304], RAWT[rrow, :, 0:2304], writes=[x])
                    k.dma(x[:, 2304:NTOK], RAWT[rrow, :, 2304:NTOK], writes=[x])
                    w = lambda j: pvc(l, PV_MLCW + ch * 3 + j)
                    for (a, b) in ((0, CTX), (CTX, NTOK)):
                        for c0 in range(a, b, 512):
                            c1 = min(c0 + 512, b)
                            ts(DVE, y[:, c0:c1], x[:, c0:c1], w(1), pvc(l, PV_MLCB + ch), ALU.mult, ALU.add, [x, pv], [y])
                            lo = max(c0, a + 1)
                            stt(DVE, y[:, lo:c1], x[:, lo - 1:c1 - 1], w(0), y[:, lo:c1], ALU.mult, ALU.add, [x, y, pv], [y])
                            hi = min(c1, b - 1)
                            stt(DVE, y[:, c0:hi], x[:, c0 + 1:hi + 1], w(2), y[:, c0:hi], ALU.mult, ALU.add, [x, y, pv], [y])
                            act(y[:, c0:c1], y[:, c0:c1], AF.Silu, [y], [y])
                            if ch < 4:
                                cp(POOL, o[:, c0:c1], y[:, c0:c1], [y], [o])
                            else:
                                ts(POOL, o[:, c0:c1], y[:, c0:c1], 128.0 ** -0.5, None, ALU.mult, None, [y], [o])
                    if ch < 4:
                        k.dma(LQT[ch], o[:], reads=[o], q=POOL)
                    else:
                        k.dma(LKT[ch - 4], o[:], reads=[o], q=POOL)
                        h = ch - 4
                        for g in range(0, NT128, 4):
                            n = min(4, NT128 - g)
                            p, t = tp[ti % 2], tb[ti % 2]
                            ti += 1
                            for j in range(n):
                                tr(p[:, j, :], o[:, (g + j) * 128:(g + j + 1) * 128], identb, [o, cstb], [p])
                            cp(ACT if ti % 2 == 0 else DVE, t[:, 0:n, :], p[:, 0:n, :], [p], [t])
                            k.dma(LK[g * 128:(g + n) * 128, h * 128:(h + 1) * 128].rearrange("(j p) d -> p j d", p=128), t[:, 0:n, :], reads=[t], q=POOL)
            k.barrier()

        def phase_ml_scan(l):
            with contextlib.ExitStack() as st:
                Graw = k.sb(st, "ls_graw", [128, NT128, 16], F32)
                G = k.sb(st, "ls_g", [128, NT128, 16], F32)
                LF = k.sb(st, "ls_lf", [128, NT128, 16], F32)
                k.dma(Graw[:], GATES.rearrange("(kt p) g -> p kt g", p=128), writes=[Graw])
                for g in range(16):
                    act(G[:, :, g], Graw[:, :, g], AF.Identity, [Graw, pv], [G], bias=pvc(l, PV_MLGB + g))
                act(LF[:], G[:], AF.Exp, [G], [LF], scale=-1.0)
                act(LF[:], LF[:], AF.Ln, [LF], [LF], bias=1.0)
                ts(DVE, LF[:], LF[:], -1.0, None, ALU.mult, None, [LF], [LF])
                QT = [k.sb(st, f"ls_qt{i}", [128, NTOK], BF16) for i in range(2)]
                KT = [k.sb(st, f"ls_kt{i}", [128, NTOK], BF16) for i in range(2)]
                Kk = [k.sb(st, f"ls_kk{i}", [128, NT128, 128], BF16) for i in range(2)]
                Vv = [k.sb(st, f"ls_vv{i}", [128, NT128, 128], BF16) for i in range(2)]
                Hd = [[k.sb(st, f"ls_h{i}{d}", [128, NTOK], F32) for d in range(2)] for i in range(2)]
                sq = k.sb(st, "ls_sq", [128, 512], F32)
                rs = k.sb(st, "ls_rs", [128, 512], F32)
                og = [k.sb(st, f"ls_og{i}", [128, 512], F32) for i in range(2)]
                ob = [k.sb(st, f"ls_ob{i}", [128, 512], BF16) for i in range(2)]
                pM = k.ps(st, "ls_pM", [128, 512], F32)

                class Ch:
                    pass
                chs = []
                for c in range(4):
                    o = Ch()
                    o.hi, o.d = c // 2, c % 2
                    f32t = lambda n, w=128: k.sb(st, f"ls_{n}{c}", [128, w], F32)
                    b16t = lambda n: k.sb(st, f"ls_{n}{c}", [128, 128], BF16)
                    o.LFb, o.ET, o.EB, o.Em, o.dn, o.Cs = f32t("lfb"), f32t("et"), f32t("eb"), f32t("em"), f32t("dn"), f32t("cs")
                    o.bias, o.Ns = f32t("bias", 2), f32t("ns", 2)
                    o.PT, o.Qd, o.Kw, o.Cb, o.Nb = b16t("pt"), b16t("qd"), b16t("kw"), b16t("cb"), b16t("nb")
                    o.bk = k.ps(st, f"ls_bk{c}", [128, 512], F32)
                    o.pB = o.pN = o.bk[:, 0:128]
                    o.pS = o.pD = o.bk[:, 128:256]
                    o.pC, o.pn, o.pb = o.bk[:, 256:384], o.bk[:, 384:386], o.bk[:, 386:388]
                    o.tri = cst[:, CS_U:CS_U + 128] if o.d == 0 else cst[:, CS_L:CS_L + 128]
                    o.ecol = 127 if o.d == 0 else 0
                    o.order = list(range(NT128)) if o.d == 0 else [1, 0] + list(range(NT128 - 1, 1, -1))
                    chs.append(o)
                oi = 0
                for hp in range(2):
                    for i in range(2):
                        h = 2 * hp + i
                        k.dma(QT[i][:], LQT[h], writes=[QT[i]])
                        k.dma(KT[i][:], LKT[h], writes=[KT[i]])
                        k.dma(Kk[i][:], LK[:, h * 128:(h + 1) * 128].rearrange("(kt p) d -> p kt d", p=128), writes=[Kk[i]])
                        k.dma(Vv[i][:], VML[:, h * 128:(h + 1) * 128].rearrange("(kt p) d -> p kt d", p=128), writes=[Vv[i]])
                    for step in range(NT128):
                        first, last = step == 0, step == NT128 - 1
                        for o in chs:
                            o.h = 2 * hp + o.hi
                            o.kt = o.order[step]
                            o.tk = slice(o.kt * 128, (o.kt + 1) * 128)
                            o.gi, o.gf = (o.h, 4 + o.h) if o.d == 0 else (8 + o.h, 12 + o.h)
                        for o in chs:
                            act(o.LFb[:], onesf, AF.Copy, [cst, LF], [o.LFb], scale=LF[:, o.kt, o.gf:o.gf + 1])
                        for o in chs:
                            mm(o.pB, o.LFb[:], o.tri, True, True, [o.LFb, cst], [o.bk])
                            mm(o.pb, o.tri, LF[:, o.kt, o.gf - 1:o.gf + 1], True, True, [cst, LF], [o.bk])
                            mm(o.pS, KT[o.hi][:, o.tk], QT[o.hi][:, o.tk], True, True, [KT[o.hi], QT[o.hi]], [o.bk])
                        for o in chs:
                            tt(DVE, o.bias[:, 0:1], G[:, o.kt, o.gi:o.gi + 1], o.bk[:, 387:388], ALU.subtract, [G, o.bk], [o.bias])
                            act(o.ET[:], o.pB, AF.Exp, [o.bk, o.bias], [o.ET], bias=o.bias[:, 0:1])
                            act(o.EB[:], o.pB, AF.Exp, [o.bk], [o.EB])
                        for o in chs:
                            tt(POOL, o.Em[:], o.ET[:], o.tri, ALU.mult, [o.ET, cst], [o.Em])
                            tt(DVE, o.PT[:], o.Em[:], o.pS, ALU.mult, [o.Em, o.bk], [o.PT])
                            if not first:
                                tt(POOL, o.Qd[:], QT[o.hi][:, o.tk], o.EB[:], ALU.mult, [QT[o.hi], o.EB], [o.Qd])
                        for o in chs:
                            mm(o.pN, Vv[o.hi][:, o.kt, :], o.PT[:], True, first, [Vv[o.hi], o.PT], [o.bk])
                            if not first:
                                mm(o.pN, o.Cb[:], o.Qd[:], False, True, [o.Cb, o.Qd], [o.bk])
                            mm(o.pD, onesb, o.PT[:], True, first, [cstb, o.PT], [o.bk])
                            if not first:
                                mm(o.pD, o.Nb[:], o.Qd[:], False, True, [o.Nb, o.Qd], [o.bk])
                        for o in chs:
                            Hh = Hd[o.hi][o.d]
                            ts(DVE, o.dn[:], o.pD, -1.0, 1.0, ALU.mult, ALU.max, [o.bk], [o.dn])
                            stt(DVE, o.dn[:], o.pD, 1.0, o.dn[:], ALU.max, ALU.max, [o.bk, o.dn], [o.dn])
                            k.op(DVE, lambda o=o: nc.vector.reciprocal(out=o.dn[:], in_=o.dn[:]), [o.dn], [o.dn])
                            tt(DVE, Hh[:, o.tk], o.pN, o.dn[:], ALU.mult, [o.bk, o.dn], [Hh])
                        if last:
                            continue
                        for o in chs:
                            act(o.Kw[:], Kk[o.hi][:, o.kt, :], AF.Copy, [Kk[o.hi], o.ET], [o.Kw], scale=o.ET[:, o.ecol:o.ecol + 1])
                        for o in chs:
                            mm(o.pC, o.Kw[:], Vv[o.hi][:, o.kt, :], True, True, [o.Kw, Vv[o.hi]], [o.bk])
                            mm(o.pn, o.Kw[:], cstb[:, 128:130], True, True, [o.Kw, cstb], [o.bk])
                        for o in chs:
                            if first:
                                cp(DVE, o.Cs[:], o.pC, [o.bk], [o.Cs])
                                cp(DVE, o.Ns[:], o.pn, [o.bk], [o.Ns])
                            else:
                                dec = o.EB[:, o.ecol:o.ecol + 1]
                                stt(DVE, o.Cs[:], o.Cs[:], dec, o.pC, ALU.mult, ALU.add, [o.Cs, o.EB, o.bk], [o.Cs])
                                stt(DVE, o.Ns[:], o.Ns[:], dec, o.pn, ALU.mult, ALU.add, [o.Ns, o.EB, o.bk], [o.Ns])
                            cp(ACT, o.Cb[:], o.Cs[:], [o.Cs], [o.Cb])
                            act(o.Nb[:], onesf, AF.Copy, [cst, o.Ns], [o.Nb], scale=o.Ns[:, 0:1])
                    for i in range(2):
                        h = 2 * hp + i
                        for (t0, nt) in BLOCKS:
                            o_, g_ = ob[oi % 2], og[oi % 2]
                            oi += 1
                            k.dma(g_[:, 0:nt], RAWT[R_MLO + h, :, t0:t0 + nt], writes=[g_])
                            act(g_[:, 0:nt], g_[:, 0:nt], AF.Sigmoid, [g_], [g_])
                            tt(POOL, rs[:, 0:nt], Hd[i][0][:, t0:t0 + nt], Hd[i][1][:, t0:t0 + nt], ALU.add, [Hd[i][0], Hd[i][1]], [rs])
                            act(sq[:, 0:nt], rs[:, 0:nt], AF.Square, [rs], [sq])
                            mm(pM[:, 0:nt], onesf, sq[:, 0:nt], True, True, [cst, sq], [pM])
                            stt(DVE, sq[:, 0:nt], rs[:, 0:nt], pvc(l, PV_MLOG + h), g_[:, 0:nt], ALU.mult, ALU.mult, [rs, g_, pv], [sq])
                            rstd_from(pM, 128, nt, 1.0 / 128, rs, [sq])
                            tt(POOL, o_[:, 0:nt], sq[:, 0:nt], rs[:, 0:nt], ALU.mult, [sq, rs], [o_])
                            k.dma(MIXT[4 + h, :, t0:t0 + nt], o_[:, 0:nt], reads=[o_], q=POOL)
            k.barrier()

        def build_gate_bcast(Gb, l, choff, j, ps_t, dg):
            for c in range(16):
                ts(POOL, dg[:], identf, modT[:, l, choff + c, j:j + 1], None, ALU.mult, None, [cst, modT], [dg])
                mm(ps_t[:, 0:128], onesf, dg[:], True, True, [cst, dg], [ps_t])
                cp(DVE, Gb[:, c * 128:(c + 1) * 128], ps_t[:, 0:128], [ps_t], [Gb])

        def phase_wout(l, src):
            need_ctx = l < DEPTH - 1
            with contextlib.ExitStack() as st:
                wo = k.sb(st, "o_w", [128, 16, D], BF16)
                for g in range(4):
                    k.dma(wo[:, g * 4:(g + 1) * 4, :], WOUT[l][g * 512:(g + 1) * 512, :].rearrange("(mc p) n -> p mc n", p=128), writes=[wo])
                Gb = [k.sb(st, f"o_gb{j}", [128, D], F32) for j in range(2)]
                dg = k.sb(st, "o_dg", [128, 128], F32)
                ps = [k.ps(st, f"o_ps{i}", [128, 512], F32) for i in range(4)]
                for j in range(2 if need_ctx else 1):
                    build_gate_bcast(Gb[j], l, 32, j, ps[0], dg)
                mx = [k.sb(st, f"o_mx{i}", [128, 16, 512], BF16) for i in range(2)]
                xt = [k.sb(st, f"o_x{i}", [128, D], F32) for i in range(2)]
                xo = [k.sb(st, f"o_xo{i}", [128, D], F32) for i in range(2)]
                tmp = [k.sb(st, f"o_t{i}", [128, 512], F32) for i in range(2)]
                bi = 0
                ti = 0
                for (t0, nt) in (BLOCKS if need_ctx else BLOCKS[1:]):
                    j = 1 if t0 == 0 else 0
                    m = mx[bi % 2]
                    bi += 1
                    k.dma(m[:, :, 0:nt], MIXT[:, :, t0:t0 + nt].rearrange("c p t -> p c t"), writes=[m])
                    for s in range(nt // 128):
                        x, o = xt[ti % 2], xo[ti % 2]
                        r0 = t0 + s * 128
                        k.dma(x[:], src[r0:r0 + 128, :], writes=[x])
                        for n in range(4):
                            p, t = ps[(ti * 4 + n) % 4], tmp[n % 2]
                            ns_ = slice(n * 512, (n + 1) * 512)
                            for mc in range(16):
                                mm(p[:], m[:, mc, s * 128:(s + 1) * 128], wo[:, mc, ns_], mc == 0, mc == 15, [m, wo], [p])
                            tt(DVE, t[:], p[:], Gb[j][:, ns_], ALU.mult, [p, Gb[j]], [t])
                            tt(POOL, o[:, ns_], t[:], x[:, ns_], ALU.add, [t, x], [o])
                        k.dma(XH[r0:r0 + 128, :], o[:], reads=[o], q=POOL)
                        ti += 1
            k.barrier()

        def phase_ffn(l):
            need_ctx = l < DEPTH - 1
            last = l == n_layers - 1 and l == DEPTH - 1
            with contextlib.ExitStack() as st:
                nb = alloc_norm_bufs(st, nx=2)
                xmT = nb[5]
                Gb = k.sb(st, "f_gb", [128, D], F32)
                dg = k.sb(st, "f_dg", [128, 128], F32)
                wg = [k.sb(st, f"f_wg{i}", [128, 16, 256], BF16) for i in range(2)]
                wu = [k.sb(st, f"f_wu{i}", [128, 16, 256], BF16) for i in range(2)]
                wd = [k.sb(st, f"f_wd{i}", [128, 11, 512], BF16) for i in range(2)]
                hid = k.sb(st, "f_hid", [128, 44, 512], BF16)
                sg = [k.sb(st, f"f_sg{i}", [128, 512], F32) for i in range(2)]
                hx = [k.sb(st, f"f_hx{i}", [128, 512], F32) for i in range(2)]
                ot = [k.sb(st, f"f_ot{i}", [128, 512], F32) for i in range(2)]
                acc = [k.ps(st, f"f_acc{i}", [128, 512], F32) for i in range(4)]
                pg = k.ps(st, "f_pg", [128, 512], F32)
                pu = k.ps(st, "f_pu", [128, 512], F32)
                wi = 0
                di = 0
                oi = 0
                gb_for = None
                fblocks = BLOCKS if need_ctx else BLOCKS[1:]
                norm_load(nb, XH, *fblocks[0])
                for bi, (t0, nt) in enumerate(fblocks):
                    is_ctx = t0 == 0
                    j = 1 if is_ctx else 0
                    ns = nt // 128
                    if gb_for != j:
                        build_gate_bcast(Gb, l, 80, j, pg, dg)
                        gb_for = j
                    norm_T(nb, nt, l, 1, is_ctx)
                    for jp in range(22):
                        g_, u_ = wg[wi % 2], wu[wi % 2]
                        wi += 1
                        k.dma(g_[:], WGU[l][:, jp * 256:(jp + 1) * 256].rearrange("(kc p) c -> p kc c", p=128), writes=[g_])
                        k.dma(u_[:], WGU[l][:, FF + jp * 256:FF + (jp + 1) * 256].rearrange("(kc p) c -> p kc c", p=128), writes=[u_])
                        for q in range(2):
                            jj = jp * 2 + q
                            for kc in range(16):
                                mm(pg[:, 0:nt], g_[:, kc, q * 128:(q + 1) * 128], xmT[:, kc, 0:nt], kc == 0, kc == 15, [g_, xmT], [pg])
                            for kc in range(16):
                                mm(pu[:, 0:nt], u_[:, kc, q * 128:(q + 1) * 128], xmT[:, kc, 0:nt], kc == 0, kc == 15, [u_, xmT], [pu])
                            s_ = sg[jj % 2]
                            act(s_[:, 0:nt], pg[:, 0:nt], AF.Silu, [pg], [s_])
                            tt(DVE, hid[:, jj, 0:nt], s_[:, 0:nt], pu[:, 0:nt], ALU.mult, [s_, pu], [hid])
                    if bi + 1 < len(fblocks):
                        norm_load(nb, XH, *fblocks[bi + 1])
                    for n in range(4):
                        ns_ = slice(n * 512, (n + 1) * 512)
                        for jg in range(4):
                            w_ = wd[di % 2]
                            di += 1
                            k.dma(w_[:], WDN[l][jg * 11 * 128:(jg + 1) * 11 * 128, ns_].rearrange("(j p) n -> p j n", p=128), writes=[w_])
                            for s in range(ns):
                                for jx in range(11):
                                    jj = jg * 11 + jx
                                    mm(acc[s][:], hid[:, jj, s * 128:(s + 1) * 128], w_[:, jx, :], jj == 0, jj == 43, [hid, w_], [acc[s]])
                        for s in range(ns):
                            h_, o_ = hx[oi % 2], ot[oi % 2]
                            oi += 1
                            r0 = t0 + s * 128
                            k.dma(h_[:], XH[r0:r0 + 128, ns_], writes=[h_], q=POOL)
                            tt(DVE, o_[:], acc[s][:], Gb[:, ns_], ALU.mult, [acc[s], Gb], [o_])
                            tt(POOL, o_[:], o_[:], h_[:], ALU.add, [o_, h_], [o_])
                            if last:
                                k.dma(y_d[r0 - CTX:r0 - CTX + 128, ns_], o_[:], reads=[o_], q=POOL)
                            else:
                                k.dma(XR[r0:r0 + 128, ns_], o_[:], reads=[o_], q=POOL)
            k.barrier()

        def done(tag):
            return stop_after == tag

        pre = "SKIPPRE" not in dump
        if pre:
            phase_cast(range(n_layers))
        if pre and not done("cast"):
            phase_mod(range(n_layers))
        if not done("cast") and not done("mod"):
            for l in range(n_layers):
                src = xin if l == 0 else XR
                if pre:
                    phase_A(l, src)
                if done(f"A{l}"):
                    break
                if "SKIPDA" not in dump:
                    phase_da_prep(l)
                    phase_attn(l, "da")
                if done(f"DA{l}"):
                    break
                phase_ml_prep(l)
                if done(f"LP{l}"):
                    break
                phase_ml_scan(l)
                if done(f"ML{l}"):
                    break
                if "SKIPMLA" not in dump:
                    phase_mla_prep(l)
                    phase_attn(l, "mla")
                if done(f"MLA{l}"):
                    break
                phase_wout(l, src)
                if done(f"O{l}"):
                    break
                phase_ffn(l)
                if done(f"F{l}"):
                    break
        if "MODT" in dump:
            md = nc.dram_tensor("MODT", [128, DEPTH * 96 * 2], F32, kind="ExternalOutput").ap()
            k.dma(md[:, :], modT[:].rearrange("p l c j -> p (l c j)"), reads=[modT])
        k.barrier()
    k.close()
    return nc


def _consts():
    cst = np.zeros((128, NCS), np.float32)
    p = np.arange(128)
    cst[:, CS_ID:CS_ID + 128] = np.eye(128, dtype=np.float32)
    cst[:, CS_ONES:CS_ONES + 128] = 1.0
    cst[:, CS_BLK64:CS_BLK64 + 128] = (p[:, None] // 64 == p[None, :] // 64)
    rot = np.zeros((128, 128), np.float32)
    for dp in range(128):
        if dp % 32 < 16:
            rot[dp + 16, dp] = -1.0
        else:
            rot[dp - 16, dp] = 1.0
    cst[:, CS_ROT:CS_ROT + 128] = rot
    cst[:, CS_U:CS_U + 128] = (p[:, None] <= p[None, :])
    cst[:, CS_L:CS_L + 128] = (p[:, None] >= p[None, :])
    t = np.arange(SEQ)
    row = (t // GRID_W).astype(np.float32)
    col = (t % GRID_W).astype(np.float32)
    half = 16
    inv = (10000.0 ** (-np.arange(half, dtype=np.float32) / half)).astype(np.float32)
    rope = np.zeros((2, 128, SEQ), np.float32)
    for d in range(128):
        pos = row if (d % 64) // 32 == 0 else col
        ang = (pos * inv[d % 16]).astype(np.float32)
        rope[0, d] = np.cos(ang)
        rope[1, d] = np.sin(ang)
    return cst, rope


def _fm(v, nchunk):
    return np.ascontiguousarray(np.asarray(v, np.float32).reshape(nchunk, 128).T)


def _pack_pv(inp):
    pv = np.zeros((128, DEPTH * PVL), np.float32)
    for l in range(DEPTH):
        o = l * PVL
        pv[:, o + PV_N1G:o + PV_N1G + 16] = _fm(inp["norm1_g"][l], 16)
        pv[:, o + PV_N2G:o + PV_N2G + 16] = _fm(inp["norm2_g"][l], 16)
        for j in range(2):
            pv[:, o + PV_DAQG + j] = np.tile(inp["da_qk_g"][l, j], 2)
        pv[:, o + PV_DAOG] = inp["da_out_g"][l]
        cw = np.asarray(inp["ml_conv_w"][l])
        for ch in range(8):
            for j in range(3):
                pv[:, o + PV_MLCW + ch * 3 + j] = cw[j, ch * 128:(ch + 1) * 128]
        pv[:, o + PV_MLCB:o + PV_MLCB + 8] = _fm(inp["ml_conv_b"][l], 8)
        pv[:, o + PV_MLOG:o + PV_MLOG + 4] = _fm(inp["ml_out_g"][l], 4)
        pv[:, o + PV_MQG:o + PV_MQG + 4] = _fm(inp["mla_q_norm_g"][l], 4)
        pv[:, o + PV_MKVG:o + PV_MKVG + 2] = _fm(inp["mla_kv_norm_g"][l], 2)
        for j in range(2):
            pv[:, o + PV_MQKN + j] = inp["mla_qk_g"][l, j, :128]
            pv[:64, o + PV_MQKR + j] = inp["mla_qk_g"][l, j, 128:]
        pv[:, o + PV_DALAM:o + PV_DALAM + 256] = np.asarray(inp["da_lambda"][l]).reshape(1, 256)
        pv[:, o + PV_MLGB:o + PV_MLGB + 16] = np.asarray(inp["ml_gate_b"][l]).reshape(1, 16)
    return pv


def make_in_maps(inp, n_cores):
    f = lambda a: np.ascontiguousarray(np.asarray(a, np.float32))
    cst, rope = _consts()
    pv = _pack_pv(inp)
    wukv = np.asarray(inp["mla_w_ukv"], np.float32).reshape(DEPTH, 256, 8, 2, 128)
    wukv = np.ascontiguousarray(wukv.transpose(0, 1, 3, 2, 4).reshape(DEPTH, 256, 2048))
    shared = {
        "mod_w": f(inp["mod_w"]), "mod_bT": np.ascontiguousarray(f(inp["mod_b"]).reshape(DEPTH, 96, 128).transpose(0, 2, 1)),
        "w_in": f(inp["w_in"]), "w_out": f(inp["w_out"]), "w_gu": f(inp["ffn_w_gu"]), "w_down": f(inp["ffn_w_down"]),
        "w_uq": f(inp["mla_w_uq"]), "w_ukv": wukv, "pv": pv, "cst": cst, "rope": rope,
    }
    maps = []
    for c in range(n_cores):
        b = c % 4
        m = dict(shared)
        m["xin"] = np.ascontiguousarray(np.concatenate([inp["ctx"][b], inp["x"][b]], axis=0).astype(np.float32))
        ccv = np.stack([np.asarray(inp["c"][b], np.float32), np.asarray(inp["c_ctx"], np.float32)], axis=-1)
        m["cc"] = np.ascontiguousarray(ccv.reshape(16, 128, 2).transpose(1, 0, 2))
        maps.append(m)
    return maps


N_CORES = 4


def kernel(**inputs):
    nc = build()
    maps = make_in_maps(inputs, N_CORES)
    res = run_bass_kernel_spmd(nc, maps, core_ids=list(range(N_CORES)))
    out = np.stack([res.results[b]["y"] for b in range(4)], axis=0)
    return out.astype(np.float32)
```

```python
import contextlib
import math
import numpy as np
import concourse.bass as bass
import concourse.mybir as mybir
from concourse.bass_utils import run_bass_kernel_spmd

F32 = mybir.dt.float32
BF16 = mybir.dt.bfloat16
AF = mybir.ActivationFunctionType
ALU = mybir.AluOpType

SAME_ENGINE_SYNC = True
RELAX_SAME_ENGINE = False
SMALL_FREE = 256
N_DMA_SLOTS = 24

D = 2048
SEQ = 4096
CTX = 256
NTOK = SEQ + CTX
DEPTH = 2
GRID_W = 64
IN_W = 4432
FF = 5632
EPS = 1e-6
NT128 = NTOK // 128
BLOCKS = [(0, 256)] + [(256 + 512 * i, 512) for i in range(8)]
C_DAQ, C_DAK, C_DAV, C_MLQ, C_MLK, C_MLV, C_MLO, C_GAT, C_CQ, C_CKV, C_KPE = (
    0, 512, 1024, 1536, 2048, 2560, 3072, 3584, 3600, 4112, 4368)
R_DAQ, R_DAK, R_MLQ, R_MLK, R_MLO, R_CQ, R_CKV, R_KPE, N_RAW = 0, 4, 8, 12, 16, 20, 24, 26, 27
PV_N1G, PV_N2G, PV_DAQG, PV_DAOG, PV_MLCW, PV_MLCB, PV_MLOG, PV_MQG, PV_MKVG, PV_MQKN, PV_MQKR, PV_DALAM, PV_MLGB, PVL = (
    0, 16, 32, 34, 35, 59, 67, 71, 75, 77, 79, 81, 337, 353)
CS_ID, CS_ONES, CS_BLK64, CS_ROT, CS_U, CS_L, NCS = 0, 128, 256, 384, 512, 640, 768


class Eng:
    def __init__(self, name, h, sem, is_pe=False):
        self.name, self.h, self.sem = name, h, sem
        self.cnt = 0
        self.seen = {}
        self.is_pe = is_pe


class Tl:
    __slots__ = ("ap", "w", "r", "name", "small")

    def __init__(self, ap, name=""):
        self.ap, self.w, self.r, self.name = ap, None, {}, name
        n = 1
        for d in list(ap.shape)[1:]:
            n *= int(d)
        self.small = n < SMALL_FREE

    def __getitem__(self, idx):
        return self.ap[idx]


class K:
    def __init__(self, nc):
        self.nc = nc
        self.es = contextlib.ExitStack()
        sem = lambda n: self.es.enter_context(nc.semaphore(n))
        self.pe = Eng("pe", nc.tensor, sem("s_pe"), is_pe=True)
        self.act = Eng("act", nc.scalar, sem("s_act"))
        self.dve = Eng("dve", nc.vector, sem("s_dve"))
        self.pool = Eng("pool", nc.gpsimd, sem("s_pool"))
        self.sp = Eng("sp", nc.sync, sem("s_sp"))
        self.engs = [self.pe, self.act, self.dve, self.pool, self.sp]
        self.slots = [Eng(f"dq{i}", None, sem(f"s_dq{i}")) for i in range(N_DMA_SLOTS)]
        self.slot_i = 0
        self.n_ins = 0
        self.uid = 0

    def close(self):
        self.es.close()

    def sb(self, stack, name, shape, dt):
        self.uid += 1
        return Tl(stack.enter_context(self.nc.sbuf_tensor(f"sb{self.uid}_{name}", list(shape), dt)), name)

    def ps(self, stack, name, shape, dt=F32):
        self.uid += 1
        shape = list(shape)
        esz = 4 if dt == F32 else 2
        rest = esz
        for d in shape[2:]:
            rest *= d
        full1 = 2048 // rest
        assert full1 >= shape[1] and full1 * rest == 2048, (name, shape)
        t = stack.enter_context(self.nc.psum_tensor(f"ps{self.uid}_{name}", [shape[0], full1] + shape[2:], dt))
        if full1 == shape[1]:
            return Tl(t, name)
        return Tl(t[:, 0:shape[1]], name)

    def _deps(self, reads, writes, eng=None):
        deps = {}
        for t in reads:
            if t.w is not None and deps.get(t.w[0], 0) < t.w[1] and not (t.w[0] is eng and not t.small):
                deps[t.w[0]] = t.w[1]
        for t in writes:
            if t.w is not None and deps.get(t.w[0], 0) < t.w[1] and not (t.w[0] is eng and not t.small):
                deps[t.w[0]] = t.w[1]
            for e, c in t.r.items():
                if deps.get(e, 0) < c and not (e is eng and not t.small):
                    deps[e] = c
        return deps

    def _waits(self, eng, deps):
        for e, c in deps.items():
            if e is eng and (eng.is_pe or not SAME_ENGINE_SYNC):
                continue
            if eng.seen.get(e, 0) >= c:
                continue
            eng.h.wait_ge(e.sem, c)
            eng.seen[e] = c

    def _mark(self, who, val, reads, writes):
        for t in reads:
            if t.r.get(who, 0) < val:
                t.r[who] = val
        for t in writes:
            t.w = (who, val)
            t.r = {}

    def op(self, eng, fn, reads=(), writes=()):
        self._waits(eng, self._deps(reads, writes, eng if RELAX_SAME_ENGINE else None))
        ins = fn()
        eng.cnt += 1
        ins.then_inc(eng.sem, 1)
        self._mark(eng, eng.cnt, reads, writes)
        self.n_ins += 1
        return ins

    def dma(self, out, in_, reads=(), writes=(), q=None):
        q = q or self.sp
        slot = self.slots[self.slot_i]
        self.slot_i = (self.slot_i + 1) % len(self.slots)
        deps = self._deps(reads, writes)
        if slot.cnt > 0:
            deps[slot] = max(deps.get(slot, 0), slot.cnt)
        self._waits(q, deps)
        ins = q.h.dma_start(out=out, in_=in_)
        slot.cnt += 16
        ins.then_inc(slot.sem, 16)
        self._mark(slot, slot.cnt, reads, writes)
        self.n_ins += 1
        return ins

    def barrier(self):
        sp = self.sp
        for s in self.slots + self.engs:
            if s is sp:
                continue
            if s.cnt > 0 and sp.seen.get(s, 0) < s.cnt:
                sp.h.wait_ge(s.sem, s.cnt)
                sp.seen[s] = s.cnt
        sp.cnt += 1
        sp.h.nop().then_inc(sp.sem, 1)
        for e in self.engs:
            if e is sp:
                continue
            e.h.wait_ge(sp.sem, sp.cnt)
            for o in self.engs + self.slots:
                e.seen[o] = o.cnt
        for o in self.engs + self.slots:
            sp.seen[o] = o.cnt


def build(n_layers=DEPTH, stop_after=None, dump=()):
    nc = bass.Bass("TRN2", target_bir_lowering=False)
    k = K(nc)
    print("[build] sbuf bytes remaining at start:", nc.sbuf_bytes_remaining, flush=True)
    PE, ACT, DVE, POOL = k.pe, k.act, k.dve, k.pool

    def din(name, shape, dt=F32):
        return nc.dram_tensor(name, list(shape), dt, kind="ExternalInput").ap()

    def dscr(name, shape, dt):
        kind = "ExternalOutput" if name in dump else "Internal"
        return nc.dram_tensor(name, list(shape), dt, kind=kind).ap()

    xin = din("xin", [NTOK, D])
    cc_d = din("cc", [128, 16, 2])
    modw_d = din("mod_w", [DEPTH, D, 6 * D])
    modb_d = din("mod_bT", [DEPTH, 128, 96])
    win_d = din("w_in", [DEPTH, D, IN_W])
    wout_d = din("w_out", [DEPTH, D, D])
    wgu_d = din("w_gu", [DEPTH, D, 2 * FF])
    wdn_d = din("w_down", [DEPTH, FF, D])
    wuq_d = din("w_uq", [DEPTH, 512, 1536])
    wukv_d = din("w_ukv", [DEPTH, 256, 2048])
    pv_d = din("pv", [128, DEPTH * PVL])
    cst_d = din("cst", [128, NCS])
    rope_d = din("rope", [2, 128, SEQ])
    y_d = nc.dram_tensor("y", [SEQ, D], F32, kind="ExternalOutput").ap()

    WIN = [dscr(f"WIN{l}", [D, IN_W], BF16) for l in range(DEPTH)]
    WOUT = [dscr(f"WOUT{l}", [D, D], BF16) for l in range(DEPTH)]
    WGU = [dscr(f"WGU{l}", [D, 2 * FF], BF16) for l in range(DEPTH)]
    WDN = [dscr(f"WDN{l}", [FF, D], BF16) for l in range(DEPTH)]
    WUQ = [dscr(f"WUQ{l}", [512, 1536], BF16) for l in range(DEPTH)]
    WUKV = [dscr(f"WUKV{l}", [256, 2048], BF16) for l in range(DEPTH)]
    XR = dscr("XR", [NTOK, D], F32)
    XH = dscr("XH", [NTOK, D], F32)
    RAWT = dscr("RAWT", [N_RAW, 128, NTOK], F32)
    VDA = dscr("VDA", [NTOK, 512], BF16)
    VML = dscr("VML", [NTOK, 512], BF16)
    GATES = dscr("GATES", [NTOK, 16], F32)
    MIXT = dscr("MIXT", [16, 128, NTOK], BF16)

    def act(out, in_, func, R, W, **kw):
        return k.op(ACT, lambda: nc.scalar.activation(out=out, in_=in_, func=func, **kw), R, W)

    def tt(e, out, a, b, op, R, W):
        return k.op(e, lambda: e.h.tensor_tensor(out=out, in0=a, in1=b, op=op), R, W)

    def ts(e, out, a, s1, s2, op0, op1, R, W):
        if s2 is None:
            return k.op(e, lambda: e.h.tensor_scalar(out=out, in0=a, scalar1=s1, scalar2=None, op0=op0), R, W)
        return k.op(e, lambda: e.h.tensor_scalar(out=out, in0=a, scalar1=s1, scalar2=s2, op0=op0, op1=op1), R, W)

    def stt(e, out, a, s, b, op0, op1, R, W):
        return k.op(e, lambda: e.h.scalar_tensor_tensor(out=out, in0=a, scalar=s, in1=b, op0=op0, op1=op1), R, W)

    def cp(e, out, in_, R, W):
        if e is ACT:
            return k.op(e, lambda: nc.scalar.copy(out=out, in_=in_), R, W)
        return k.op(e, lambda: e.h.tensor_copy(out=out, in_=in_), R, W)

    def mm(out, lhsT, rhs, start, stop, R, W):
        return k.op(PE, lambda: nc.tensor.matmul(out, lhsT=lhsT, rhs=rhs, start=start, stop=stop), R, W)

    def tr(out, in_, ident, R, W):
        return k.op(PE, lambda: nc.tensor.transpose(out, in_, ident), R, W)

    with contextlib.ExitStack() as glob:
        cst = k.sb(glob, "cst", [128, NCS], F32)
        cstb = k.sb(glob, "cstb", [128, 256], BF16)
        pv = k.sb(glob, "pv", [128, DEPTH * PVL], F32)
        modT = k.sb(glob, "modT", [128, DEPTH, 96, 2], F32)
        AB = k.sb(glob, "AB", [128, DEPTH, 2, 16, 2], F32)
        k.dma(cst[:], cst_d[:, :], writes=[cst])
        k.dma(pv[:], pv_d[:, :], writes=[pv])
        cp(DVE, cstb[:], cst[:, 0:256], [cst], [cstb])
        identb = cstb[:, 0:128]
        onesb = cstb[:, 128:256]
        identf = cst[:, CS_ID:CS_ID + 128]
        onesf = cst[:, CS_ONES:CS_ONES + 128]

        def pvc(l, off, n=1):
            return pv[:, l * PVL + off:l * PVL + off + n]

        def phase_cast(layers):
            with contextlib.ExitStack() as st:
                CW = 2048
                fb = [k.sb(st, f"cw_f{i}", [128, CW], F32) for i in range(3)]
                bb = [k.sb(st, f"cw_b{i}", [128, CW], BF16) for i in range(3)]
                engs = [DVE, POOL, ACT]
                it = 0
                for l in layers:
                    for src, dst, Rr, Cc in ((win_d[l], WIN[l], D, IN_W), (wuq_d[l], WUQ[l], 512, 1536),
                                             (wukv_d[l], WUKV[l], 256, 2048), (wout_d[l], WOUT[l], D, D),
                                             (wgu_d[l], WGU[l], D, 2 * FF), (wdn_d[l], WDN[l], FF, D)):
                        for r0 in range(0, Rr, 128):
                            for c0 in range(0, Cc, CW):
                                cw = min(CW, Cc - c0)
                                f, b, e = fb[it % 3], bb[it % 3], engs[it % 3]
                                k.dma(f[:, 0:cw], src[r0:r0 + 128, c0:c0 + cw], writes=[f])
                                cp(e, b[:, 0:cw], f[:, 0:cw], [f], [b])
                                k.dma(dst[r0:r0 + 128, c0:c0 + cw], b[:, 0:cw], reads=[b], q=POOL)
                                it += 1
            k.barrier()

        def phase_mod(layers):
            with contextlib.ExitStack() as st:
                cc = k.sb(st, "cc", [128, 16, 2], F32)
                sc = k.sb(st, "sc", [128, 16, 2], F32)
                mb = k.sb(st, "mb", [128, 96], F32)
                wb = [k.sb(st, f"mw{i}", [128, 16, 512], F32) for i in range(2)]
                mps = k.ps(st, "mps", [128, 96, 2], F32)
                k.dma(cc[:], cc_d[:, :, :], writes=[cc])
                act(sc[:], cc[:], AF.Silu, [cc], [sc])
                it = 0
                for l in layers:
                    k.dma(mb[:], modb_d[l], writes=[mb])
                    for g in range(24):
                        w = wb[it % 2]
                        it += 1
                        k.dma(w[:], modw_d[l][:, g * 512:(g + 1) * 512].rearrange("(kc p) c -> p kc c", p=128), writes=[w])
                        for j in range(4):
                            c = g * 4 + j
                            for kc in range(16):
                                mm(mps[:, c, :], w[:, kc, j * 128:(j + 1) * 128], sc[:, kc, :], kc == 0, kc == 15, [w, sc], [mps])
                    for j in range(2):
                        tt(DVE, modT[:, l, :, j], mps[:, :, j], mb[:], ALU.add, [mps, mb], [modT])
                    for n, (goff, soff) in enumerate(((PV_N1G, 16), (PV_N2G, 64))):
                        for j in range(2):
                            stt(DVE, AB[:, l, n, :, j], modT[:, l, soff:soff + 16, j], 1.0, pvc(l, goff, 16),
                                ALU.add, ALU.mult, [modT, pv], [AB])
            k.barrier()

        def norm_load(st_bufs, src, t0, nt):
            xt, xs, junk, stats, tps, xmT = st_bufs
            for s in range(nt // 128):
                x = xt[s % len(xt)]
                stat = stats[s]
                k.dma(x[:], src[t0 + s * 128:t0 + (s + 1) * 128, :], writes=[x])
                act(junk[:], x[:], AF.Square, [x], [junk, stat], accum_out=stat[:, 0:1])
                act(stat[:, 1:2], stat[:, 0:1], AF.Sqrt, [stat], [stat], scale=1.0 / D, bias=EPS)
                k.op(DVE, lambda: nc.vector.reciprocal(out=stat[:, 2:3], in_=stat[:, 1:2]), [stat], [stat])
                e = DVE if s % 2 == 0 else POOL
                ts(e, xs[s][:], x[:], stat[:, 2:3], None, ALU.mult, None, [x, stat], [xs[s]])

        def norm_T(st_bufs, nt, l, which, is_ctx):
            xt, xs, junk, stats, tps, xmT = st_bufs
            ns = nt // 128
            j = 1 if is_ctx else 0
            shoff = 0 if which == 0 else 48
            for kp in range(8):
                tp = tps[kp % 2]
                for q in range(2):
                    kc = kp * 2 + q
                    for s in range(ns):
                        tr(tp[:, q, s * 128:(s + 1) * 128], xs[s][:, kc * 128:(kc + 1) * 128], identb, [xs[s], cstb], [tp])
                for q in range(2):
                    kc = kp * 2 + q
                    a_ap = AB[:, l, which, kc, j:j + 1]
                    b_ap = modT[:, l, shoff + kc, j:j + 1]
                    if q == 0:
                        act(xmT[:, kc, 0:nt], tp[:, q, 0:nt], AF.Identity, [tp, AB, modT], [xmT], scale=a_ap, bias=b_ap)
                    else:
                        ts(DVE, xmT[:, kc, 0:nt], tp[:, q, 0:nt], a_ap, b_ap, ALU.mult, ALU.add, [tp, AB, modT], [xmT])

        def alloc_norm_bufs(st, nx=2):
            xt = [k.sb(st, f"nm_x{i}", [128, D], F32) for i in range(nx)]
            xs = [k.sb(st, f"nm_xs{i}", [128, D], BF16) for i in range(4)]
            junk = k.sb(st, "nm_junk", [128, D], BF16)
            stat = [k.sb(st, f"nm_stat{i}", [128, 4], F32) for i in range(4)]
            tps = [k.ps(st, f"nm_tp{i}", [128, 2, 512], BF16) for i in range(2)]
            xmT = k.sb(st, "nm_xmT", [128, 16, 512], BF16)
            return xt, xs, junk, stat, tps, xmT

        GROUPS = [
            (C_DAQ, 512, "F", R_DAQ), (C_DAK, 512, "F", R_DAK), (C_DAV, 512, "V", VDA),
            (C_MLQ, 512, "F", R_MLQ), (C_MLK, 512, "F", R_MLK), (C_MLV, 512, "V", VML),
            (C_MLO, 512, "F", R_MLO), (C_GAT, 528, "G", R_CQ), (C_CKV, 320, "F", R_CKV)]

        def phase_A(l, src):
            with contextlib.ExitStack() as st:
                nb = alloc_norm_bufs(st)
                xmT = nb[5]
                wt = [k.sb(st, f"a_w{i}", [128, 16, 528], BF16) for i in range(2)]
                ps = [k.ps(st, f"a_ps{i}", [128, 512], F32) for i in range(4)]
                stg = [k.sb(st, f"a_sf{i}", [128, 512], F32) for i in range(4)]
                stb = [k.sb(st, f"a_sb{i}", [128, 512], BF16) for i in range(4)]
                stgt = [k.sb(st, f"a_sg{i}", [128, 16], F32) for i in range(2)]
                wi = 0
                oi = 0
                norm_load(nb, src, *BLOCKS[0])
                for bi, (t0, nt) in enumerate(BLOCKS):
                    is_ctx = t0 == 0
                    ns = nt // 128
                    norm_T(nb, nt, l, 0, is_ctx)
                    for gi, (c0, ncol, kind, arg) in enumerate(GROUPS):
                        if gi == 2 and bi + 1 < len(BLOCKS):
                            norm_load(nb, src, *BLOCKS[bi + 1])
                        w = wt[wi % 2]
                        wi += 1
                        k.dma(w[:, :, 0:ncol], WIN[l][:, c0:c0 + ncol].rearrange("(kc p) c -> p kc c", p=128), writes=[w])

                        def feat(cofs, rows, rawrow):
                            nonlocal oi
                            p, sf = ps[oi % 4], stg[oi % 4]
                            for kc in range(16):
                                mm(p[0:rows, 0:nt], w[:, kc, cofs:cofs + rows], xmT[:, kc, 0:nt], kc == 0, kc == 15, [w, xmT], [p])
                            cp(ACT if oi % 2 == 0 else DVE, sf[0:rows, 0:nt], p[0:rows, 0:nt], [p], [sf])
                            k.dma(RAWT[rawrow, 0:rows, t0:t0 + nt], sf[0:rows, 0:nt], reads=[sf], q=POOL)
                            oi += 1

                        if kind == "F":
                            nch = (ncol + 127) // 128
                            for ch in range(nch):
                                rows = min(128, ncol - ch * 128)
                                feat(ch * 128, rows, arg + ch)
                        elif kind == "G":
                            for s in range(ns):
                                p, sg = ps[oi % 4], stgt[s % 2]
                                for kc in range(16):
                                    mm(p[:, 0:16], xmT[:, kc, s * 128:(s + 1) * 128], w[:, kc, 0:16], kc == 0, kc == 15, [w, xmT], [p])
                                cp(DVE, sg[:], p[:, 0:16], [p], [sg])
                                k.dma(GATES[t0 + s * 128:t0 + (s + 1) * 128, :], sg[:], reads=[sg], q=POOL)
                                oi += 1
                            for ch in range(4):
                                feat(16 + ch * 128, 128, R_CQ + ch)
                        else:
                            for s in range(ns):
                                p, sbt = ps[oi % 4], stb[oi % 4]
                                for kc in range(16):
                                    mm(p[:, :], xmT[:, kc, s * 128:(s + 1) * 128], w[:, kc, 0:512], kc == 0, kc == 15, [w, xmT], [p])
                                cp(ACT if oi % 2 == 0 else DVE, sbt[:], p[:], [p], [sbt])
                                k.dma(arg[t0 + s * 128:t0 + (s + 1) * 128, :], sbt[:], reads=[sbt], q=POOL)
                                oi += 1
            k.barrier()


        def rstd_from(ss_ps, rows, nt, inv_n, r_sb, R):
            act(r_sb[0:rows, 0:nt], ss_ps[0:rows, 0:nt], AF.Sqrt, [ss_ps] + R, [r_sb], scale=inv_n, bias=EPS)
            k.op(DVE, lambda: nc.vector.reciprocal(out=r_sb[0:rows, 0:nt], in_=r_sb[0:rows, 0:nt]), [r_sb], [r_sb])

        def rope_apply(xn, rows, tl0, nt, ropet, rot_ps, t1, out_t):
            mm(rot_ps[0:rows, 0:nt], cst[0:rows, CS_ROT:CS_ROT + rows], xn[0:rows, 0:nt], True, True, [cst, xn], [rot_ps])
            tt(POOL, t1[0:rows, 0:nt], xn[0:rows, 0:nt], ropet[0:rows, 0, tl0:tl0 + nt], ALU.mult, [xn, ropet], [t1])
            tt(DVE, xn[0:rows, 0:nt], rot_ps[0:rows, 0:nt], ropet[0:rows, 1, tl0:tl0 + nt], ALU.mult, [rot_ps, ropet, xn], [xn])
            tt(POOL, out_t[0:rows, 0:nt], t1[0:rows, 0:nt], xn[0:rows, 0:nt], ALU.add, [t1, xn], [out_t])

        DAQT = dscr("DAQT", [4, 128, NTOK], BF16)
        DAKT = dscr("DAKT", [4, 128, NTOK], BF16)
        MQN = dscr("MQN", [8, 128, NTOK], BF16)
        MQR = dscr("MQR", [8, 64, NTOK], BF16)
        MKN = dscr("MKN", [8, 128, NTOK], BF16)
        MKR = dscr("MKR", [8, 64, NTOK], BF16)
        MV = dscr("MV", [NTOK, 1024], BF16)

        def phase_da_prep(l):
            with contextlib.ExitStack() as st:
                ropet = k.sb(st, "dp_rope", [128, 2, SEQ], F32)
                k.dma(ropet[:], rope_d.rearrange("a p t -> p a t"), writes=[ropet])
                rt = [k.sb(st, f"dp_rt{i}", [128, 512], F32) for i in range(2)]
                sq_ = [k.sb(st, f"dp_sq{i}", [128, 512], F32) for i in range(2)]
                rr_ = [k.sb(st, f"dp_r{i}", [128, 512], F32) for i in range(2)]
                xn_ = [k.sb(st, f"dp_xn{i}", [128, 512], F32) for i in range(2)]
                t1_ = [k.sb(st, f"dp_t1{i}", [128, 512], F32) for i in range(2)]
                ob = [k.sb(st, f"dp_o{i}", [128, 512], BF16) for i in range(2)]
                ss_ = [k.ps(st, f"dp_ss{i}", [128, 512], F32) for i in range(2)]
                rps_ = [k.ps(st, f"dp_rot{i}", [128, 512], F32) for i in range(2)]
                it = 0
                for which, (rrow, dst) in enumerate(((R_DAQ, DAQT), (R_DAK, DAKT))):
                    for h in range(4):
                        for (t0, nt) in BLOCKS:
                            r, o = rt[it % 2], ob[it % 2]
                            sq, rr, xn, t1, ss, rps = sq_[it % 2], rr_[it % 2], xn_[it % 2], t1_[it % 2], ss_[it % 2], rps_[it % 2]
                            it += 1
                            k.dma(r[:, 0:nt], RAWT[rrow + h, :, t0:t0 + nt], writes=[r])
                            act(sq[:, 0:nt], r[:, 0:nt], AF.Square, [r], [sq])
                            mm(ss[:, 0:nt], cst[:, CS_BLK64:CS_BLK64 + 128], sq[:, 0:nt], True, True, [cst, sq], [ss])
                            rstd_from(ss, 128, nt, 1.0 / 64, rr, [])
                            stt(DVE, xn[:, 0:nt], r[:, 0:nt], pvc(l, PV_DAQG + which), rr[:, 0:nt], ALU.mult, ALU.mult, [r, rr, pv], [xn])
                            if t0 == 0:
                                cp(POOL, o[:, 0:nt], xn[:, 0:nt], [xn], [o])
                            else:
                                rope_apply(xn, 128, t0 - CTX, nt, ropet, rps, t1, o)
                            k.dma(dst[h, :, t0:t0 + nt], o[:, 0:nt], reads=[o], q=POOL)
            k.barrier()

        def phase_mla_prep(l):
            with contextlib.ExitStack() as st:
                ropet = k.sb(st, "mp_rope", [64, 2, SEQ], F32)
                k.dma(ropet[:], rope_d[:, 0:64, :].rearrange("a p t -> p a t"), writes=[ropet])
                wuq = k.sb(st, "mp_wuq", [128, 4, 1536], BF16)
                wukv = k.sb(st, "mp_wukv", [128, 2, 2048], BF16)
                k.dma(wuq[:], WUQ[l].rearrange("(c p) n -> p c n", p=128), writes=[wuq])
                k.dma(wukv[:], WUKV[l].rearrange("(c p) n -> p c n", p=128), writes=[wukv])
                rq = k.sb(st, "mp_rq", [128, 4, 512], F32)
                rkv = k.sb(st, "mp_rkv", [128, 2, 512], F32)
                rkp = k.sb(st, "mp_rkp", [64, 512], F32)
                sq4 = k.sb(st, "mp_sq4", [128, 4, 512], F32)
                rr = k.sb(st, "mp_rr", [128, 512], F32)
                cqn = k.sb(st, "mp_cqn", [128, 4, 512], BF16)
                ckvn = k.sb(st, "mp_ckvn", [128, 2, 512], BF16)
                sqN = [k.sb(st, f"mp_sqN{i}", [128, 512], F32) for i in range(2)]
                sqR = [k.sb(st, f"mp_sqR{i}", [64, 512], F32) for i in range(2)]
                sqRk = k.sb(st, "mp_sqRk", [64, 512], F32)
                kpr = k.sb(st, "mp_kpr", [64, 512], F32)
                kprb = k.sb(st, "mp_kprb", [64, 512], BF16)
                rh = [k.sb(st, f"mp_rh{i}", [128, 512], F32) for i in range(2)]
                qr0 = [k.sb(st, f"mp_qr0{i}", [64, 512], F32) for i in range(2)]
                t1 = k.sb(st, "mp_t1", [64, 512], F32)
                on = [k.sb(st, f"mp_on{i}", [128, 512], BF16) for i in range(3)]
                orr = [k.sb(st, f"mp_or{i}", [64, 512], BF16) for i in range(3)]
                vb = [k.sb(st, f"mp_vb{i}", [128, 512], BF16) for i in range(2)]
                ss = [k.ps(st, f"mp_ss{i}", [128, 512], F32) for i in range(2)]
                pn = [k.ps(st, f"mp_pn{i}", [128, 512], F32) for i in range(2)]
                pr = [k.ps(st, f"mp_pr{i}", [64, 512], F32) for i in range(2)]
                rps = k.ps(st, "mp_rot", [64, 512], F32)
                gq = lambda c: pvc(l, PV_MQG + c)
                gkv = lambda c: pvc(l, PV_MKVG + c)
                it = 0
                vi = 0
                for (t0, nt) in BLOCKS:
                    is_ctx = t0 == 0
                    ns = nt // 128
                    k.dma(rq[:, :, 0:nt], RAWT[R_CQ:R_CQ + 4, :, t0:t0 + nt].rearrange("c p t -> p c t"), writes=[rq])
                    k.dma(rkv[:, :, 0:nt], RAWT[R_CKV:R_CKV + 2, :, t0:t0 + nt].rearrange("c p t -> p c t"), writes=[rkv])
                    k.dma(rkp[:, 0:nt], RAWT[R_KPE, 0:64, t0:t0 + nt], writes=[rkp])
                    act(sq4[:, :, 0:nt], rq[:, :, 0:nt], AF.Square, [rq], [sq4])
                    for c in range(4):
                        mm(ss[0][:, 0:nt], onesf, sq4[:, c, 0:nt], c == 0, c == 3, [cst, sq4], [ss[0]])
                    rstd_from(ss[0], 128, nt, 1.0 / 512, rr, [])
                    for c in range(4):
                        stt(DVE, cqn[:, c, 0:nt], rq[:, c, 0:nt], gq(c), rr[:, 0:nt], ALU.mult, ALU.mult, [rq, rr, pv], [cqn])
                    act(sq4[:, 0:2, 0:nt], rkv[:, :, 0:nt], AF.Square, [rkv], [sq4])
                    for c in range(2):
                        mm(ss[1][:, 0:nt], onesf, sq4[:, c, 0:nt], c == 0, c == 1, [cst, sq4], [ss[1]])
                    rstd_from(ss[1], 128, nt, 1.0 / 256, rr, [])
                    for c in range(2):
                        stt(DVE, ckvn[:, c, 0:nt], rkv[:, c, 0:nt], gkv(c), rr[:, 0:nt], ALU.mult, ALU.mult, [rkv, rr, pv], [ckvn])
                    act(sqRk[:, 0:nt], rkp[:, 0:nt], AF.Square, [rkp], [sqRk])
                    ts(DVE, kpr[:, 0:nt], rkp[:, 0:nt], pv[0:64, l * PVL + PV_MQKR + 1:l * PVL + PV_MQKR + 2], None, ALU.mult, None, [rkp, pv], [kpr])
                    if not is_ctx:
                        rope_apply(kpr, 64, t0 - CTX, nt, ropet, rps, t1, kprb)
                        cp(POOL, kpr[:, 0:nt], kprb[:, 0:nt], [kprb], [kpr])
                    for s in range(ns):
                        for g2 in range(2):
                            p, v = pn[vi % 2], vb[vi % 2]
                            vi += 1
                            for c in range(2):
                                mm(p[:, :], ckvn[:, c, s * 128:(s + 1) * 128], wukv[:, c, 1024 + g2 * 512:1024 + (g2 + 1) * 512], c == 0, c == 1, [ckvn, wukv], [p])
                            cp(ACT, v[:], p[:], [p], [v])
                            k.dma(MV[t0 + s * 128:t0 + (s + 1) * 128, g2 * 512:(g2 + 1) * 512], v[:], reads=[v], q=POOL)
                    for h in range(8):
                        for side in range(2):
                            b = it % 2
                            it += 1
                            PN, PR, SS, SQN, SQR, RH = pn[b], pr[b], ss[b], sqN[b], sqR[b], rh[b]
                            o_n, o_r = on[it % 3], orr[it % 3]
                            if side == 0:
                                for c in range(4):
                                    mm(PN[:, 0:nt], wuq[:, c, h * 192:h * 192 + 128], cqn[:, c, 0:nt], c == 0, c == 3, [wuq, cqn], [PN])
                                for c in range(4):
                                    mm(PR[:, 0:nt], wuq[:, c, h * 192 + 128:h * 192 + 192], cqn[:, c, 0:nt], c == 0, c == 3, [wuq, cqn], [PR])
                                act(SQN[:, 0:nt], PN[:, 0:nt], AF.Square, [PN], [SQN])
                                act(SQR[:, 0:nt], PR[:, 0:nt], AF.Square, [PR], [SQR])
                                sqr_t = SQR
                            else:
                                for c in range(2):
                                    mm(PN[:, 0:nt], wukv[:, c, h * 128:(h + 1) * 128], ckvn[:, c, 0:nt], c == 0, c == 1, [wukv, ckvn], [PN])
                                act(SQN[:, 0:nt], PN[:, 0:nt], AF.Square, [PN], [SQN])
                                sqr_t = sqRk
                            mm(SS[:, 0:nt], onesf, SQN[:, 0:nt], True, False, [cst, SQN], [SS])
                            mm(SS[:, 0:nt], cst[0:64, CS_ONES:CS_ONES + 128], sqr_t[:, 0:nt], False, True, [cst, sqr_t], [SS])
                            rstd_from(SS, 128, nt, 1.0 / 192, RH, [])
                            stt(DVE, o_n[:, 0:nt], PN[:, 0:nt], pvc(l, PV_MQKN + side), RH[:, 0:nt], ALU.mult, ALU.mult, [PN, RH, pv], [o_n])
                            if side == 0:
                                Q0 = qr0[b]
                                stt(DVE, Q0[:, 0:nt], PR[:, 0:nt], pv[0:64, l * PVL + PV_MQKR:l * PVL + PV_MQKR + 1], RH[0:64, 0:nt], ALU.mult, ALU.mult, [PR, RH, pv], [Q0])
                                if is_ctx:
                                    cp(POOL, o_r[:, 0:nt], Q0[:, 0:nt], [Q0], [o_r])
                                else:
                                    rope_apply(Q0, 64, t0 - CTX, nt, ropet, rps, t1, o_r)
                                k.dma(MQN[h, :, t0:t0 + nt], o_n[:, 0:nt], reads=[o_n], q=POOL)
                                k.dma(MQR[h, :, t0:t0 + nt], o_r[:, 0:nt], reads=[o_r], q=POOL)
                            else:
                                tt(POOL, o_r[:, 0:nt], kpr[:, 0:nt], RH[0:64, 0:nt], ALU.mult, [kpr, RH], [o_r])
                                k.dma(MKN[h, :, t0:t0 + nt], o_n[:, 0:nt], reads=[o_n], q=POOL)
                                k.dma(MKR[h, :, t0:t0 + nt], o_r[:, 0:nt], reads=[o_r], q=POOL)
            k.barrier()

        def phase_attn(l, kind):
            need_ctx = l < DEPTH - 1
            da = kind == "da"
            ncomp = 2 if da else 1
            nh = 4 if da else 8
            scale = (64 if da else 192) ** -0.5
            lam_init = 0.8 - 0.6 * math.exp(-0.3 * l)
            with contextlib.ExitStack() as st:
                K0 = k.sb(st, "at_k0", [128, NTOK], BF16)
                K1 = None if da else k.sb(st, "at_k1", [64, NTOK], BF16)
                Vg = k.sb(st, "at_v", [128, NT128, 512], BF16)
                Q0 = [k.sb(st, f"at_q0{i}", [128, 512], BF16) for i in range(2)]
                Q1 = None if da else [k.sb(st, f"at_q1{i}", [64, 512], BF16) for i in range(2)]
                SD = 3 if da else 4
                S = [[k.ps(st, f"at_s{c}{b}", [128, 512], F32) for b in range(SD)] for c in range(ncomp)]
                O2 = [k.ps(st, f"at_o{c}", [128, 512], F32) for c in range(2)]
                Dps = None if da else k.ps(st, "at_d", [128, 512], F32)
                E = [[k.sb(st, f"at_e{c}{b}", [128, 512], BF16) for b in range(SD)] for c in range(ncomp)]
                Es = [[k.sb(st, f"at_es{c}{p}", [128, 512], F32) for p in range(2)] for c in range(ncomp)]
                rec = [k.sb(st, f"at_rec{c}", [128, 512], F32) for c in range(2)]
                oa = [k.sb(st, f"at_oa{c}", [128, 512], F32) for c in range(2)]
                sqt = k.sb(st, "at_sq", [128, 512], F32)
                ob = [k.sb(st, f"at_ob{i}", [128, 512], BF16) for i in range(2)]
                sm = k.sb(st, "at_sm", [128, 8], F32)
                junk = k.sb(st, "at_junk", [128, 64], F32)
                if da:
                    lamv = pvc(l, PV_DALAM, 256)
                    for i in range(2):
                        tt(DVE, junk[:], pv[:, l * PVL + PV_DALAM + 128 * i:l * PVL + PV_DALAM + 128 * i + 64],
                           pv[:, l * PVL + PV_DALAM + 128 * i + 64:l * PVL + PV_DALAM + 128 * i + 128], ALU.mult, [pv], [junk])
                        act(junk[:], junk[:], AF.Identity, [junk], [junk, sm], accum_out=sm[:, i:i + 1])
                        act(sm[:, 2 + i:3 + i], sm[:, i:i + 1], AF.Exp, [sm], [sm])
                    tt(DVE, sm[:, 4:5], sm[:, 3:4], sm[:, 2:3], ALU.subtract, [sm], [sm])
                    ts(DVE, sm[:, 4:5], sm[:, 4:5], -lam_init, None, ALU.add, None, [sm], [sm])
                    ts(DVE, sm[:, 5:6], pvc(l, PV_DAOG), 1.0 - lam_init, None, ALU.mult, None, [pv], [sm])
                qblocks = BLOCKS if need_ctx else BLOCKS[1:]
                qi = 0
                for h in range(nh):
                    if h % 4 == 0:
                        vsrc = VDA if da else MV[:, (h // 4) * 512:(h // 4 + 1) * 512]
                        k.dma(Vg[:], vsrc.rearrange("(kt p) v -> p kt v", p=128), writes=[Vg])
                    if da:
                        k.dma(K0[:], DAKT[h], writes=[K0])
                    else:
                        k.dma(K0[:], MKN[h], writes=[K0])
                        k.dma(K1[:], MKR[h], writes=[K1])
                    vcol = (h % 4) * 128
                    for (t0, nt) in qblocks:
                        is_ctx = t0 == 0
                        nkt = 2 if is_ctx else NT128
                        q0 = Q0[qi % 2]
                        q1 = None if da else Q1[qi % 2]
                        qi += 1
                        if da:
                            k.dma(q0[:, 0:nt], DAQT[h, :, t0:t0 + nt], writes=[q0])
                        else:
                            k.dma(q0[:, 0:nt], MQN[h, :, t0:t0 + nt], writes=[q0])
                            k.dma(q1[:, 0:nt], MQR[h, :, t0:t0 + nt], writes=[q1])

                        def scores(kt):
                            b = kt % SD
                            ks = slice(kt * 128, (kt + 1) * 128)
                            if da:
                                for c in range(2):
                                    mm(S[c][b][:, 0:nt], K0[64 * c:64 * c + 64, ks], q0[64 * c:64 * c + 64, 0:nt], True, True, [K0, q0], [S[c][b]])
                            else:
                                mm(S[0][b][:, 0:nt], K0[:, ks], q0[:, 0:nt], True, False, [K0, q0], [S[0][b]])
                                mm(S[0][b][:, 0:nt], K1[:, ks], q1[:, 0:nt], False, True, [K1, q1], [S[0][b]])

                        O = O2 if da else [O2[qi % 2]]
                        NFILL = 3 if da else 1
                        LA = SD - 2
                        for j0 in range(min(LA, nkt)):
                            scores(j0)

                        def pv_step(kt):
                            b = kt % SD
                            for c in range(ncomp):
                                mm(O[c][:, 0:nt], Vg[:, kt, vcol:vcol + 128], E[c][b][:, 0:nt], kt == 0, kt == nkt - 1, [Vg, E[c][b]], [O[c]])
                                p_ = kt % 2
                                e_ = DVE if (c + kt) % 2 == 0 else POOL
                                if kt < 2:
                                    cp(e_, Es[c][p_][:, 0:nt], E[c][b][:, 0:nt], [E[c][b]], [Es[c][p_]])
                                else:
                                    tt(e_, Es[c][p_][:, 0:nt], Es[c][p_][:, 0:nt], E[c][b][:, 0:nt], ALU.add, [Es[c][p_], E[c][b]], [Es[c][p_]])

                        for kt in range(nkt):
                            b = kt % SD
                            if kt + LA < nkt:
                                scores(kt + LA)
                            for c in range(ncomp):
                                act(E[c][b][:, 0:nt], S[c][b][:, 0:nt], AF.Exp, [S[c][b]], [E[c][b]], scale=scale)
                            if kt >= 1:
                                pv_step(kt - 1)
                                fb = (kt + LA + 1) % SD
                                for f_ in range(NFILL):
                                    c_ = f_ % ncomp
                                    mm(S[c_][fb][:, 0:512], onesb, K0[:, 0:512], True, True, [cstb, K0], [S[c_][fb]])
                        pv_step(nkt - 1)
                        Dn = [S[c][0] for c in range(ncomp)] if da else [Dps]
                        for c in range(ncomp):
                            mm(Dn[c][:, 0:nt], onesf, Es[c][0][:, 0:nt], True, False, [cst, Es[c][0]], [Dn[c]])
                            mm(Dn[c][:, 0:nt], onesf, Es[c][1][:, 0:nt], False, True, [cst, Es[c][1]], [Dn[c]])
                        o = ob[qi % 2]
                        for c in range(ncomp):
                            k.op(DVE, lambda c=c: nc.vector.reciprocal(out=rec[c][:, 0:nt], in_=Dn[c][:, 0:nt]), [Dn[c]], [rec[c]])
                        if da:
                            for c in range(2):
                                tt(DVE, oa[c][:, 0:nt], O[c][:, 0:nt], rec[c][:, 0:nt], ALU.mult, [O[c], rec[c]], [oa[c]])
                            stt(DVE, oa[0][:, 0:nt], oa[1][:, 0:nt], sm[:, 4:5], oa[0][:, 0:nt], ALU.mult, ALU.add, [oa[0], oa[1], sm], [oa[0]])
                            act(sqt[:, 0:nt], oa[0][:, 0:nt], AF.Square, [oa[0]], [sqt])
                            ms = S[0][1]
                            mm(ms[:, 0:nt], onesf, sqt[:, 0:nt], True, True, [cst, sqt], [ms])
                            act(sqt[:, 0:nt], ms[:, 0:nt], AF.Ln, [ms], [sqt], scale=1.0 / 128, bias=EPS)
                            act(sqt[:, 0:nt], sqt[:, 0:nt], AF.Exp, [sqt], [sqt], scale=-0.5)
                            stt(DVE, o[:, 0:nt], oa[0][:, 0:nt], sm[:, 5:6], sqt[:, 0:nt], ALU.mult, ALU.mult, [oa[0], sqt, sm], [o])
                            k.dma(MIXT[h, :, t0:t0 + nt], o[:, 0:nt], reads=[o], q=POOL)
                        else:
                            tt(DVE, o[:, 0:nt], O[0][:, 0:nt], rec[0][:, 0:nt], ALU.mult, [O[0], rec[0]], [o])
                            k.dma(MIXT[8 + h, :, t0:t0 + nt], o[:, 0:nt], reads=[o], q=POOL)
            k.barrier()


        LQT = dscr("LQT", [4, 128, NTOK], BF16)
        LKT = dscr("LKT", [4, 128, NTOK], BF16)
        LK = dscr("LK", [NTOK, 512], BF16)

        def phase_ml_prep(l):
            with contextlib.ExitStack() as st:
                xr = [k.sb(st, f"lp_x{i}", [128, NTOK], F32) for i in range(2)]
                y = k.sb(st, "lp_y", [128, NTOK], F32)
                ob = [k.sb(st, f"lp_o{i}", [128, NTOK], BF16) for i in range(2)]
                tp = [k.ps(st, f"lp_tp{i}", [128, 4, 128], BF16) for i in range(2)]
                tb = [k.sb(st, f"lp_tb{i}", [128, 4, 128], BF16) for i in range(2)]
                ti = 0
                for ch in range(8):
                    x, o = xr[ch % 2], ob[ch % 2]
                    rrow = (R_MLQ + ch) if ch < 4 else (R_MLK + ch - 4)
                    k.dma(x[:, 0:2304], RAWT[rrow, :, 0:2304], writes=[x])
                    k.dma(x[:, 2304:NTOK], RAWT[rrow, :, 2304:NTOK], writes=[x])
                    w = lambda j: pvc(l, PV_MLCW + ch * 3 + j)
                    for (a, b) in ((0, CTX), (CTX, NTOK)):
                        for c0 in range(a, b, 512):
                            c1 = min(c0 + 512, b)
                            ts(DVE, y[:, c0:c1], x[:, c0:c1], w(1), pvc(l, PV_MLCB + ch), ALU.mult, ALU.add, [x, pv], [y])
                            lo = max(c0, a + 1)
                            stt(DVE, y[:, lo:c1], x[:, lo - 1:c1 - 1], w(0), y[:, lo:c1], ALU.mult, ALU.add, [x, y, pv], [y])
                            hi = min(c1, b - 1)
                            stt(DVE, y[:, c0:hi], x[:, c0 + 1:hi + 1], w(2), y[:, c0:hi], ALU.mult, ALU.add, [x, y, pv], [y])
                            act(y[:, c0:c1], y[:, c0:c1], AF.Silu, [y], [y])
                            if ch < 4:
                                cp(POOL, o[:, c0:c1], y[:, c0:c1], [y], [o])
                            else:
                                ts(POOL, o[:, c0:c1], y[:, c0:c1], 128.0 ** -0.5, None, ALU.mult, None, [y], [o])
                    if ch < 4:
                        k.dma(LQT[ch], o[:], reads=[o], q=POOL)
                    else:
                        k.dma(LKT[ch - 4], o[:], reads=[o], q=POOL)
                        h = ch - 4
                        for g in range(0, NT128, 4):
                            n = min(4, NT128 - g)
                            p, t = tp[ti % 2], tb[ti % 2]
                            ti += 1
                            for j in range(n):
                                tr(p[:, j, :], o[:, (g + j) * 128:(g + j + 1) * 128], identb, [o, cstb], [p])
                            cp(ACT if ti % 2 == 0 else DVE, t[:, 0:n, :], p[:, 0:n, :], [p], [t])
                            k.dma(LK[g * 128:(g + n) * 128, h * 128:(h + 1) * 128].rearrange("(j p) d -> p j d", p=128), t[:, 0:n, :], reads=[t], q=POOL)
            k.barrier()

        def phase_ml_scan(l):
            with contextlib.ExitStack() as st:
                Graw = k.sb(st, "ls_graw", [128, NT128, 16], F32)
                G = k.sb(st, "ls_g", [128, NT128, 16], F32)
                LF = k.sb(st, "ls_lf", [128, NT128, 16], F32)
                k.dma(Graw[:], GATES.rearrange("(kt p) g -> p kt g", p=128), writes=[Graw])
                for g in range(16):
                    act(G[:, :, g], Graw[:, :, g], AF.Identity, [Graw, pv], [G], bias=pvc(l, PV_MLGB + g))
                act(LF[:], G[:], AF.Exp, [G], [LF], scale=-1.0)
                act(LF[:], LF[:], AF.Ln, [LF], [LF], bias=1.0)
                ts(DVE, LF[:], LF[:], -1.0, None, ALU.mult, None, [LF], [LF])
                QT = [k.sb(st, f"ls_qt{i}", [128, NTOK], BF16) for i in range(2)]
                KT = [k.sb(st, f"ls_kt{i}", [128, NTOK], BF16) for i in range(2)]
                Kk = [k.sb(st, f"ls_kk{i}", [128, NT128, 128], BF16) for i in range(2)]
                Vv = [k.sb(st, f"ls_vv{i}", [128, NT128, 128], BF16) for i in range(2)]
                Hd = [[k.sb(st, f"ls_h{i}{d}", [128, NTOK], F32) for d in range(2)] for i in range(2)]
                sq = k.sb(st, "ls_sq", [128, 512], F32)
                rs = k.sb(st, "ls_rs", [128, 512], F32)
                og = [k.sb(st, f"ls_og{i}", [128, 512], F32) for i in range(2)]
                ob = [k.sb(st, f"ls_ob{i}", [128, 512], BF16) for i in range(2)]
                pM = k.ps(st, "ls_pM", [128, 512], F32)

                class Ch:
                    pass
                chs = []
                for c in range(4):
                    o = Ch()
                    o.hi, o.d = c // 2, c % 2
                    f32t = lambda n, w=128: k.sb(st, f"ls_{n}{c}", [128, w], F32)
                    b16t = lambda n: k.sb(st, f"ls_{n}{c}", [128, 128], BF16)
                    o.LFb, o.ET, o.EB, o.Em, o.dn, o.Cs = f32t("lfb"), f32t("et"), f32t("eb"), f32t("em"), f32t("dn"), f32t("cs")
                    o.bias, o.Ns = f32t("bias", 2), f32t("ns", 2)
                    o.PT, o.Qd, o.Kw, o.Cb, o.Nb = b16t("pt"), b16t("qd"), b16t("kw"), b16t("cb"), b16t("nb")
                    o.bk = k.ps(st, f"ls_bk{c}", [128, 512], F32)
                    o.pB = o.pN = o.bk[:, 0:128]
                    o.pS = o.pD = o.bk[:, 128:256]
                    o.pC, o.pn, o.pb = o.bk[:, 256:384], o.bk[:, 384:386], o.bk[:, 386:388]
                    o.tri = cst[:, CS_U:CS_U + 128] if o.d == 0 else cst[:, CS_L:CS_L + 128]
                    o.ecol = 127 if o.d == 0 else 0
                    o.order = list(range(NT128)) if o.d == 0 else [1, 0] + list(range(NT128 - 1, 1, -1))
                    chs.append(o)
                oi = 0
                for hp in range(2):
                    for i in range(2):
                        h = 2 * hp + i
                        k.dma(QT[i][:], LQT[h], writes=[QT[i]])
                        k.dma(KT[i][:], LKT[h], writes=[KT[i]])
                        k.dma(Kk[i][:], LK[:, h * 128:(h + 1) * 128].rearrange("(kt p) d -> p kt d", p=128), writes=[Kk[i]])
                        k.dma(Vv[i][:], VML[:, h * 128:(h + 1) * 128].rearrange("(kt p) d -> p kt d", p=128), writes=[Vv[i]])
                    for step in range(NT128):
                        first, last = step == 0, step == NT128 - 1
                        for o in chs:
                            o.h = 2 * hp + o.hi
                            o.kt = o.order[step]
                            o.tk = slice(o.kt * 128, (o.kt + 1) * 128)
                            o.gi, o.gf = (o.h, 4 + o.h) if o.d == 0 else (8 + o.h, 12 + o.h)
                        for o in chs:
                            act(o.LFb[:], onesf, AF.Copy, [cst, LF], [o.LFb], scale=LF[:, o.kt, o.gf:o.gf + 1])
                        for o in chs:
                            mm(o.pB, o.LFb[:], o.tri, True, True, [o.LFb, cst], [o.bk])
                            mm(o.pb, o.tri, LF[:, o.kt, o.gf - 1:o.gf + 1], True, True, [cst, LF], [o.bk])
                            mm(o.pS, KT[o.hi][:, o.tk], QT[o.hi][:, o.tk], True, True, [KT[o.hi], QT[o.hi]], [o.bk])
                        for o in chs:
                            tt(DVE, o.bias[:, 0:1], G[:, o.kt, o.gi:o.gi + 1], o.bk[:, 387:388], ALU.subtract, [G, o.bk], [o.bias])
                            act(o.ET[:], o.pB, AF.Exp, [o.bk, o.bias], [o.ET], bias=o.bias[:, 0:1])
                            act(o.EB[:], o.pB, AF.Exp, [o.bk], [o.EB])
                        for o in chs:
                            tt(POOL, o.Em[:], o.ET[:], o.tri, ALU.mult, [o.ET, cst], [o.Em])
                            tt(DVE, o.PT[:], o.Em[:], o.pS, ALU.mult, [o.Em, o.bk], [o.PT])
                            if not first:
                                tt(POOL, o.Qd[:], QT[o.hi][:, o.tk], o.EB[:], ALU.mult, [QT[o.hi], o.EB], [o.Qd])
                        for o in chs:
                            mm(o.pN, Vv[o.hi][:, o.kt, :], o.PT[:], True, first, [Vv[o.hi], o.PT], [o.bk])
                            if not first:
                                mm(o.pN, o.Cb[:], o.Qd[:], False, True, [o.Cb, o.Qd], [o.bk])
                            mm(o.pD, onesb, o.PT[:], True, first, [cstb, o.PT], [o.bk])
                            if not first:
                                mm(o.pD, o.Nb[:], o.Qd[:], False, True, [o.Nb, o.Qd], [o.bk])
                        for o in chs:
                            Hh = Hd[o.hi][o.d]
                            ts(DVE, o.dn[:], o.pD, -1.0, 1.0, ALU.mult, ALU.max, [o.bk], [o.dn])
                            stt(DVE, o.dn[:], o.pD, 1.0, o.dn[:], ALU.max, ALU.max, [o.bk, o.dn], [o.dn])
                            k.op(DVE, lambda o=o: nc.vector.reciprocal(out=o.dn[:], in_=o.dn[:]), [o.dn], [o.dn])
                            tt(DVE, Hh[:, o.tk], o.pN, o.dn[:], ALU.mult, [o.bk, o.dn], [Hh])
                        if last:
                            continue
                        for o in chs:
                            act(o.Kw[:], Kk[o.hi][:, o.kt, :], AF.Copy, [Kk[o.hi], o.ET], [o.Kw], scale=o.ET[:, o.ecol:o.ecol + 1])
                        for o in chs:
                            mm(o.pC, o.Kw[:], Vv[o.hi][:, o.kt, :], True, True, [o.Kw, Vv[o.hi]], [o.bk])
                            mm(o.pn, o.Kw[:], cstb[:, 128:130], True, True, [o.Kw, cstb], [o.bk])
                        for o in chs:
                            if first:
                                cp(DVE, o.Cs[:], o.pC, [o.bk], [o.Cs])
                                cp(DVE, o.Ns[:], o.pn, [o.bk], [o.Ns])
                            else:
                                dec = o.EB[:, o.ecol:o.ecol + 1]
                                stt(DVE, o.Cs[:], o.Cs[:], dec, o.pC, ALU.mult, ALU.add, [o.Cs, o.EB, o.bk], [o.Cs])
                                stt(DVE, o.Ns[:], o.Ns[:], dec, o.pn, ALU.mult, ALU.add, [o.Ns, o.EB, o.bk], [o.Ns])
                            cp(ACT, o.Cb[:], o.Cs[:], [o.Cs], [o.Cb])
                            act(o.Nb[:], onesf, AF.Copy, [cst, o.Ns], [o.Nb], scale=o.Ns[:, 0:1])
                    for i in range(2):
                        h = 2 * hp + i
                        for (t0, nt) in BLOCKS:
                            o_, g_ = ob[oi % 2], og[oi % 2]
                            oi += 1
                            k.dma(g_[:, 0:nt], RAWT[R_MLO + h, :, t0:t0 + nt], writes=[g_])
                            act(g_[:, 0:nt], g_[:, 0:nt], AF.Sigmoid, [g_], [g_])
                            tt(POOL, rs[:, 0:nt], Hd[i][0][:, t0:t0 + nt], Hd[i][1][:, t0:t0 + nt], ALU.add, [Hd[i][0], Hd[i][1]], [rs])
                            act(sq[:, 0:nt], rs[:, 0:nt], AF.Square, [rs], [sq])
                            mm(pM[:, 0:nt], onesf, sq[:, 0:nt], True, True, [cst, sq], [pM])
                            stt(DVE, sq[:, 0:nt], rs[:, 0:nt], pvc(l, PV_MLOG + h), g_[:, 0:nt], ALU.mult, ALU.mult, [rs, g_, pv], [sq])
                            rstd_from(pM, 128, nt, 1.0 / 128, rs, [sq])
                            tt(POOL, o_[:, 0:nt], sq[:, 0:nt], rs[:, 0:nt], ALU.mult, [sq, rs], [o_])
                            k.dma(MIXT[4 + h, :, t0:t0 + nt], o_[:, 0:nt], reads=[o_], q=POOL)
            k.barrier()

        def build_gate_bcast(Gb, l, choff, j, ps_t, dg):
            for c in range(16):
                ts(POOL, dg[:], identf, modT[:, l, choff + c, j:j + 1], None, ALU.mult, None, [cst, modT], [dg])
                mm(ps_t[:, 0:128], onesf, dg[:], True, True, [cst, dg], [ps_t])
                cp(DVE, Gb[:, c * 128:(c + 1) * 128], ps_t[:, 0:128], [ps_t], [Gb])

        def phase_wout(l, src):
            need_ctx = l < DEPTH - 1
            with contextlib.ExitStack() as st:
                wo = k.sb(st, "o_w", [128, 16, D], BF16)
                for g in range(4):
                    k.dma(wo[:, g * 4:(g + 1) * 4, :], WOUT[l][g * 512:(g + 1) * 512, :].rearrange("(mc p) n -> p mc n", p=128), writes=[wo])
                Gb = [k.sb(st, f"o_gb{j}", [128, D], F32) for j in range(2)]
                dg = k.sb(st, "o_dg", [128, 128], F32)
                ps = [k.ps(st, f"o_ps{i}", [128, 512], F32) for i in range(4)]
                for j in range(2 if need_ctx else 1):
                    build_gate_bcast(Gb[j], l, 32, j, ps[0], dg)
                mx = [k.sb(st, f"o_mx{i}", [128, 16, 512], BF16) for i in range(2)]
                xt = [k.sb(st, f"o_x{i}", [128, D], F32) for i in range(2)]
                xo = [k.sb(st, f"o_xo{i}", [128, D], F32) for i in range(2)]
                tmp = [k.sb(st, f"o_t{i}", [128, 512], F32) for i in range(2)]
                bi = 0
                ti = 0
                for (t0, nt) in (BLOCKS if need_ctx else BLOCKS[1:]):
                    j = 1 if t0 == 0 else 0
                    m = mx[bi % 2]
                    bi += 1
                    k.dma(m[:, :, 0:nt], MIXT[:, :, t0:t0 + nt].rearrange("c p t -> p c t"), writes=[m])
                    for s in range(nt // 128):
                        x, o = xt[ti % 2], xo[ti % 2]
                        r0 = t0 + s * 128
                        k.dma(x[:], src[r0:r0 + 128, :], writes=[x])
                        for n in range(4):
                            p, t = ps[(ti * 4 + n) % 4], tmp[n % 2]
                            ns_ = slice(n * 512, (n + 1) * 512)
                            for mc in range(16):
                                mm(p[:], m[:, mc, s * 128:(s + 1) * 128], wo[:, mc, ns_], mc == 0, mc == 15, [m, wo], [p])
                            tt(DVE, t[:], p[:], Gb[j][:, ns_], ALU.mult, [p, Gb[j]], [t])
                            tt(POOL, o[:, ns_], t[:], x[:, ns_], ALU.add, [t, x], [o])
                        k.dma(XH[r0:r0 + 128, :], o[:], reads=[o], q=POOL)
                        ti += 1
            k.barrier()

        def phase_ffn(l):
            need_ctx = l < DEPTH - 1
            last = l == n_layers - 1 and l == DEPTH - 1
            with contextlib.ExitStack() as st:
                nb = alloc_norm_bufs(st, nx=2)
                xmT = nb[5]
                Gb = k.sb(st, "f_gb", [128, D], F32)
                dg = k.sb(st, "f_dg", [128, 128], F32)
                wg = [k.sb(st, f"f_wg{i}", [128, 16, 256], BF16) for i in range(2)]
                wu = [k.sb(st, f"f_wu{i}", [128, 16, 256], BF16) for i in range(2)]
                wd = [k.sb(st, f"f_wd{i}", [128, 11, 512], BF16) for i in range(2)]
                hid = k.sb(st, "f_hid", [128, 44, 512], BF16)
                sg = [k.sb(st, f"f_sg{i}", [128, 512], F32) for i in range(2)]
                hx = [k.sb(st, f"f_hx{i}", [128, 512], F32) for i in range(2)]
                ot = [k.sb(st, f"f_ot{i}", [128, 512], F32) for i in range(2)]
                acc = [k.ps(st, f"f_acc{i}", [128, 512], F32) for i in range(4)]
                pg = k.ps(st, "f_pg", [128, 512], F32)
                pu = k.ps(st, "f_pu", [128, 512], F32)
                wi = 0
                di = 0
                oi = 0
                gb_for = None
                fblocks = BLOCKS if need_ctx else BLOCKS[1:]
                norm_load(nb, XH, *fblocks[0])
                for bi, (t0, nt) in enumerate(fblocks):
                    is_ctx = t0 == 0
                    j = 1 if is_ctx else 0
                    ns = nt // 128
                    if gb_for != j:
                        build_gate_bcast(Gb, l, 80, j, pg, dg)
                        gb_for = j
                    norm_T(nb, nt, l, 1, is_ctx)
                    for jp in range(22):
                        g_, u_ = wg[wi % 2], wu[wi % 2]
                        wi += 1
                        k.dma(g_[:], WGU[l][:, jp * 256:(jp + 1) * 256].rearrange("(kc p) c -> p kc c", p=128), writes=[g_])
                        k.dma(u_[:], WGU[l][:, FF + jp * 256:FF + (jp + 1) * 256].rearrange("(kc p) c -> p kc c", p=128), writes=[u_])
                        for q in range(2):
                            jj = jp * 2 + q
                            for kc in range(16):
                                mm(pg[:, 0:nt], g_[:, kc, q * 128:(q + 1) * 128], xmT[:, kc, 0:nt], kc == 0, kc == 15, [g_, xmT], [pg])
                            for kc in range(16):
                                mm(pu[:, 0:nt], u_[:, kc, q * 128:(q + 1) * 128], xmT[:, kc, 0:nt], kc == 0, kc == 15, [u_, xmT], [pu])
                            s_ = sg[jj % 2]
                            act(s_[:, 0:nt], pg[:, 0:nt], AF.Silu, [pg], [s_])
                            tt(DVE, hid[:, jj, 0:nt], s_[:, 0:nt], pu[:, 0:nt], ALU.mult, [s_, pu], [hid])
                    if bi + 1 < len(fblocks):
                        norm_load(nb, XH, *fblocks[bi + 1])
                    for n in range(4):
                        ns_ = slice(n * 512, (n + 1) * 512)
                        for jg in range(4):
                            w_ = wd[di % 2]
                            di += 1
                            k.dma(w_[:], WDN[l][jg * 11 * 128:(jg + 1) * 11 * 128, ns_].rearrange("(j p) n -> p j n", p=128), writes=[w_])
                            for s in range(ns):
                                for jx in range(11):
                                    jj = jg * 11 + jx
                                    mm(acc[s][:], hid[:, jj, s * 128:(s + 1) * 128], w_[:, jx, :], jj == 0, jj == 43, [hid, w_], [acc[s]])
                        for s in range(ns):
                            h_, o_ = hx[oi % 2], ot[oi % 2]
                            oi += 1
                            r0 = t0 + s * 128
                            k.dma(h_[:], XH[r0:r0 + 128, ns_], writes=[h_], q=POOL)
                            tt(DVE, o_[:], acc[s][:], Gb[:, ns_], ALU.mult, [acc[s], Gb], [o_])
                            tt(POOL, o_[:], o_[:], h_[:], ALU.add, [o_, h_], [o_])
                            if last:
                                k.dma(y_d[r0 - CTX:r0 - CTX + 128, ns_], o_[:], reads=[o_], q=POOL)
                            else:
                                k.dma(XR[r0:r0 + 128, ns_], o_[:], reads=[o_], q=POOL)
            k.barrier()

        def done(tag):
            return stop_after == tag

        pre = "SKIPPRE" not in dump
        if pre:
            phase_cast(range(n_layers))
        if pre and not done("cast"):
            phase_mod(range(n_layers))
        if not done("cast") and not done("mod"):
            for l in range(n_layers):
                src = xin if l == 0 else XR
                if pre:
                    phase_A(l, src)
                if done(f"A{l}"):
                    break
                if "SKIPDA" not in dump:
                    phase_da_prep(l)
                    phase_attn(l, "da")
                if done(f"DA{l}"):
                    break
                phase_ml_prep(l)
                if done(f"LP{l}"):
                    break
                phase_ml_scan(l)
                if done(f"ML{l}"):
                    break
                if "SKIPMLA" not in dump:
                    phase_mla_prep(l)
                    phase_attn(l, "mla")
                if done(f"MLA{l}"):
                    break
                phase_wout(l, src)
                if done(f"O{l}"):
                    break
                phase_ffn(l)
                if done(f"F{l}"):
                    break
        if "MODT" in dump:
            md = nc.dram_tensor("MODT", [128, DEPTH * 96 * 2], F32, kind="ExternalOutput").ap()
            k.dma(md[:, :], modT[:].rearrange("p l c j -> p (l c j)"), reads=[modT])
        k.barrier()
    k.close()
    return nc


def _consts():
    cst = np.zeros((128, NCS), np.float32)
    p = np.arange(128)
    cst[:, CS_ID:CS_ID + 128] = np.eye(128, dtype=np.float32)
    cst[:, CS_ONES:CS_ONES + 128] = 1.0
    cst[:, CS_BLK64:CS_BLK64 + 128] = (p[:, None] // 64 == p[None, :] // 64)
    rot = np.zeros((128, 128), np.float32)
    for dp in range(128):
        if dp % 32 < 16:
            rot[dp + 16, dp] = -1.0
        else:
            rot[dp - 16, dp] = 1.0
    cst[:, CS_ROT:CS_ROT + 128] = rot
    cst[:, CS_U:CS_U + 128] = (p[:, None] <= p[None, :])
    cst[:, CS_L:CS_L + 128] = (p[:, None] >= p[None, :])
    t = np.arange(SEQ)
    row = (t // GRID_W).astype(np.float32)
    col = (t % GRID_W).astype(np.float32)
    half = 16
    inv = (10000.0 ** (-np.arange(half, dtype=np.float32) / half)).astype(np.float32)
    rope = np.zeros((2, 128, SEQ), np.float32)
    for d in range(128):
        pos = row if (d % 64) // 32 == 0 else col
        ang = (pos * inv[d % 16]).astype(np.float32)
        rope[0, d] = np.cos(ang)
        rope[1, d] = np.sin(ang)
    return cst, rope


def _fm(v, nchunk):
    return np.ascontiguousarray(np.asarray(v, np.float32).reshape(nchunk, 128).T)


def _pack_pv(inp):
    pv = np.zeros((128, DEPTH * PVL), np.float32)
    for l in range(DEPTH):
        o = l * PVL
        pv[:, o + PV_N1G:o + PV_N1G + 16] = _fm(inp["norm1_g"][l], 16)
        pv[:, o + PV_N2G:o + PV_N2G + 16] = _fm(inp["norm2_g"][l], 16)
        for j in range(2):
            pv[:, o + PV_DAQG + j] = np.tile(inp["da_qk_g"][l, j], 2)
        pv[:, o + PV_DAOG] = inp["da_out_g"][l]
        cw = np.asarray(inp["ml_conv_w"][l])
        for ch in range(8):
            for j in range(3):
                pv[:, o + PV_MLCW + ch * 3 + j] = cw[j, ch * 128:(ch + 1) * 128]
        pv[:, o + PV_MLCB:o + PV_MLCB + 8] = _fm(inp["ml_conv_b"][l], 8)
        pv[:, o + PV_MLOG:o + PV_MLOG + 4] = _fm(inp["ml_out_g"][l], 4)
        pv[:, o + PV_MQG:o + PV_MQG + 4] = _fm(inp["mla_q_norm_g"][l], 4)
        pv[:, o + PV_MKVG:o + PV_MKVG + 2] = _fm(inp["mla_kv_norm_g"][l], 2)
        for j in range(2):
            pv[:, o + PV_MQKN + j] = inp["mla_qk_g"][l, j, :128]
            pv[:64, o + PV_MQKR + j] = inp["mla_qk_g"][l, j, 128:]
        pv[:, o + PV_DALAM:o + PV_DALAM + 256] = np.asarray(inp["da_lambda"][l]).reshape(1, 256)
        pv[:, o + PV_MLGB:o + PV_MLGB + 16] = np.asarray(inp["ml_gate_b"][l]).reshape(1, 16)
    return pv


def make_in_maps(inp, n_cores):
    f = lambda a: np.ascontiguousarray(np.asarray(a, np.float32))
    cst, rope = _consts()
    pv = _pack_pv(inp)
    wukv = np.asarray(inp["mla_w_ukv"], np.float32).reshape(DEPTH, 256, 8, 2, 128)
    wukv = np.ascontiguousarray(wukv.transpose(0, 1, 3, 2, 4).reshape(DEPTH, 256, 2048))
    shared = {
        "mod_w": f(inp["mod_w"]), "mod_bT": np.ascontiguousarray(f(inp["mod_b"]).reshape(DEPTH, 96, 128).transpose(0, 2, 1)),
        "w_in": f(inp["w_in"]), "w_out": f(inp["w_out"]), "w_gu": f(inp["ffn_w_gu"]), "w_down": f(inp["ffn_w_down"]),
        "w_uq": f(inp["mla_w_uq"]), "w_ukv": wukv, "pv": pv, "cst": cst, "rope": rope,
    }
    maps = []
    for c in range(n_cores):
        b = c % 4
        m = dict(shared)
        m["xin"] = np.ascontiguousarray(np.concatenate([inp["ctx"][b], inp["x"][b]], axis=0).astype(np.float32))
        ccv = np.stack([np.asarray(inp["c"][b], np.float32), np.asarray(inp["c_ctx"], np.float32)], axis=-1)
        m["cc"] = np.ascontiguousarray(ccv.reshape(16, 128, 2).transpose(1, 0, 2))
        maps.append(m)
    return maps


N_CORES = 4


def kernel(**inputs):
    nc = build()
    maps = make_in_maps(inputs, N_CORES)
    res = run_bass_kernel_spmd(nc, maps, core_ids=list(range(N_CORES)))
    out = np.stack([res.results[b]["y"] for b in range(4)], axis=0)
    return out.astype(np.float32)
```

```python
import contextlib
import math
import numpy as np
import concourse.bass as bass
import concourse.mybir as mybir
from concourse.bass_utils import run_bass_kernel_spmd

F32 = mybir.dt.float32
BF16 = mybir.dt.bfloat16
AF = mybir.ActivationFunctionType
ALU = mybir.AluOpType

SAME_ENGINE_SYNC = True
RELAX_SAME_ENGINE = False
SMALL_FREE = 256
N_DMA_SLOTS = 24

D = 2048
SEQ = 4096
CTX = 256
NTOK = SEQ + CTX
DEPTH = 2
GRID_W = 64
IN_W = 4432
FF = 5632
EPS = 1e-6
NT128 = NTOK // 128
BLOCKS = [(0, 256)] + [(256 + 512 * i, 512) for i in range(8)]
C_DAQ, C_DAK, C_DAV, C_MLQ, C_MLK, C_MLV, C_MLO, C_GAT, C_CQ, C_CKV, C_KPE = (
    0, 512, 1024, 1536, 2048, 2560, 3072, 3584, 3600, 4112, 4368)
R_DAQ, R_DAK, R_MLQ, R_MLK, R_MLO, R_CQ, R_CKV, R_KPE, N_RAW = 0, 4, 8, 12, 16, 20, 24, 26, 27
PV_N1G, PV_N2G, PV_DAQG, PV_DAOG, PV_MLCW, PV_MLCB, PV_MLOG, PV_MQG, PV_MKVG, PV_MQKN, PV_MQKR, PV_DALAM, PV_MLGB, PVL = (
    0, 16, 32, 34, 35, 59, 67, 71, 75, 77, 79, 81, 337, 353)
CS_ID, CS_ONES, CS_BLK64, CS_ROT, CS_U, CS_L, NCS = 0, 128, 256, 384, 512, 640, 768


class Eng:
    def __init__(self, name, h, sem, is_pe=False):
        self.name, self.h, self.sem = name, h, sem
        self.cnt = 0
        self.seen = {}
        self.is_pe = is_pe


class Tl:
    __slots__ = ("ap", "w", "r", "name", "small")

    def __init__(self, ap, name=""):
        self.ap, self.w, self.r, self.name = ap, None, {}, name
        n = 1
        for d in list(ap.shape)[1:]:
            n *= int(d)
        self.small = n < SMALL_FREE

    def __getitem__(self, idx):
        return self.ap[idx]


class K:
    def __init__(self, nc):
        self.nc = nc
        self.es = contextlib.ExitStack()
        sem = lambda n: self.es.enter_context(nc.semaphore(n))
        self.pe = Eng("pe", nc.tensor, sem("s_pe"), is_pe=True)
        self.act = Eng("act", nc.scalar, sem("s_act"))
        self.dve = Eng("dve", nc.vector, sem("s_dve"))
        self.pool = Eng("pool", nc.gpsimd, sem("s_pool"))
        self.sp = Eng("sp", nc.sync, sem("s_sp"))
        self.engs = [self.pe, self.act, self.dve, self.pool, self.sp]
        self.slots = [Eng(f"dq{i}", None, sem(f"s_dq{i}")) for i in range(N_DMA_SLOTS)]
        self.slot_i = 0
        self.n_ins = 0
        self.uid = 0

    def close(self):
        self.es.close()

    def sb(self, stack, name, shape, dt):
        self.uid += 1
        return Tl(stack.enter_context(self.nc.sbuf_tensor(f"sb{self.uid}_{name}", list(shape), dt)), name)

    def ps(self, stack, name, shape, dt=F32):
        self.uid += 1
        shape = list(shape)
        esz = 4 if dt == F32 else 2
        rest = esz
        for d in shape[2:]:
            rest *= d
        full1 = 2048 // rest
        assert full1 >= shape[1] and full1 * rest == 2048, (name, shape)
        t = stack.enter_context(self.nc.psum_tensor(f"ps{self.uid}_{name}", [shape[0], full1] + shape[2:], dt))
        if full1 == shape[1]:
            return Tl(t, name)
        return Tl(t[:, 0:shape[1]], name)

    def _deps(self, reads, writes, eng=None):
        deps = {}
        for t in reads:
            if t.w is not None and deps.get(t.w[0], 0) < t.w[1] and not (t.w[0] is eng and not t.small):
                deps[t.w[0]] = t.w[1]
        for t in writes:
            if t.w is not None and deps.get(t.w[0], 0) < t.w[1] and not (t.w[0] is eng and not t.small):
                deps[t.w[0]] = t.w[1]
            for e, c in t.r.items():
                if deps.get(e, 0) < c and not (e is eng and not t.small):
                    deps[e] = c
        return deps

    def _waits(self, eng, deps):
        for e, c in deps.items():
            if e is eng and (eng.is_pe or not SAME_ENGINE_SYNC):
                continue
            if eng.seen.get(e, 0) >= c:
                continue
            eng.h.wait_ge(e.sem, c)
            eng.seen[e] = c

    def _mark(self, who, val, reads, writes):
        for t in reads:
            if t.r.get(who, 0) < val:
                t.r[who] = val
        for t in writes:
            t.w = (who, val)
            t.r = {}

    def op(self, eng, fn, reads=(), writes=()):
        self._waits(eng, self._deps(reads, writes, eng if RELAX_SAME_ENGINE else None))
        ins = fn()
        eng.cnt += 1
        ins.then_inc(eng.sem, 1)
        self._mark(eng, eng.cnt, reads, writes)
        self.n_ins += 1
        return ins

    def dma(self, out, in_, reads=(), writes=(), q=None):
        q = q or self.sp
        slot = self.slots[self.slot_i]
        self.slot_i = (self.slot_i + 1) % len(self.slots)
        deps = self._deps(reads, writes)
        if slot.cnt > 0:
            deps[slot] = max(deps.get(slot, 0), slot.cnt)
        self._waits(q, deps)
        ins = q.h.dma_start(out=out, in_=in_)
        slot.cnt += 16
        ins.then_inc(slot.sem, 16)
        self._mark(slot, slot.cnt, reads, writes)
        self.n_ins += 1
        return ins

    def barrier(self):
        sp = self.sp
        for s in self.slots + self.engs:
            if s is sp:
                continue
            if s.cnt > 0 and sp.seen.get(s, 0) < s.cnt:
                sp.h.wait_ge(s.sem, s.cnt)
                sp.seen[s] = s.cnt
        sp.cnt += 1
        sp.h.nop().then_inc(sp.sem, 1)
        for e in self.engs:
            if e is sp:
                continue
            e.h.wait_ge(sp.sem, sp.cnt)
            for o in self.engs + self.slots:
                e.seen[o] = o.cnt
        for o in self.engs + self.slots:
            sp.seen[o] = o.cnt


def build(n_layers=DEPTH, stop_after=None, dump=()):
    nc = bass.Bass("TRN2", target_bir_lowering=False)
    k = K(nc)
    print("[build] sbuf bytes remaining at start:", nc.sbuf_bytes_remaining, flush=True)
    PE, ACT, DVE, POOL = k.pe, k.act, k.dve, k.pool

    def din(name, shape, dt=F32):
        return nc.dram_tensor(name, list(shape), dt, kind="ExternalInput").ap()

    def dscr(name, shape, dt):
        kind = "ExternalOutput" if name in dump else "Internal"
        return nc.dram_tensor(name, list(shape), dt, kind=kind).ap()

    xin = din("xin", [NTOK, D])
    cc_d = din("cc", [128, 16, 2])
    modw_d = din("mod_w", [DEPTH, D, 6 * D])
    modb_d = din("mod_bT", [DEPTH, 128, 96])
    win_d = din("w_in", [DEPTH, D, IN_W])
    wout_d = din("w_out", [DEPTH, D, D])
    wgu_d = din("w_gu", [DEPTH, D, 2 * FF])
    wdn_d = din("w_down", [DEPTH, FF, D])
    wuq_d = din("w_uq", [DEPTH, 512, 1536])
    wukv_d = din("w_ukv", [DEPTH, 256, 2048])
    pv_d = din("pv", [128, DEPTH * PVL])
    cst_d = din("cst", [128, NCS])
    rope_d = din("rope", [2, 128, SEQ])
    y_d = nc.dram_tensor("y", [SEQ, D], F32, kind="ExternalOutput").ap()

    WIN = [dscr(f"WIN{l}", [D, IN_W], BF16) for l in range(DEPTH)]
    WOUT = [dscr(f"WOUT{l}", [D, D], BF16) for l in range(DEPTH)]
    WGU = [dscr(f"WGU{l}", [D, 2 * FF], BF16) for l in range(DEPTH)]
    WDN = [dscr(f"WDN{l}", [FF, D], BF16) for l in range(DEPTH)]
    WUQ = [dscr(f"WUQ{l}", [512, 1536], BF16) for l in range(DEPTH)]
    WUKV = [dscr(f"WUKV{l}", [256, 2048], BF16) for l in range(DEPTH)]
    XR = dscr("XR", [NTOK, D], F32)
    XH = dscr("XH", [NTOK, D], F32)
    RAWT = dscr("RAWT", [N_RAW, 128, NTOK], F32)
    VDA = dscr("VDA", [NTOK, 512], BF16)
    VML = dscr("VML", [NTOK, 512], BF16)
    GATES = dscr("GATES", [NTOK, 16], F32)
    MIXT = dscr("MIXT", [16, 128, NTOK], BF16)

    def act(out, in_, func, R, W, **kw):
        return k.op(ACT, lambda: nc.scalar.activation(out=out, in_=in_, func=func, **kw), R, W)

    def tt(e, out, a, b, op, R, W):
        return k.op(e, lambda: e.h.tensor_tensor(out=out, in0=a, in1=b, op=op), R, W)

    def ts(e, out, a, s1, s2, op0, op1, R, W):
        if s2 is None:
            return k.op(e, lambda: e.h.tensor_scalar(out=out, in0=a, scalar1=s1, scalar2=None, op0=op0), R, W)
        return k.op(e, lambda: e.h.tensor_scalar(out=out, in0=a, scalar1=s1, scalar2=s2, op0=op0, op1=op1), R, W)

    def stt(e, out, a, s, b, op0, op1, R, W):
        return k.op(e, lambda: e.h.scalar_tensor_tensor(out=out, in0=a, scalar=s, in1=b, op0=op0, op1=op1), R, W)

    def cp(e, out, in_, R, W):
        if e is ACT:
            return k.op(e, lambda: nc.scalar.copy(out=out, in_=in_), R, W)
        return k.op(e, lambda: e.h.tensor_copy(out=out, in_=in_), R, W)

    def mm(out, lhsT, rhs, start, stop, R, W):
        return k.op(PE, lambda: nc.tensor.matmul(out, lhsT=lhsT, rhs=rhs, start=start, stop=stop), R, W)

    def tr(out, in_, ident, R, W):
        return k.op(PE, lambda: nc.tensor.transpose(out, in_, ident), R, W)

    with contextlib.ExitStack() as glob:
        cst = k.sb(glob, "cst", [128, NCS], F32)
        cstb = k.sb(glob, "cstb", [128, 256], BF16)
        pv = k.sb(glob, "pv", [128, DEPTH * PVL], F32)
        modT = k.sb(glob, "modT", [128, DEPTH, 96, 2], F32)
        AB = k.sb(glob, "AB", [128, DEPTH, 2, 16, 2], F32)
        k.dma(cst[:], cst_d[:, :], writes=[cst])
        k.dma(pv[:], pv_d[:, :], writes=[pv])
        cp(DVE, cstb[:], cst[:, 0:256], [cst], [cstb])
        identb = cstb[:, 0:128]
        onesb = cstb[:, 128:256]
        identf = cst[:, CS_ID:CS_ID + 128]
        onesf = cst[:, CS_ONES:CS_ONES + 128]

        def pvc(l, off, n=1):
            return pv[:, l * PVL + off:l * PVL + off + n]

        def mod_steps(st, layers):
            cc = k.sb(st, "cc", [128, 16, 2], F32)
            sc = k.sb(st, "sc", [128, 16, 2], F32)
            mb = k.sb(st, "mb", [128, 96], F32)
            wb = [k.sb(st, f"mw{i}", [128, 16, 512], F32) for i in range(2)]
            mps = k.ps(st, "mps", [128, 96, 2], F32)
            k.dma(cc[:], cc_d[:, :, :], writes=[cc])
            act(sc[:], cc[:], AF.Silu, [cc], [sc])
            it = 0
            for l in layers:
                k.dma(mb[:], modb_d[l], writes=[mb])
                for g in range(24):
                    w = wb[it % 2]
                    it += 1
                    k.dma(w[:], modw_d[l][:, g * 512:(g + 1) * 512].rearrange("(kc p) c -> p kc c", p=128), writes=[w])
                    for j in range(4):
                        c = g * 4 + j
                        for kc in range(16):
                            mm(mps[:, c, :], w[:, kc, j * 128:(j + 1) * 128], sc[:, kc, :], kc == 0, kc == 15, [w, sc], [mps])
                    yield
                for j in range(2):
                    tt(DVE, modT[:, l, :, j], mps[:, :, j], mb[:], ALU.add, [mps, mb], [modT])
                for n, (goff, soff) in enumerate(((PV_N1G, 16), (PV_N2G, 64))):
                    for j in range(2):
                        stt(DVE, AB[:, l, n, :, j], modT[:, l, soff:soff + 16, j], 1.0, pvc(l, goff, 16),
                            ALU.add, ALU.mult, [modT, pv], [AB])
                yield

        def phase_cast_mod(layers, do_mod=True):
            with contextlib.ExitStack() as st:
                CW = 2048
                fb = [k.sb(st, f"cw_f{i}", [128, CW], F32) for i in range(3)]
                bb = [k.sb(st, f"cw_b{i}", [128, CW], BF16) for i in range(3)]
                engs = [DVE, POOL, ACT]
                gen = mod_steps(st, layers) if do_mod else iter(())
                it = 0
                for l in layers:
                    for src, dst, Rr, Cc in ((win_d[l], WIN[l], D, IN_W), (wuq_d[l], WUQ[l], 512, 1536),
                                             (wukv_d[l], WUKV[l], 256, 2048), (wout_d[l], WOUT[l], D, D),
                                             (wgu_d[l], WGU[l], D, 2 * FF), (wdn_d[l], WDN[l], FF, D)):
                        for r0 in range(0, Rr, 128):
                            for c0 in range(0, Cc, CW):
                                cw = min(CW, Cc - c0)
                                f, b, e = fb[it % 3], bb[it % 3], engs[it % 3]
                                k.dma(f[:, 0:cw], src[r0:r0 + 128, c0:c0 + cw], writes=[f])
                                cp(e, b[:, 0:cw], f[:, 0:cw], [f], [b])
                                k.dma(dst[r0:r0 + 128, c0:c0 + cw], b[:, 0:cw], reads=[b], q=POOL)
                                it += 1
                                if it % 8 == 0:
                                    next(gen, None)
                for _ in gen:
                    pass
            k.barrier()

        def norm_load(st_bufs, src, t0, nt):
            xt, xs, junk, stats, tps, xmT = st_bufs
            for s in range(nt // 128):
                x = xt[s % len(xt)]
                stat = stats[s]
                k.dma(x[:], src[t0 + s * 128:t0 + (s + 1) * 128, :], writes=[x])
                act(junk[:], x[:], AF.Square, [x], [junk, stat], accum_out=stat[:, 0:1])
                act(stat[:, 1:2], stat[:, 0:1], AF.Sqrt, [stat], [stat], scale=1.0 / D, bias=EPS)
                k.op(DVE, lambda: nc.vector.reciprocal(out=stat[:, 2:3], in_=stat[:, 1:2]), [stat], [stat])
                e = DVE if s % 2 == 0 else POOL
                ts(e, xs[s][:], x[:], stat[:, 2:3], None, ALU.mult, None, [x, stat], [xs[s]])

        def norm_T(st_bufs, nt, l, which, is_ctx):
            xt, xs, junk, stats, tps, xmT = st_bufs
            ns = nt // 128
            j = 1 if is_ctx else 0
            shoff = 0 if which == 0 else 48
            for kp in range(8):
                tp = tps[kp % 2]
                for q in range(2):
                    kc = kp * 2 + q
                    for s in range(ns):
                        tr(tp[:, q, s * 128:(s + 1) * 128], xs[s][:, kc * 128:(kc + 1) * 128], identb, [xs[s], cstb], [tp])
                for q in range(2):
                    kc = kp * 2 + q
                    a_ap = AB[:, l, which, kc, j:j + 1]
                    b_ap = modT[:, l, shoff + kc, j:j + 1]
                    if q == 0:
                        act(xmT[:, kc, 0:nt], tp[:, q, 0:nt], AF.Identity, [tp, AB, modT], [xmT], scale=a_ap, bias=b_ap)
                    else:
                        ts(DVE, xmT[:, kc, 0:nt], tp[:, q, 0:nt], a_ap, b_ap, ALU.mult, ALU.add, [tp, AB, modT], [xmT])

        def alloc_norm_bufs(st, nx=2):
            xt = [k.sb(st, f"nm_x{i}", [128, D], F32) for i in range(nx)]
            xs = [k.sb(st, f"nm_xs{i}", [128, D], BF16) for i in range(4)]
            junk = k.sb(st, "nm_junk", [128, D], BF16)
            stat = [k.sb(st, f"nm_stat{i}", [128, 4], F32) for i in range(4)]
            tps = [k.ps(st, f"nm_tp{i}", [128, 2, 512], BF16) for i in range(2)]
            xmT = k.sb(st, "nm_xmT", [128, 16, 512], BF16)
            return xt, xs, junk, stat, tps, xmT

        GROUPS = [
            (C_DAQ, 512, "F", R_DAQ), (C_DAK, 512, "F", R_DAK), (C_DAV, 512, "V", VDA),
            (C_MLQ, 512, "F", R_MLQ), (C_MLK, 512, "F", R_MLK), (C_MLV, 512, "V", VML),
            (C_MLO, 512, "F", R_MLO), (C_GAT, 528, "G", R_CQ), (C_CKV, 320, "F", R_CKV)]

        def phase_A(l, src):
            with contextlib.ExitStack() as st:
                nb = alloc_norm_bufs(st)
                xmT = nb[5]
                wt = [k.sb(st, f"a_w{i}", [128, 16, 528], BF16) for i in range(2)]
                ps = [k.ps(st, f"a_ps{i}", [128, 512], F32) for i in range(4)]
                stg = [k.sb(st, f"a_sf{i}", [128, 512], F32) for i in range(4)]
                stb = [k.sb(st, f"a_sb{i}", [128, 512], BF16) for i in range(4)]
                stgt = [k.sb(st, f"a_sg{i}", [128, 16], F32) for i in range(2)]
                wi = 0
                oi = 0
                norm_load(nb, src, *BLOCKS[0])
                for bi, (t0, nt) in enumerate(BLOCKS):
                    is_ctx = t0 == 0
                    ns = nt // 128
                    norm_T(nb, nt, l, 0, is_ctx)
                    for gi, (c0, ncol, kind, arg) in enumerate(GROUPS):
                        if gi == 2 and bi + 1 < len(BLOCKS):
                            norm_load(nb, src, *BLOCKS[bi + 1])
                        w = wt[wi % 2]
                        wi += 1
                        k.dma(w[:, :, 0:ncol], WIN[l][:, c0:c0 + ncol].rearrange("(kc p) c -> p kc c", p=128), writes=[w])

                        def feat(cofs, rows, rawrow):
                            nonlocal oi
                            p, sf = ps[oi % 4], stg[oi % 4]
                            for kc in range(16):
                                mm(p[0:rows, 0:nt], w[:, kc, cofs:cofs + rows], xmT[:, kc, 0:nt], kc == 0, kc == 15, [w, xmT], [p])
                            cp(ACT if oi % 2 == 0 else DVE, sf[0:rows, 0:nt], p[0:rows, 0:nt], [p], [sf])
                            k.dma(RAWT[rawrow, 0:rows, t0:t0 + nt], sf[0:rows, 0:nt], reads=[sf], q=POOL)
                            oi += 1

                        if kind == "F":
                            nch = (ncol + 127) // 128
                            for ch in range(nch):
                                rows = min(128, ncol - ch * 128)
                                feat(ch * 128, rows, arg + ch)
                        elif kind == "G":
                            for s in range(ns):
                                p, sg = ps[oi % 4], stgt[s % 2]
                                for kc in range(16):
                                    mm(p[:, 0:16], xmT[:, kc, s * 128:(s + 1) * 128], w[:, kc, 0:16], kc == 0, kc == 15, [w, xmT], [p])
                                cp(DVE, sg[:], p[:, 0:16], [p], [sg])
                                k.dma(GATES[t0 + s * 128:t0 + (s + 1) * 128, :], sg[:], reads=[sg], q=POOL)
                                oi += 1
                            for ch in range(4):
                                feat(16 + ch * 128, 128, R_CQ + ch)
                        else:
                            for s in range(ns):
                                p, sbt = ps[oi % 4], stb[oi % 4]
                                for kc in range(16):
                                    mm(p[:, :], xmT[:, kc, s * 128:(s + 1) * 128], w[:, kc, 0:512], kc == 0, kc == 15, [w, xmT], [p])
                                cp(ACT if oi % 2 == 0 else DVE, sbt[:], p[:], [p], [sbt])
                                k.dma(arg[t0 + s * 128:t0 + (s + 1) * 128, :], sbt[:], reads=[sbt], q=POOL)
                                oi += 1
            k.barrier()


        def rstd_from(ss_ps, rows, nt, inv_n, r_sb, R):
            act(r_sb[0:rows, 0:nt], ss_ps[0:rows, 0:nt], AF.Sqrt, [ss_ps] + R, [r_sb], scale=inv_n, bias=EPS)
            k.op(DVE, lambda: nc.vector.reciprocal(out=r_sb[0:rows, 0:nt], in_=r_sb[0:rows, 0:nt]), [r_sb], [r_sb])

        def rope_apply(xn, rows, tl0, nt, ropet, rot_ps, t1, out_t):
            mm(rot_ps[0:rows, 0:nt], cst[0:rows, CS_ROT:CS_ROT + rows], xn[0:rows, 0:nt], True, True, [cst, xn], [rot_ps])
            tt(POOL, t1[0:rows, 0:nt], xn[0:rows, 0:nt], ropet[0:rows, 0, tl0:tl0 + nt], ALU.mult, [xn, ropet], [t1])
            tt(DVE, xn[0:rows, 0:nt], rot_ps[0:rows, 0:nt], ropet[0:rows, 1, tl0:tl0 + nt], ALU.mult, [rot_ps, ropet, xn], [xn])
            tt(POOL, out_t[0:rows, 0:nt], t1[0:rows, 0:nt], xn[0:rows, 0:nt], ALU.add, [t1, xn], [out_t])

        DAQT = dscr("DAQT", [4, 128, NTOK], BF16)
        DAKT = dscr("DAKT", [4, 128, NTOK], BF16)
        MQN = dscr("MQN", [8, 128, NTOK], BF16)
        MQR = dscr("MQR", [8, 64, NTOK], BF16)
        MKN = dscr("MKN", [8, 128, NTOK], BF16)
        MKR = dscr("MKR", [8, 64, NTOK], BF16)
        MV = dscr("MV", [NTOK, 1024], BF16)

        def phase_da_prep(l):
            with contextlib.ExitStack() as st:
                ropet = k.sb(st, "dp_rope", [128, 2, SEQ], F32)
                k.dma(ropet[:], rope_d.rearrange("a p t -> p a t"), writes=[ropet])
                rt = [k.sb(st, f"dp_rt{i}", [128, 512], F32) for i in range(2)]
                sq_ = [k.sb(st, f"dp_sq{i}", [128, 512], F32) for i in range(2)]
                rr_ = [k.sb(st, f"dp_r{i}", [128, 512], F32) for i in range(2)]
                xn_ = [k.sb(st, f"dp_xn{i}", [128, 512], F32) for i in range(2)]
                t1_ = [k.sb(st, f"dp_t1{i}", [128, 512], F32) for i in range(2)]
                ob = [k.sb(st, f"dp_o{i}", [128, 512], BF16) for i in range(2)]
                ss_ = [k.ps(st, f"dp_ss{i}", [128, 512], F32) for i in range(2)]
                rps_ = [k.ps(st, f"dp_rot{i}", [128, 512], F32) for i in range(2)]
                it = 0
                for which, (rrow, dst) in enumerate(((R_DAQ, DAQT), (R_DAK, DAKT))):
                    for h in range(4):
                        for (t0, nt) in BLOCKS:
                            r, o = rt[it % 2], ob[it % 2]
                            sq, rr, xn, t1, ss, rps = sq_[it % 2], rr_[it % 2], xn_[it % 2], t1_[it % 2], ss_[it % 2], rps_[it % 2]
                            it += 1
                            k.dma(r[:, 0:nt], RAWT[rrow + h, :, t0:t0 + nt], writes=[r])
                            act(sq[:, 0:nt], r[:, 0:nt], AF.Square, [r], [sq])
                            mm(ss[:, 0:nt], cst[:, CS_BLK64:CS_BLK64 + 128], sq[:, 0:nt], True, True, [cst, sq], [ss])
                            rstd_from(ss, 128, nt, 1.0 / 64, rr, [])
                            stt(DVE, xn[:, 0:nt], r[:, 0:nt], pvc(l, PV_DAQG + which), rr[:, 0:nt], ALU.mult, ALU.mult, [r, rr, pv], [xn])
                            if t0 == 0:
                                cp(POOL, o[:, 0:nt], xn[:, 0:nt], [xn], [o])
                            else:
                                rope_apply(xn, 128, t0 - CTX, nt, ropet, rps, t1, o)
                            k.dma(dst[h, :, t0:t0 + nt], o[:, 0:nt], reads=[o], q=POOL)
            k.barrier()

        def phase_mla_prep(l):
            with contextlib.ExitStack() as st:
                ropet = k.sb(st, "mp_rope", [64, 2, SEQ], F32)
                k.dma(ropet[:], rope_d[:, 0:64, :].rearrange("a p t -> p a t"), writes=[ropet])
                wuq = k.sb(st, "mp_wuq", [128, 4, 1536], BF16)
                wukv = k.sb(st, "mp_wukv", [128, 2, 2048], BF16)
                k.dma(wuq[:], WUQ[l].rearrange("(c p) n -> p c n", p=128), writes=[wuq])
                k.dma(wukv[:], WUKV[l].rearrange("(c p) n -> p c n", p=128), writes=[wukv])
                rq = k.sb(st, "mp_rq", [128, 4, 512], F32)
                rkv = k.sb(st, "mp_rkv", [128, 2, 512], F32)
                rkp = k.sb(st, "mp_rkp", [64, 512], F32)
                sq4 = k.sb(st, "mp_sq4", [128, 4, 512], F32)
                rr = k.sb(st, "mp_rr", [128, 512], F32)
                cqn = k.sb(st, "mp_cqn", [128, 4, 512], BF16)
                ckvn = k.sb(st, "mp_ckvn", [128, 2, 512], BF16)
                sqN = [k.sb(st, f"mp_sqN{i}", [128, 512], F32) for i in range(2)]
                sqR = [k.sb(st, f"mp_sqR{i}", [64, 512], F32) for i in range(2)]
                sqRk = k.sb(st, "mp_sqRk", [64, 512], F32)
                kpr = k.sb(st, "mp_kpr", [64, 512], F32)
                kprb = k.sb(st, "mp_kprb", [64, 512], BF16)
                rh = [k.sb(st, f"mp_rh{i}", [128, 512], F32) for i in range(2)]
                qr0 = [k.sb(st, f"mp_qr0{i}", [64, 512], F32) for i in range(2)]
                t1 = k.sb(st, "mp_t1", [64, 512], F32)
                on = [k.sb(st, f"mp_on{i}", [128, 512], BF16) for i in range(3)]
                orr = [k.sb(st, f"mp_or{i}", [64, 512], BF16) for i in range(3)]
                vb = [k.sb(st, f"mp_vb{i}", [128, 512], BF16) for i in range(2)]
                ss = [k.ps(st, f"mp_ss{i}", [128, 512], F32) for i in range(2)]
                pn = [k.ps(st, f"mp_pn{i}", [128, 512], F32) for i in range(2)]
                pr = [k.ps(st, f"mp_pr{i}", [64, 512], F32) for i in range(2)]
                rps = k.ps(st, "mp_rot", [64, 512], F32)
                gq = lambda c: pvc(l, PV_MQG + c)
                gkv = lambda c: pvc(l, PV_MKVG + c)
                it = 0
                vi = 0
                for (t0, nt) in BLOCKS:
                    is_ctx = t0 == 0
                    ns = nt // 128
                    k.dma(rq[:, :, 0:nt], RAWT[R_CQ:R_CQ + 4, :, t0:t0 + nt].rearrange("c p t -> p c t"), writes=[rq])
                    k.dma(rkv[:, :, 0:nt], RAWT[R_CKV:R_CKV + 2, :, t0:t0 + nt].rearrange("c p t -> p c t"), writes=[rkv])
                    k.dma(rkp[:, 0:nt], RAWT[R_KPE, 0:64, t0:t0 + nt], writes=[rkp])
                    act(sq4[:, :, 0:nt], rq[:, :, 0:nt], AF.Square, [rq], [sq4])
                    for c in range(4):
                        mm(ss[0][:, 0:nt], onesf, sq4[:, c, 0:nt], c == 0, c == 3, [cst, sq4], [ss[0]])
                    rstd_from(ss[0], 128, nt, 1.0 / 512, rr, [])
                    for c in range(4):
                        stt(DVE, cqn[:, c, 0:nt], rq[:, c, 0:nt], gq(c), rr[:, 0:nt], ALU.mult, ALU.mult, [rq, rr, pv], [cqn])
                    act(sq4[:, 0:2, 0:nt], rkv[:, :, 0:nt], AF.Square, [rkv], [sq4])
                    for c in range(2):
                        mm(ss[1][:, 0:nt], onesf, sq4[:, c, 0:nt], c == 0, c == 1, [cst, sq4], [ss[1]])
                    rstd_from(ss[1], 128, nt, 1.0 / 256, rr, [])
                    for c in range(2):
                        stt(DVE, ckvn[:, c, 0:nt], rkv[:, c, 0:nt], gkv(c), rr[:, 0:nt], ALU.mult, ALU.mult, [rkv, rr, pv], [ckvn])
                    act(sqRk[:, 0:nt], rkp[:, 0:nt], AF.Square, [rkp], [sqRk])
                    ts(DVE, kpr[:, 0:nt], rkp[:, 0:nt], pv[0:64, l * PVL + PV_MQKR + 1:l * PVL + PV_MQKR + 2], None, ALU.mult, None, [rkp, pv], [kpr])
                    if not is_ctx:
                        rope_apply(kpr, 64, t0 - CTX, nt, ropet, rps, t1, kprb)
                        cp(POOL, kpr[:, 0:nt], kprb[:, 0:nt], [kprb], [kpr])
                    for s in range(ns):
                        for g2 in range(2):
                            p, v = pn[vi % 2], vb[vi % 2]
                            vi += 1
                            for c in range(2):
                                mm(p[:, :], ckvn[:, c, s * 128:(s + 1) * 128], wukv[:, c, 1024 + g2 * 512:1024 + (g2 + 1) * 512], c == 0, c == 1, [ckvn, wukv], [p])
                            cp(ACT, v[:], p[:], [p], [v])
                            k.dma(MV[t0 + s * 128:t0 + (s + 1) * 128, g2 * 512:(g2 + 1) * 512], v[:], reads=[v], q=POOL)
                    for h in range(8):
                        for side in range(2):
                            b = it % 2
                            it += 1
                            PN, PR, SS, SQN, SQR, RH = pn[b], pr[b], ss[b], sqN[b], sqR[b], rh[b]
                            o_n, o_r = on[it % 3], orr[it % 3]
                            if side == 0:
                                for c in range(4):
                                    mm(PN[:, 0:nt], wuq[:, c, h * 192:h * 192 + 128], cqn[:, c, 0:nt], c == 0, c == 3, [wuq, cqn], [PN])
                                for c in range(4):
                                    mm(PR[:, 0:nt], wuq[:, c, h * 192 + 128:h * 192 + 192], cqn[:, c, 0:nt], c == 0, c == 3, [wuq, cqn], [PR])
                                act(SQN[:, 0:nt], PN[:, 0:nt], AF.Square, [PN], [SQN])
                                act(SQR[:, 0:nt], PR[:, 0:nt], AF.Square, [PR], [SQR])
                                sqr_t = SQR
                            else:
                                for c in range(2):
                                    mm(PN[:, 0:nt], wukv[:, c, h * 128:(h + 1) * 128], ckvn[:, c, 0:nt], c == 0, c == 1, [wukv, ckvn], [PN])
                                act(SQN[:, 0:nt], PN[:, 0:nt], AF.Square, [PN], [SQN])
                                sqr_t = sqRk
                            mm(SS[:, 0:nt], onesf, SQN[:, 0:nt], True, False, [cst, SQN], [SS])
                            mm(SS[:, 0:nt], cst[0:64, CS_ONES:CS_ONES + 128], sqr_t[:, 0:nt], False, True, [cst, sqr_t], [SS])
                            rstd_from(SS, 128, nt, 1.0 / 192, RH, [])
                            stt(DVE, o_n[:, 0:nt], PN[:, 0:nt], pvc(l, PV_MQKN + side), RH[:, 0:nt], ALU.mult, ALU.mult, [PN, RH, pv], [o_n])
                            if side == 0:
                                Q0 = qr0[b]
                                stt(DVE, Q0[:, 0:nt], PR[:, 0:nt], pv[0:64, l * PVL + PV_MQKR:l * PVL + PV_MQKR + 1], RH[0:64, 0:nt], ALU.mult, ALU.mult, [PR, RH, pv], [Q0])
                                if is_ctx:
                                    cp(POOL, o_r[:, 0:nt], Q0[:, 0:nt], [Q0], [o_r])
                                else:
                                    rope_apply(Q0, 64, t0 - CTX, nt, ropet, rps, t1, o_r)
                                k.dma(MQN[h, :, t0:t0 + nt], o_n[:, 0:nt], reads=[o_n], q=POOL)
                                k.dma(MQR[h, :, t0:t0 + nt], o_r[:, 0:nt], reads=[o_r], q=POOL)
                            else:
                                tt(POOL, o_r[:, 0:nt], kpr[:, 0:nt], RH[0:64, 0:nt], ALU.mult, [kpr, RH], [o_r])
                                k.dma(MKN[h, :, t0:t0 + nt], o_n[:, 0:nt], reads=[o_n], q=POOL)
                                k.dma(MKR[h, :, t0:t0 + nt], o_r[:, 0:nt], reads=[o_r], q=POOL)
            k.barrier()

        def phase_attn(l, kind):
            need_ctx = l < DEPTH - 1
            da = kind == "da"
            ncomp = 2 if da else 1
            nh = 4 if da else 8
            scale = (64 if da else 192) ** -0.5
            lam_init = 0.8 - 0.6 * math.exp(-0.3 * l)
            with contextlib.ExitStack() as st:
                K0 = k.sb(st, "at_k0", [128, NTOK], BF16)
                K1 = None if da else k.sb(st, "at_k1", [64, NTOK], BF16)
                Vg = k.sb(st, "at_v", [128, NT128, 512], BF16)
                Q0 = [k.sb(st, f"at_q0{i}", [128, 512], BF16) for i in range(2)]
                Q1 = None if da else [k.sb(st, f"at_q1{i}", [64, 512], BF16) for i in range(2)]
                SD = 3 if da else 4
                S = [[k.ps(st, f"at_s{c}{b}", [128, 512], F32) for b in range(SD)] for c in range(ncomp)]
                O2 = [k.ps(st, f"at_o{c}", [128, 512], F32) for c in range(2)]
                Dps = None if da else k.ps(st, "at_d", [128, 512], F32)
                E = [[k.sb(st, f"at_e{c}{b}", [128, 512], BF16) for b in range(SD)] for c in range(ncomp)]
                Es = [[k.sb(st, f"at_es{c}{p}", [128, 512], F32) for p in range(2)] for c in range(ncomp)]
                rec = [k.sb(st, f"at_rec{c}", [128, 512], F32) for c in range(2)]
                oa = [k.sb(st, f"at_oa{c}", [128, 512], F32) for c in range(2)]
                sqt = k.sb(st, "at_sq", [128, 512], F32)
                ob = [k.sb(st, f"at_ob{i}", [128, 512], BF16) for i in range(2)]
                sm = k.sb(st, "at_sm", [128, 8], F32)
                junk = k.sb(st, "at_junk", [128, 64], F32)
                if da:
                    lamv = pvc(l, PV_DALAM, 256)
                    for i in range(2):
                        tt(DVE, junk[:], pv[:, l * PVL + PV_DALAM + 128 * i:l * PVL + PV_DALAM + 128 * i + 64],
                           pv[:, l * PVL + PV_DALAM + 128 * i + 64:l * PVL + PV_DALAM + 128 * i + 128], ALU.mult, [pv], [junk])
                        act(junk[:], junk[:], AF.Identity, [junk], [junk, sm], accum_out=sm[:, i:i + 1])
                        act(sm[:, 2 + i:3 + i], sm[:, i:i + 1], AF.Exp, [sm], [sm])
                    tt(DVE, sm[:, 4:5], sm[:, 3:4], sm[:, 2:3], ALU.subtract, [sm], [sm])
                    ts(DVE, sm[:, 4:5], sm[:, 4:5], -lam_init, None, ALU.add, None, [sm], [sm])
                    ts(DVE, sm[:, 5:6], pvc(l, PV_DAOG), 1.0 - lam_init, None, ALU.mult, None, [pv], [sm])
                qblocks = BLOCKS if need_ctx else BLOCKS[1:]
                qi = 0
                for h in range(nh):
                    if h % 4 == 0:
                        vsrc = VDA if da else MV[:, (h // 4) * 512:(h // 4 + 1) * 512]
                        k.dma(Vg[:], vsrc.rearrange("(kt p) v -> p kt v", p=128), writes=[Vg])
                    if da:
                        k.dma(K0[:], DAKT[h], writes=[K0])
                    else:
                        k.dma(K0[:], MKN[h], writes=[K0])
                        k.dma(K1[:], MKR[h], writes=[K1])
                    vcol = (h % 4) * 128
                    for (t0, nt) in qblocks:
                        is_ctx = t0 == 0
                        nkt = 2 if is_ctx else NT128
                        q0 = Q0[qi % 2]
                        q1 = None if da else Q1[qi % 2]
                        qi += 1
                        if da:
                            k.dma(q0[:, 0:nt], DAQT[h, :, t0:t0 + nt], writes=[q0])
                        else:
                            k.dma(q0[:, 0:nt], MQN[h, :, t0:t0 + nt], writes=[q0])
                            k.dma(q1[:, 0:nt], MQR[h, :, t0:t0 + nt], writes=[q1])

                        def scores(kt):
                            b = kt % SD
                            ks = slice(kt * 128, (kt + 1) * 128)
                            if da:
                                for c in range(2):
                                    mm(S[c][b][:, 0:nt], K0[64 * c:64 * c + 64, ks], q0[64 * c:64 * c + 64, 0:nt], True, True, [K0, q0], [S[c][b]])
                            else:
                                mm(S[0][b][:, 0:nt], K0[:, ks], q0[:, 0:nt], True, False, [K0, q0], [S[0][b]])
                                mm(S[0][b][:, 0:nt], K1[:, ks], q1[:, 0:nt], False, True, [K1, q1], [S[0][b]])

                        O = O2 if da else [O2[qi % 2]]
                        NFILL = 3 if da else 1
                        LA = SD - 2
                        for j0 in range(min(LA, nkt)):
                            scores(j0)

                        def pv_step(kt):
                            b = kt % SD
                            for c in range(ncomp):
                                mm(O[c][:, 0:nt], Vg[:, kt, vcol:vcol + 128], E[c][b][:, 0:nt], kt == 0, kt == nkt - 1, [Vg, E[c][b]], [O[c]])
                                p_ = kt % 2
                                e_ = DVE if (c + kt) % 2 == 0 else POOL
                                if kt < 2:
                                    cp(e_, Es[c][p_][:, 0:nt], E[c][b][:, 0:nt], [E[c][b]], [Es[c][p_]])
                                else:
                                    tt(e_, Es[c][p_][:, 0:nt], Es[c][p_][:, 0:nt], E[c][b][:, 0:nt], ALU.add, [Es[c][p_], E[c][b]], [Es[c][p_]])

                        for kt in range(nkt):
                            b = kt % SD
                            if kt + LA < nkt:
                                scores(kt + LA)
                            for c in range(ncomp):
                                act(E[c][b][:, 0:nt], S[c][b][:, 0:nt], AF.Exp, [S[c][b]], [E[c][b]], scale=scale)
                            if kt >= 1:
                                pv_step(kt - 1)
                                fb = (kt + LA + 1) % SD
                                for f_ in range(NFILL):
                                    c_ = f_ % ncomp
                                    mm(S[c_][fb][:, 0:512], onesb, K0[:, 0:512], True, True, [cstb, K0], [S[c_][fb]])
                        pv_step(nkt - 1)
                        Dn = [S[c][0] for c in range(ncomp)] if da else [Dps]
                        for c in range(ncomp):
                            mm(Dn[c][:, 0:nt], onesf, Es[c][0][:, 0:nt], True, False, [cst, Es[c][0]], [Dn[c]])
                            mm(Dn[c][:, 0:nt], onesf, Es[c][1][:, 0:nt], False, True, [cst, Es[c][1]], [Dn[c]])
                        o = ob[qi % 2]
                        for c in range(ncomp):
                            k.op(DVE, lambda c=c: nc.vector.reciprocal(out=rec[c][:, 0:nt], in_=Dn[c][:, 0:nt]), [Dn[c]], [rec[c]])
                        if da:
                            for c in range(2):
                                tt(DVE, oa[c][:, 0:nt], O[c][:, 0:nt], rec[c][:, 0:nt], ALU.mult, [O[c], rec[c]], [oa[c]])
                            stt(DVE, oa[0][:, 0:nt], oa[1][:, 0:nt], sm[:, 4:5], oa[0][:, 0:nt], ALU.mult, ALU.add, [oa[0], oa[1], sm], [oa[0]])
                            act(sqt[:, 0:nt], oa[0][:, 0:nt], AF.Square, [oa[0]], [sqt])
                            ms = S[0][1]
                            mm(ms[:, 0:nt], onesf, sqt[:, 0:nt], True, True, [cst, sqt], [ms])
                            act(sqt[:, 0:nt], ms[:, 0:nt], AF.Ln, [ms], [sqt], scale=1.0 / 128, bias=EPS)
                            act(sqt[:, 0:nt], sqt[:, 0:nt], AF.Exp, [sqt], [sqt], scale=-0.5)
                            stt(DVE, o[:, 0:nt], oa[0][:, 0:nt], sm[:, 5:6], sqt[:, 0:nt], ALU.mult, ALU.mult, [oa[0], sqt, sm], [o])
                            k.dma(MIXT[h, :, t0:t0 + nt], o[:, 0:nt], reads=[o], q=POOL)
                        else:
                            tt(DVE, o[:, 0:nt], O[0][:, 0:nt], rec[0][:, 0:nt], ALU.mult, [O[0], rec[0]], [o])
                            k.dma(MIXT[8 + h, :, t0:t0 + nt], o[:, 0:nt], reads=[o], q=POOL)
            k.barrier()


        LQT = dscr("LQT", [4, 128, NTOK], BF16)
        LKT = dscr("LKT", [4, 128, NTOK], BF16)
        LK = dscr("LK", [NTOK, 512], BF16)

        def phase_ml_prep(l):
            with contextlib.ExitStack() as st:
                xr = [k.sb(st, f"lp_x{i}", [128, NTOK], F32) for i in range(2)]
                y = k.sb(st, "lp_y", [128, NTOK], F32)
                ob = [k.sb(st, f"lp_o{i}", [128, NTOK], BF16) for i in range(2)]
                tp = [k.ps(st, f"lp_tp{i}", [128, 4, 128], BF16) for i in range(2)]
                tb = [k.sb(st, f"lp_tb{i}", [128, 4, 128], BF16) for i in range(2)]
                ti = 0
                for ch in range(8):
                    x, o = xr[ch % 2], ob[ch % 2]
                    rrow = (R_MLQ + ch) if ch < 4 else (R_MLK + ch - 4)
                    k.dma(x[:, 0:2304], RAWT[rrow, :, 0:2304], writes=[x])
                    k.dma(x[:, 2304:NTOK], RAWT[rrow, :, 2304:NTOK], writes=[x])
                    w = lambda j: pvc(l, PV_MLCW + ch * 3 + j)
                    for (a, b) in ((0, CTX), (CTX, NTOK)):
                        for c0 in range(a, b, 512):
                            c1 = min(c0 + 512, b)
                            ts(DVE, y[:, c0:c1], x[:, c0:c1], w(1), pvc(l, PV_MLCB + ch), ALU.mult, ALU.add, [x, pv], [y])
                            lo = max(c0, a + 1)
                            stt(DVE, y[:, lo:c1], x[:, lo - 1:c1 - 1], w(0), y[:, lo:c1], ALU.mult, ALU.add, [x, y, pv], [y])
                            hi = min(c1, b - 1)
                            stt(DVE, y[:, c0:hi], x[:, c0 + 1:hi + 1], w(2), y[:, c0:hi], ALU.mult, ALU.add, [x, y, pv], [y])
                            act(y[:, c0:c1], y[:, c0:c1], AF.Silu, [y], [y])
                            if ch < 4:
                                cp(POOL, o[:, c0:c1], y[:, c0:c1], [y], [o])
                            else:
                                ts(POOL, o[:, c0:c1], y[:, c0:c1], 128.0 ** -0.5, None, ALU.mult, None, [y], [o])
                    if ch < 4:
                        k.dma(LQT[ch], o[:], reads=[o], q=POOL)
                    else:
                        k.dma(LKT[ch - 4], o[:], reads=[o], q=POOL)
                        h = ch - 4
                        for g in range(0, NT128, 4):
                            n = min(4, NT128 - g)
                            p, t = tp[ti % 2], tb[ti % 2]
                            ti += 1
                            for j in range(n):
                                tr(p[:, j, :], o[:, (g + j) * 128:(g + j + 1) * 128], identb, [o, cstb], [p])
                            cp(ACT if ti % 2 == 0 else DVE, t[:, 0:n, :], p[:, 0:n, :], [p], [t])
                            k.dma(LK[g * 128:(g + n) * 128, h * 128:(h + 1) * 128].rearrange("(j p) d -> p j d", p=128), t[:, 0:n, :], reads=[t], q=POOL)
            k.barrier()

        def phase_ml_scan(l):
            with contextlib.ExitStack() as st:
                Graw = k.sb(st, "ls_graw", [128, NT128, 16], F32)
                G = k.sb(st, "ls_g", [128, NT128, 16], F32)
                LF = k.sb(st, "ls_lf", [128, NT128, 16], F32)
                k.dma(Graw[:], GATES.rearrange("(kt p) g -> p kt g", p=128), writes=[Graw])
                for g in range(16):
                    act(G[:, :, g], Graw[:, :, g], AF.Identity, [Graw, pv], [G], bias=pvc(l, PV_MLGB + g))
                act(LF[:], G[:], AF.Exp, [G], [LF], scale=-1.0)
                act(LF[:], LF[:], AF.Ln, [LF], [LF], bias=1.0)
                ts(DVE, LF[:], LF[:], -1.0, None, ALU.mult, None, [LF], [LF])
                QT = [k.sb(st, f"ls_qt{i}", [128, NTOK], BF16) for i in range(2)]
                KT = [k.sb(st, f"ls_kt{i}", [128, NTOK], BF16) for i in range(2)]
                Kk = [k.sb(st, f"ls_kk{i}", [128, NT128, 128], BF16) for i in range(2)]
                Vv = [k.sb(st, f"ls_vv{i}", [128, NT128, 128], BF16) for i in range(2)]
                Hd = [[k.sb(st, f"ls_h{i}{d}", [128, NTOK], F32) for d in range(2)] for i in range(2)]
                sq = k.sb(st, "ls_sq", [128, 512], F32)
                rs = k.sb(st, "ls_rs", [128, 512], F32)
                og = [k.sb(st, f"ls_og{i}", [128, 512], F32) for i in range(2)]
                ob = [k.sb(st, f"ls_ob{i}", [128, 512], BF16) for i in range(2)]
                pM = k.ps(st, "ls_pM", [128, 512], F32)

                class Ch:
                    pass
                chs = []
                for c in range(4):
                    o = Ch()
                    o.hi, o.d = c // 2, c % 2
                    f32t = lambda n, w=128: k.sb(st, f"ls_{n}{c}", [128, w], F32)
                    b16t = lambda n: k.sb(st, f"ls_{n}{c}", [128, 128], BF16)
                    o.LFb, o.ET, o.EB, o.Em, o.dn, o.Cs = f32t("lfb"), f32t("et"), f32t("eb"), f32t("em"), f32t("dn"), f32t("cs")
                    o.bias, o.Ns = f32t("bias", 2), f32t("ns", 2)
                    o.PT, o.Qd, o.Kw, o.Cb, o.Nb = b16t("pt"), b16t("qd"), b16t("kw"), b16t("cb"), b16t("nb")
                    o.bk = k.ps(st, f"ls_bk{c}", [128, 512], F32)
                    o.pB = o.pN = o.bk[:, 0:128]
                    o.pS = o.pD = o.bk[:, 128:256]
                    o.pC, o.pn, o.pb = o.bk[:, 256:384], o.bk[:, 384:386], o.bk[:, 386:388]
                    o.tri = cst[:, CS_U:CS_U + 128] if o.d == 0 else cst[:, CS_L:CS_L + 128]
                    o.ecol = 127 if o.d == 0 else 0
                    o.order = list(range(NT128)) if o.d == 0 else [1, 0] + list(range(NT128 - 1, 1, -1))
                    chs.append(o)
                oi = 0
                for hp in range(2):
                    for i in range(2):
                        h = 2 * hp + i
                        k.dma(QT[i][:], LQT[h], writes=[QT[i]])
                        k.dma(KT[i][:], LKT[h], writes=[KT[i]])
                        k.dma(Kk[i][:], LK[:, h * 128:(h + 1) * 128].rearrange("(kt p) d -> p kt d", p=128), writes=[Kk[i]])
                        k.dma(Vv[i][:], VML[:, h * 128:(h + 1) * 128].rearrange("(kt p) d -> p kt d", p=128), writes=[Vv[i]])
                    for step in range(NT128):
                        first, last = step == 0, step == NT128 - 1
                        for o in chs:
                            o.h = 2 * hp + o.hi
                            o.kt = o.order[step]
                            o.tk = slice(o.kt * 128, (o.kt + 1) * 128)
                            o.gi, o.gf = (o.h, 4 + o.h) if o.d == 0 else (8 + o.h, 12 + o.h)
                        for o in chs:
                            act(o.LFb[:], onesf, AF.Copy, [cst, LF], [o.LFb], scale=LF[:, o.kt, o.gf:o.gf + 1])
                        for o in chs:
                            mm(o.pB, o.LFb[:], o.tri, True, True, [o.LFb, cst], [o.bk])
                            mm(o.pb, o.tri, LF[:, o.kt, o.gf - 1:o.gf + 1], True, True, [cst, LF], [o.bk])
                            mm(o.pS, KT[o.hi][:, o.tk], QT[o.hi][:, o.tk], True, True, [KT[o.hi], QT[o.hi]], [o.bk])
                        for o in chs:
                            tt(DVE, o.bias[:, 0:1], G[:, o.kt, o.gi:o.gi + 1], o.bk[:, 387:388], ALU.subtract, [G, o.bk], [o.bias])
                            act(o.ET[:], o.pB, AF.Exp, [o.bk, o.bias], [o.ET], bias=o.bias[:, 0:1])
                            act(o.EB[:], o.pB, AF.Exp, [o.bk], [o.EB])
                        for o in chs:
                            tt(POOL, o.Em[:], o.ET[:], o.tri, ALU.mult, [o.ET, cst], [o.Em])
                            tt(DVE, o.PT[:], o.Em[:], o.pS, ALU.mult, [o.Em, o.bk], [o.PT])
                            if not first:
                                tt(POOL, o.Qd[:], QT[o.hi][:, o.tk], o.EB[:], ALU.mult, [QT[o.hi], o.EB], [o.Qd])
                        for o in chs:
                            mm(o.pN, Vv[o.hi][:, o.kt, :], o.PT[:], True, first, [Vv[o.hi], o.PT], [o.bk])
                            if not first:
                                mm(o.pN, o.Cb[:], o.Qd[:], False, True, [o.Cb, o.Qd], [o.bk])
                            mm(o.pD, onesb, o.PT[:], True, first, [cstb, o.PT], [o.bk])
                            if not first:
                                mm(o.pD, o.Nb[:], o.Qd[:], False, True, [o.Nb, o.Qd], [o.bk])
                        for o in chs:
                            Hh = Hd[o.hi][o.d]
                            ts(DVE, o.dn[:], o.pD, -1.0, 1.0, ALU.mult, ALU.max, [o.bk], [o.dn])
                            stt(DVE, o.dn[:], o.pD, 1.0, o.dn[:], ALU.max, ALU.max, [o.bk, o.dn], [o.dn])
                            k.op(DVE, lambda o=o: nc.vector.reciprocal(out=o.dn[:], in_=o.dn[:]), [o.dn], [o.dn])
                            tt(DVE, Hh[:, o.tk], o.pN, o.dn[:], ALU.mult, [o.bk, o.dn], [Hh])
                        if last:
                            continue
                        for o in chs:
                            act(o.Kw[:], Kk[o.hi][:, o.kt, :], AF.Copy, [Kk[o.hi], o.ET], [o.Kw], scale=o.ET[:, o.ecol:o.ecol + 1])
                        for o in chs:
                            mm(o.pC, o.Kw[:], Vv[o.hi][:, o.kt, :], True, True, [o.Kw, Vv[o.hi]], [o.bk])
                            mm(o.pn, o.Kw[:], cstb[:, 128:130], True, True, [o.Kw, cstb], [o.bk])
                        for o in chs:
                            if first:
                                cp(DVE, o.Cs[:], o.pC, [o.bk], [o.Cs])
                                cp(DVE, o.Ns[:], o.pn, [o.bk], [o.Ns])
                            else:
                                dec = o.EB[:, o.ecol:o.ecol + 1]
                                stt(DVE, o.Cs[:], o.Cs[:], dec, o.pC, ALU.mult, ALU.add, [o.Cs, o.EB, o.bk], [o.Cs])
                                stt(DVE, o.Ns[:], o.Ns[:], dec, o.pn, ALU.mult, ALU.add, [o.Ns, o.EB, o.bk], [o.Ns])
                            cp(ACT, o.Cb[:], o.Cs[:], [o.Cs], [o.Cb])
                            act(o.Nb[:], onesf, AF.Copy, [cst, o.Ns], [o.Nb], scale=o.Ns[:, 0:1])
                    for i in range(2):
                        h = 2 * hp + i
                        for (t0, nt) in BLOCKS:
                            o_, g_ = ob[oi % 2], og[oi % 2]
                            oi += 1
                            k.dma(g_[:, 0:nt], RAWT[R_MLO + h, :, t0:t0 + nt], writes=[g_])
                            act(g_[:, 0:nt], g_[:, 0:nt], AF.Sigmoid, [g_], [g_])
                            tt(POOL, rs[:, 0:nt], Hd[i][0][:, t0:t0 + nt], Hd[i][1][:, t0:t0 + nt], ALU.add, [Hd[i][0], Hd[i][1]], [rs])
                            act(sq[:, 0:nt], rs[:, 0:nt], AF.Square, [rs], [sq])
                            mm(pM[:, 0:nt], onesf, sq[:, 0:nt], True, True, [cst, sq], [pM])
                            stt(DVE, sq[:, 0:nt], rs[:, 0:nt], pvc(l, PV_MLOG + h), g_[:, 0:nt], ALU.mult, ALU.mult, [rs, g_, pv], [sq])
                            rstd_from(pM, 128, nt, 1.0 / 128, rs, [sq])
                            tt(POOL, o_[:, 0:nt], sq[:, 0:nt], rs[:, 0:nt], ALU.mult, [sq, rs], [o_])
                            k.dma(MIXT[4 + h, :, t0:t0 + nt], o_[:, 0:nt], reads=[o_], q=POOL)
            k.barrier()

        def build_gate_bcast(Gb, l, choff, j, ps_t, dg):
            for c in range(16):
                ts(POOL, dg[:], identf, modT[:, l, choff + c, j:j + 1], None, ALU.mult, None, [cst, modT], [dg])
                mm(ps_t[:, 0:128], onesf, dg[:], True, True, [cst, dg], [ps_t])
                cp(DVE, Gb[:, c * 128:(c + 1) * 128], ps_t[:, 0:128], [ps_t], [Gb])

        def phase_wout(l, src):
            need_ctx = l < DEPTH - 1
            with contextlib.ExitStack() as st:
                wo = k.sb(st, "o_w", [128, 16, D], BF16)
                for g in range(4):
                    k.dma(wo[:, g * 4:(g + 1) * 4, :], WOUT[l][g * 512:(g + 1) * 512, :].rearrange("(mc p) n -> p mc n", p=128), writes=[wo])
                Gb = [k.sb(st, f"o_gb{j}", [128, D], F32) for j in range(2)]
                dg = k.sb(st, "o_dg", [128, 128], F32)
                ps = [k.ps(st, f"o_ps{i}", [128, 512], F32) for i in range(4)]
                for j in range(2 if need_ctx else 1):
                    build_gate_bcast(Gb[j], l, 32, j, ps[0], dg)
                mx = [k.sb(st, f"o_mx{i}", [128, 16, 512], BF16) for i in range(2)]
                xt = [k.sb(st, f"o_x{i}", [128, D], F32) for i in range(2)]
                xo = [k.sb(st, f"o_xo{i}", [128, D], F32) for i in range(2)]
                tmp = [k.sb(st, f"o_t{i}", [128, 512], F32) for i in range(2)]
                bi = 0
                ti = 0
                for (t0, nt) in (BLOCKS if need_ctx else BLOCKS[1:]):
                    j = 1 if t0 == 0 else 0
                    m = mx[bi % 2]
                    bi += 1
                    k.dma(m[:, :, 0:nt], MIXT[:, :, t0:t0 + nt].rearrange("c p t -> p c t"), writes=[m])
                    for s in range(nt // 128):
                        x, o = xt[ti % 2], xo[ti % 2]
                        r0 = t0 + s * 128
                        k.dma(x[:], src[r0:r0 + 128, :], writes=[x])
                        for n in range(4):
                            p, t = ps[(ti * 4 + n) % 4], tmp[n % 2]
                            ns_ = slice(n * 512, (n + 1) * 512)
                            for mc in range(16):
                                mm(p[:], m[:, mc, s * 128:(s + 1) * 128], wo[:, mc, ns_], mc == 0, mc == 15, [m, wo], [p])
                            tt(DVE, t[:], p[:], Gb[j][:, ns_], ALU.mult, [p, Gb[j]], [t])
                            tt(POOL, o[:, ns_], t[:], x[:, ns_], ALU.add, [t, x], [o])
                        k.dma(XH[r0:r0 + 128, :], o[:], reads=[o], q=POOL)
                        ti += 1
            k.barrier()

        def phase_ffn(l):
            need_ctx = l < DEPTH - 1
            last = l == n_layers - 1 and l == DEPTH - 1
            with contextlib.ExitStack() as st:
                nb = alloc_norm_bufs(st, nx=2)
                xmT = nb[5]
                Gb = k.sb(st, "f_gb", [128, D], F32)
                dg = k.sb(st, "f_dg", [128, 128], F32)
                wg = [k.sb(st, f"f_wg{i}", [128, 16, 256], BF16) for i in range(2)]
                wu = [k.sb(st, f"f_wu{i}", [128, 16, 256], BF16) for i in range(2)]
                wd = [k.sb(st, f"f_wd{i}", [128, 11, 512], BF16) for i in range(2)]
                hid = k.sb(st, "f_hid", [128, 44, 512], BF16)
                sg = [k.sb(st, f"f_sg{i}", [128, 512], F32) for i in range(2)]
                hx = [k.sb(st, f"f_hx{i}", [128, 512], F32) for i in range(2)]
                ot = [k.sb(st, f"f_ot{i}", [128, 512], F32) for i in range(2)]
                acc = [k.ps(st, f"f_acc{i}", [128, 512], F32) for i in range(4)]
                pg = k.ps(st, "f_pg", [128, 512], F32)
                pu = k.ps(st, "f_pu", [128, 512], F32)
                wi = 0
                di = 0
                oi = 0
                gb_for = None
                fblocks = BLOCKS if need_ctx else BLOCKS[1:]
                norm_load(nb, XH, *fblocks[0])
                for bi, (t0, nt) in enumerate(fblocks):
                    is_ctx = t0 == 0
                    j = 1 if is_ctx else 0
                    ns = nt // 128
                    if gb_for != j:
                        build_gate_bcast(Gb, l, 80, j, pg, dg)
                        gb_for = j
                    norm_T(nb, nt, l, 1, is_ctx)
                    for jp in range(22):
                        g_, u_ = wg[wi % 2], wu[wi % 2]
                        wi += 1
                        k.dma(g_[:], WGU[l][:, jp * 256:(jp + 1) * 256].rearrange("(kc p) c -> p kc c", p=128), writes=[g_])
                        k.dma(u_[:], WGU[l][:, FF + jp * 256:FF + (jp + 1) * 256].rearrange("(kc p) c -> p kc c", p=128), writes=[u_])
                        for q in range(2):
                            jj = jp * 2 + q
                            for kc in range(16):
                                mm(pg[:, 0:nt], g_[:, kc, q * 128:(q + 1) * 128], xmT[:, kc, 0:nt], kc == 0, kc == 15, [g_, xmT], [pg])
                            for kc in range(16):
                                mm(pu[:, 0:nt], u_[:, kc, q * 128:(q + 1) * 128], xmT[:, kc, 0:nt], kc == 0, kc == 15, [u_, xmT], [pu])
                            s_ = sg[jj % 2]
                            act(s_[:, 0:nt], pg[:, 0:nt], AF.Silu, [pg], [s_])
                            tt(DVE, hid[:, jj, 0:nt], s_[:, 0:nt], pu[:, 0:nt], ALU.mult, [s_, pu], [hid])
                    if bi + 1 < len(fblocks):
                        norm_load(nb, XH, *fblocks[bi + 1])
                    for n in range(4):
                        ns_ = slice(n * 512, (n + 1) * 512)
                        for jg in range(4):
                            w_ = wd[di % 2]
                            di += 1
                            k.dma(w_[:], WDN[l][jg * 11 * 128:(jg + 1) * 11 * 128, ns_].rearrange("(j p) n -> p j n", p=128), writes=[w_])
                            for s in range(ns):
                                for jx in range(11):
                                    jj = jg * 11 + jx
                                    mm(acc[s][:], hid[:, jj, s * 128:(s + 1) * 128], w_[:, jx, :], jj == 0, jj == 43, [hid, w_], [acc[s]])
                        for s in range(ns):
                            h_, o_ = hx[oi % 2], ot[oi % 2]
                            oi += 1
                            r0 = t0 + s * 128
                            k.dma(h_[:], XH[r0:r0 + 128, ns_], writes=[h_], q=POOL)
                            tt(DVE, o_[:], acc[s][:], Gb[:, ns_], ALU.mult, [acc[s], Gb], [o_])
                            tt(POOL, o_[:], o_[:], h_[:], ALU.add, [o_, h_], [o_])
                            if last:
                                k.dma(y_d[r0 - CTX:r0 - CTX + 128, ns_], o_[:], reads=[o_], q=POOL)
                            else:
                                k.dma(XR[r0:r0 + 128, ns_], o_[:], reads=[o_], q=POOL)
            k.barrier()

        def done(tag):
            return stop_after == tag

        pre = "SKIPPRE" not in dump
        if pre:
            phase_cast_mod(range(n_layers), do_mod=not done("cast"))
        if not done("cast") and not done("mod"):
            for l in range(n_layers):
                src = xin if l == 0 else XR
                if pre:
                    phase_A(l, src)
                if done(f"A{l}"):
                    break
                if "SKIPDA" not in dump:
                    phase_da_prep(l)
                    phase_attn(l, "da")
                if done(f"DA{l}"):
                    break
                phase_ml_prep(l)
                if done(f"LP{l}"):
                    break
                phase_ml_scan(l)
                if done(f"ML{l}"):
                    break
                if "SKIPMLA" not in dump:
                    phase_mla_prep(l)
                    phase_attn(l, "mla")
                if done(f"MLA{l}"):
                    break
                phase_wout(l, src)
                if done(f"O{l}"):
                    break
                phase_ffn(l)
                if done(f"F{l}"):
                    break
        if "MODT" in dump:
            md = nc.dram_tensor("MODT", [128, DEPTH * 96 * 2], F32, kind="ExternalOutput").ap()
            k.dma(md[:, :], modT[:].rearrange("p l c j -> p (l c j)"), reads=[modT])
        k.barrier()
    k.close()
    return nc


def _consts():
    cst = np.zeros((128, NCS), np.float32)
    p = np.arange(128)
    cst[:, CS_ID:CS_ID + 128] = np.eye(128, dtype=np.float32)
    cst[:, CS_ONES:CS_ONES + 128] = 1.0
    cst[:, CS_BLK64:CS_BLK64 + 128] = (p[:, None] // 64 == p[None, :] // 64)
    rot = np.zeros((128, 128), np.float32)
    for dp in range(128):
        if dp % 32 < 16:
            rot[dp + 16, dp] = -1.0
        else:
            rot[dp - 16, dp] = 1.0
    cst[:, CS_ROT:CS_ROT + 128] = rot
    cst[:, CS_U:CS_U + 128] = (p[:, None] <= p[None, :])
    cst[:, CS_L:CS_L + 128] = (p[:, None] >= p[None, :])
    t = np.arange(SEQ)
    row = (t // GRID_W).astype(np.float32)
    col = (t % GRID_W).astype(np.float32)
    half = 16
    inv = (10000.0 ** (-np.arange(half, dtype=np.float32) / half)).astype(np.float32)
    rope = np.zeros((2, 128, SEQ), np.float32)
    for d in range(128):
        pos = row if (d % 64) // 32 == 0 else col
        ang = (pos * inv[d % 16]).astype(np.float32)
        rope[0, d] = np.cos(ang)
        rope[1, d] = np.sin(ang)
    return cst, rope


def _fm(v, nchunk):
    return np.ascontiguousarray(np.asarray(v, np.float32).reshape(nchunk, 128).T)


def _pack_pv(inp):
    pv = np.zeros((128, DEPTH * PVL), np.float32)
    for l in range(DEPTH):
        o = l * PVL
        pv[:, o + PV_N1G:o + PV_N1G + 16] = _fm(inp["norm1_g"][l], 16)
        pv[:, o + PV_N2G:o + PV_N2G + 16] = _fm(inp["norm2_g"][l], 16)
        for j in range(2):
            pv[:, o + PV_DAQG + j] = np.tile(inp["da_qk_g"][l, j], 2)
        pv[:, o + PV_DAOG] = inp["da_out_g"][l]
        cw = np.asarray(inp["ml_conv_w"][l])
        for ch in range(8):
            for j in range(3):
                pv[:, o + PV_MLCW + ch * 3 + j] = cw[j, ch * 128:(ch + 1) * 128]
        pv[:, o + PV_MLCB:o + PV_MLCB + 8] = _fm(inp["ml_conv_b"][l], 8)
        pv[:, o + PV_MLOG:o + PV_MLOG + 4] = _fm(inp["ml_out_g"][l], 4)
        pv[:, o + PV_MQG:o + PV_MQG + 4] = _fm(inp["mla_q_norm_g"][l], 4)
        pv[:, o + PV_MKVG:o + PV_MKVG + 2] = _fm(inp["mla_kv_norm_g"][l], 2)
        for j in range(2):
            pv[:, o + PV_MQKN + j] = inp["mla_qk_g"][l, j, :128]
            pv[:64, o + PV_MQKR + j] = inp["mla_qk_g"][l, j, 128:]
        pv[:, o + PV_DALAM:o + PV_DALAM + 256] = np.asarray(inp["da_lambda"][l]).reshape(1, 256)
        pv[:, o + PV_MLGB:o + PV_MLGB + 16] = np.asarray(inp["ml_gate_b"][l]).reshape(1, 16)
    return pv


def make_in_maps(inp, n_cores):
    f = lambda a: np.ascontiguousarray(np.asarray(a, np.float32))
    cst, rope = _consts()
    pv = _pack_pv(inp)
    wukv = np.asarray(inp["mla_w_ukv"], np.float32).reshape(DEPTH, 256, 8, 2, 128)
    wukv = np.ascontiguousarray(wukv.transpose(0, 1, 3, 2, 4).reshape(DEPTH, 256, 2048))
    shared = {
        "mod_w": f(inp["mod_w"]), "mod_bT": np.ascontiguousarray(f(inp["mod_b"]).reshape(DEPTH, 96, 128).transpose(0, 2, 1)),
        "w_in": f(inp["w_in"]), "w_out": f(inp["w_out"]), "w_gu": f(inp["ffn_w_gu"]), "w_down": f(inp["ffn_w_down"]),
        "w_uq": f(inp["mla_w_uq"]), "w_ukv": wukv, "pv": pv, "cst": cst, "rope": rope,
    }
    maps = []
    for c in range(n_cores):
        b = c % 4
        m = dict(shared)
        m["xin"] = np.ascontiguousarray(np.concatenate([inp["ctx"][b], inp["x"][b]], axis=0).astype(np.float32))
        ccv = np.stack([np.asarray(inp["c"][b], np.float32), np.asarray(inp["c_ctx"], np.float32)], axis=-1)
        m["cc"] = np.ascontiguousarray(ccv.reshape(16, 128, 2).transpose(1, 0, 2))
        maps.append(m)
    return maps


N_CORES = 4


def kernel(**inputs):
    nc = build()
    maps = make_in_maps(inputs, N_CORES)
    res = run_bass_kernel_spmd(nc, maps, core_ids=list(range(N_CORES)))
    out = np.stack([res.results[b]["y"] for b in range(4)], axis=0)
    return out.astype(np.float32)
```
